# Optimizing a Trainium2 kernel written in Bass

```python
import jax, jax.numpy as jnp
from jax import lax
import numpy as np

D_MODEL = 4096
BATCH = 2
SEQ = 4096
DEPTH = 1

CONV_WIDTH = 2048
CONV_GROUPS = 16
CONV_K = 3
DN_HEADS = 16
DN_HEAD_DIM = 128
DN_WIDTH = DN_HEADS * DN_HEAD_DIM
DN_CONV_K = 4
CHUNK = 64
D_FF = 11008
EPS = 1e-6
L2_EPS = 1e-6
SPLIT_SIZES = (CONV_WIDTH, CONV_WIDTH, CONV_WIDTH, DN_WIDTH, DN_WIDTH, DN_WIDTH, DN_WIDTH, DN_HEADS, DN_HEADS, D_MODEL, D_MODEL)
SPLIT_POINTS = (2048, 4096, 6144, 8192, 10240, 12288, 14336, 14352, 14368, 18464)
N_IN = 22560

kernel_name = "hybrid_shortconv_gdn_macaron"


def rms_norm(x, g):
    xf = x.astype(jnp.float32)
    y = xf * lax.rsqrt(jnp.mean(xf * xf, axis=-1, keepdims=True) + EPS)
    return (y * g.astype(jnp.float32)).astype(x.dtype)


def l2_normalize(x):
    return x * lax.rsqrt(jnp.sum(x * x, axis=-1, keepdims=True) + L2_EPS)


def swiglu(x, w_gate, w_up, w_down):
    return (jax.nn.silu(x @ w_gate) * (x @ w_up)) @ w_down


def causal_depthwise_conv(x, w):
    K = w.shape[1]
    T = x.shape[1]
    xp = jnp.pad(x, ((0, 0), (K - 1, 0), (0, 0)))
    y = xp[:, 0:T] * w[:, 0]
    for j in range(1, K):
        y = y + xp[:, j:j + T] * w[:, j]
    return y


def chunk_gated_delta_rule(q, k, v, g, beta):
    B, T, H, Dk = q.shape
    Dv = v.shape[-1]
    N = T // CHUNK

    def to_chunks(t):
        t = jnp.moveaxis(t, 2, 1)
        return t.reshape((B, H, N, CHUNK) + t.shape[3:])

    q = to_chunks(q) * (Dk ** -0.5)
    k = to_chunks(k)
    v = to_chunks(v)
    beta = to_chunks(beta)
    g = jnp.cumsum(to_chunks(g), axis=-1)
    idx = jnp.arange(CHUNK)
    causal = idx[:, None] >= idx[None, :]
    strict = idx[:, None] > idx[None, :]
    decay = jnp.exp(jnp.where(causal, g[..., :, None] - g[..., None, :], -jnp.inf))
    k_beta = k * beta[..., None]
    lower = jnp.where(strict, jnp.einsum('bhncd,bhnsd->bhncs', k_beta, k) * decay, 0.0) + jnp.eye(CHUNK, dtype=jnp.float32)
    rhs = jnp.concatenate([v * beta[..., None], k_beta * jnp.exp(g)[..., None]], axis=-1)
    sol = lax.linalg.triangular_solve(lower, rhs, left_side=True, lower=True, unit_diagonal=True)
    u, w = sol[..., :Dv], sol[..., Dv:]
    attn_intra = jnp.einsum('bhncd,bhnsd->bhncs', q, k) * decay
    q_dec = q * jnp.exp(g)[..., None]
    k_dec = k * jnp.exp(g[..., -1:] - g)[..., None]
    g_last = jnp.exp(g[..., -1])

    def step(S, xs):
        q_d, k_d, u_c, w_c, a_c, gl = xs
        v_new = u_c - jnp.einsum('bhcd,bhde->bhce', w_c, S)
        o = jnp.einsum('bhcd,bhde->bhce', q_d, S) + jnp.einsum('bhcs,bhse->bhce', a_c, v_new)
        S = S * gl[..., None, None] + jnp.einsum('bhcd,bhce->bhde', k_d, v_new)
        return S, o

    xs = tuple(jnp.moveaxis(t, 2, 0) for t in (q_dec, k_dec, u, w, attn_intra, g_last))
    S0 = jnp.zeros((B, H, Dk, Dv), jnp.float32)
    _, o = lax.scan(step, S0, xs)
    o = jnp.moveaxis(o, 0, 2).reshape(B, H, T, Dv)
    return jnp.moveaxis(o, 1, 2)


def setup_inputs(seed: int = 0) -> dict:
    key = jax.random.key(seed)
    ks = jax.random.split(key, 24)
    f32 = jnp.float32

    def nrm(k, shape, fan_in):
        return jax.random.normal(k, shape, f32) * (fan_in ** -0.5)

    def gain(k, shape):
        return 1.0 + 0.01 * jax.random.normal(k, shape, f32)

    L = DEPTH
    dt = jnp.exp(jax.random.uniform(ks[10], (L, DN_HEADS), f32, np.log(1e-3), np.log(1e-1)))
    return {
        "x": jax.random.normal(ks[0], (BATCH, SEQ, D_MODEL), f32),
        "ffn1_norm": gain(ks[1], (L, D_MODEL)),
        "ffn1_w_gate": nrm(ks[2], (L, D_MODEL, D_FF), D_MODEL),
        "ffn1_w_up": nrm(ks[3], (L, D_MODEL, D_FF), D_MODEL),
        "ffn1_w_down": nrm(ks[4], (L, D_FF, D_MODEL), D_FF),
        "mix_norm": gain(ks[5], (L, D_MODEL)),
        "w_in": nrm(ks[6], (L, D_MODEL, N_IN), D_MODEL),
        "conv_mixer_w": nrm(ks[7], (L, CONV_WIDTH, CONV_K), CONV_K),
        "dn_conv_w": nrm(ks[8], (L, 3 * DN_WIDTH, DN_CONV_K), DN_CONV_K),
        "dn_a_log": jnp.log(jax.random.uniform(ks[9], (L, DN_HEADS), f32, 1.0, 16.0)),
        "dn_dt_bias": dt + jnp.log(-jnp.expm1(-dt)),
        "dn_out_norm": gain(ks[11], (L, DN_HEAD_DIM)),
        "w_conv_branch": nrm(ks[12], (L, CONV_WIDTH, D_MODEL), CONV_WIDTH),
        "w_dn_branch": nrm(ks[13], (L, DN_WIDTH, D_MODEL), DN_WIDTH),
        "w_out": nrm(ks[14], (L, D_MODEL, D_MODEL), D_MODEL),
        "ffn2_norm": gain(ks[15], (L, D_MODEL)),
        "ffn2_w_gate": nrm(ks[16], (L, D_MODEL, D_FF), D_MODEL),
        "ffn2_w_up": nrm(ks[17], (L, D_MODEL, D_FF), D_MODEL),
        "ffn2_w_down": nrm(ks[18], (L, D_FF, D_MODEL), D_FF),
        "final_norm": gain(ks[19], (D_MODEL,)),
    }


def reference(x, ffn1_norm, ffn1_w_gate, ffn1_w_up, ffn1_w_down, mix_norm, w_in, conv_mixer_w, dn_conv_w, dn_a_log, dn_dt_bias, dn_out_norm, w_conv_branch, w_dn_branch, w_out, ffn2_norm, ffn2_w_gate, ffn2_w_up, ffn2_w_down, final_norm):
    B, T, _ = x.shape
    f32 = jnp.float32
    h = x
    for l in range(DEPTH):
        h = h + 0.5 * swiglu(rms_norm(h, ffn1_norm[l]), ffn1_w_gate[l], ffn1_w_up[l], ffn1_w_down[l])

        u = rms_norm(h, mix_norm[l])
        p = u @ w_in[l]
        cB, cC, cx, dq, dk, dv, dz, da, db, gc, gd = jnp.split(p, SPLIT_POINTS, axis=-1)

        y_conv = cB * causal_depthwise_conv(cC * cx, conv_mixer_w[l])

        qkv = jax.nn.silu(causal_depthwise_conv(jnp.concatenate([dq, dk, dv], axis=-1), dn_conv_w[l])).astype(f32)
        q = l2_normalize(qkv[..., :DN_WIDTH].reshape(B, T, DN_HEADS, DN_HEAD_DIM))
        k = l2_normalize(qkv[..., DN_WIDTH:2 * DN_WIDTH].reshape(B, T, DN_HEADS, DN_HEAD_DIM))
        v = qkv[..., 2 * DN_WIDTH:].reshape(B, T, DN_HEADS, DN_HEAD_DIM)
        g = -jnp.exp(dn_a_log[l].astype(f32)) * jax.nn.softplus(da.astype(f32) + dn_dt_bias[l].astype(f32))
        beta = jax.nn.sigmoid(db.astype(f32))
        o = chunk_gated_delta_rule(q, k, v, g, beta)
        o = rms_norm(o, dn_out_norm[l]) * jax.nn.silu(dz.astype(f32).reshape(B, T, DN_HEADS, DN_HEAD_DIM))
        y_dn = o.reshape(B, T, DN_WIDTH).astype(x.dtype)

        merged = jax.nn.sigmoid(gc) * (y_conv @ w_conv_branch[l]) + jax.nn.sigmoid(gd) * (y_dn @ w_dn_branch[l])
        h = h + merged @ w_out[l]

        h = h + 0.5 * swiglu(rms_norm(h, ffn2_norm[l]), ffn2_w_gate[l], ffn2_w_up[l], ffn2_w_down[l])
    return rms_norm(h, final_norm)
```

```python
import contextlib
import numpy as np
import concourse.bass as bass
import concourse.mybir as mybir
from concourse.bass_utils import run_bass_kernel_spmd

F32 = mybir.dt.float32
BF16 = mybir.dt.bfloat16
AF = mybir.ActivationFunctionType
ALU = mybir.AluOpType

NCORES = 8
TOK = 1024
D = 4096
KC = 32
DFF = 11008
NIN = 22560
EPS = 1e-6
HEADS = 16
NT = 8
ENGS = ("sync", "scalar", "vector", "gpsimd", "tensor")


class Sem:
    def __init__(self, h, name):
        self.h = h
        self.n = 0
        self.name = name


class Prog:
    def __init__(self, nc, stack):
        self.nc = nc
        self.stack = stack
        self.ops = {e: [] for e in ENGS}
        self.nsem = 0
        self.done = {e: self.sem("dn_" + e) for e in ENGS}
        self.seen = {e: {} for e in ENGS}
        self.dma_sems = []

    def sem(self, name):
        self.nsem += 1
        return Sem(self.stack.enter_context(self.nc.semaphore(name)), name)

    def dsem(self, name):
        s = self.sem(name)
        self.dma_sems.append(s)
        return s

    def _waits(self, eng, waits):
        best = {}
        for tok in waits:
            if tok is None:
                continue
            s, v = tok
            if v > best.get(s, (None, 0))[1]:
                best[s] = (s, v)
        out = []
        seen = self.seen[eng]
        for s, v in best.values():
            if seen.get(s, 0) >= v:
                continue
            seen[s] = v
            out.append((s.h, v))
        return out

    def op(self, eng, fn, waits=(), count=True):
        w = self._waits(eng, waits)
        inc = None
        tok = None
        if count:
            s = self.done[eng]
            if s.n >= 30000:
                s = self.done[eng] = self.sem("dn_" + eng + str(self.nsem))
            s.n += 1
            inc = (s.h, 1)
            tok = (s, s.n)
        self.ops[eng].append((w, fn, inc))
        return tok

    def dma(self, eng, sem, out, in_, waits=()):
        w = self._waits(eng, waits)
        sem.n += 16
        self.ops[eng].append((w, lambda e: e.dma_start(out=out, in_=in_), (sem.h, 16)))
        return (sem, sem.n)

    def wait_only(self, eng, waits):
        w = self._waits(eng, waits)
        if w:
            self.ops[eng].append((w, None, None))

    def barrier(self):
        toks = [(s, s.n) for s in self.done.values()] + [(s, s.n) for s in self.dma_sems]
        for en in ENGS:
            self.wait_only(en, toks)

    def replay(self, block):
        def mk(lst):
            def run(e):
                for w, fn, inc in lst:
                    for h, v in w:
                        e.wait_ge(h, v)
                    if fn is None:
                        continue
                    ins = fn(e)
                    if inc is not None:
                        ins.then_inc(inc[0], inc[1])
            return run
        for en in ENGS:
            if self.ops[en]:
                getattr(block, en)(mk(self.ops[en]))


class Buf:
    def __init__(self, t):
        self.t = t
        self.ready = []
        self.readers = []

    def ww(self):
        return self.ready + self.readers

    def wrote(self, tok):
        self.ready = [tok]
        self.readers = []

    def wrote_more(self, tok):
        self.ready.append(tok)

    def rw(self):
        return list(self.ready)

    def read(self, tok):
        self.readers.append(tok)


def build(debug=None):
    nc = bass.Bass("TRN2", target_bir_lowering=False)
    stack = contextlib.ExitStack()
    P = Prog(nc, stack)

    def din(name, shape):
        return nc.dram_tensor(name, list(shape), F32, kind="ExternalInput").ap()

    x = din("x", [TOK, D])
    gam_in = din("gam", [128, 4 * KC])
    DBGM = debug in ("mixA", "mixB") or (isinstance(debug, str) and debug.startswith("delta"))
    if not DBGM:
        w1g = din("w1g", [D, DFF]); w1u = din("w1u", [D, DFF]); w1d = din("w1d", [DFF, D])
    if debug != "hT" and not (isinstance(debug, str) and debug.startswith("delta")):
        w_in = din("w_in", [D, NIN])
        if not DBGM:
            w2g = din("w2g", [D, DFF]); w2u = din("w2u", [D, DFF]); w2d = din("w2d", [DFF, D])
            w_cb = din("w_cb", [2048, D]); w_db = din("w_db", [2048, D]); w_out = din("w_out", [D, D])
    ident_in = din("ident", [128, 128])
    MIX = debug != "hT"
    if MIX:
        cw_in = din("cw", [128, 240])
        tab_in = din("tab", [128, 9, 128])
        def dscr(name, shape, dt=F32):
            return nc.dram_tensor(name, list(shape), dt).ap()
        DELTA = isinstance(debug, str) and debug.startswith("delta")
        pc_d = dscr("pc_d", [16, 128, TOK])
        if DELTA:
            pqkv_d = din("pqkv_in", [48, 128, TOK])
            sz_d = nc.dram_tensor("sz_in", [16, 128, TOK], BF16, kind="ExternalInput").ap()
            sc_in = din("sc_in", [128, 12 * 128]); halo_in = din("halo_in", [128, 192])
        else:
            pqkv_d = dscr("pqkv_d", [48, 128, TOK])
            sz_d = dscr("sz_d", [16, 128, TOK], BF16)
        spb_d = dscr("spb_d", [16, 128, 8 * 4 * 128], BF16); spu_d = dscr("spu_d", [16, 128, 8 * 128])
        gsend_t = [nc.dram_tensor(f"gsend{q}", [4 * 128, 256], F32) for q in range(4)]
        gall_t = [nc.dram_tensor(f"gall{q}", [4 * 4 * 128, 256], F32) for q in range(4)]
        hsend_t = nc.dram_tensor("hsend", [128, 192], F32); hall_t = nc.dram_tensor("hall", [4 * 128, 192], F32)
        mrg_d = dscr("mrg_d", [KC, 128, TOK], BF16)
    out = nc.dram_tensor("out", [TOK, D], F32, kind="ExternalOutput").ap()
    hT = nc.dram_tensor("hT", [KC, 128, TOK], F32,
                        kind=("ExternalOutput" if debug in ("hT", "h2") else "Internal")).ap()

    SB_BASE = 16512
    SB_END = 229375
    cur = [SB_BASE]

    def alloc(name, shape, dt, at=None):
        n = int(np.prod(shape[1:])) * (4 if dt == F32 else 2)
        if at is None:
            at = cur[0]
            cur[0] += (n + 31) // 32 * 32
        return nc.alloc_sbuf_tensor_at(name, list(shape), dt, offset=at)

    actA = alloc("actA", [128, KC, TOK], BF16)
    regB = cur[0]
    actT = alloc("actT", [128, 29, TOK], BF16)
    cur[0] = regB + 64 * 1024
    NWB = 4
    wbuf = [Buf(alloc(f"wb{i}", [128, KC, 128], BF16)) for i in range(NWB)]
    hold = [Buf(alloc(f"hold{i}", [128, TOK], F32)) for i in range(2)]
    hnew = [Buf(alloc(f"hnew{i}", [128, TOK], F32)) for i in range(2)]
    stmp = [Buf(alloc(f"stmp{i}", [128, TOK], BF16)) for i in range(2)]
    rstd = Buf(alloc("rstd", [128, TOK], F32))
    ident = alloc("ident", [128, 128], F32)
    ones_bf = alloc("ones_bf", [128, 128], BF16)
    gam = alloc("gam", [128, 4, KC], F32)
    xin = Buf(alloc("xin", [128, D], F32, at=regB))
    xst = [Buf(alloc(f"xst{i}", [128, 4, 128], F32, at=regB + 16384 + 2048 * i)) for i in range(2)]
    yconv = alloc("yconv", [128, 16, TOK], BF16, at=regB)
    ydn = alloc("ydn", [128, 16, TOK], BF16, at=regB + 32768)
    tab = alloc("tab", [128, 9, 128], F32)
    cw = alloc("cw", [128, 240], F32)
    sc = alloc("sc", [128, 12, 128], F32)
    hsb = alloc("hsb", [128, 192], F32)
    hal = alloc("hal", [128, 4, 192], F32)
    halo = alloc("halo", [128, 192], F32)
    wab = alloc("wab", [128, KC, 32], BF16)
    assert cur[0] <= SB_END, cur[0]
    dcur = [SB_BASE]
    def dalloc(name, shape, dt):
        n = int(np.prod(shape[1:])) * (4 if dt == F32 else 2)
        sz_ = (n + 31) // 32 * 32
        if dcur[0] + sz_ > SB_BASE + 65536 and dcur[0] <= SB_BASE + 65536:
            dcur[0] = SB_BASE + 163840
        at = dcur[0]
        dcur[0] += sz_
        assert dcur[0] <= SB_BASE + 65536 or (at >= SB_BASE + 163840 and dcur[0] <= SB_BASE + 163840 + 24576), (name, dcur[0])
        return nc.alloc_sbuf_tensor_at(name, list(shape), dt, offset=at)

    NPS = 3
    psum = [Buf(stack.enter_context(nc.psum_tensor(f"ps{i}", [128, TOK], F32))) for i in range(NPS)]
    pss = [stack.enter_context(nc.psum_tensor(f"pss{i}", [128, 512], F32)) for i in range(2)]
    A = Buf(actA)
    ACT = Buf(actT)
    HT = [Buf(None) for _ in range(KC)]

    c_ld = P.dsem("c_ld")
    ctok = [P.dma("sync", c_ld, ident[:], ident_in),
            P.dma("sync", c_ld, gam[:].rearrange("p a b -> p (a b)"), gam_in),
            P.op("vector", lambda e: e.memset(ones_bf[:], 1.0))]

    st = {"wb": 0, "ps": 0, "ld": 0, "st": 0, "tmp": 0}
    wsem = [P.dsem(f"wld{i}") for i in range(NWB)]
    hld = [P.dsem(f"hld{i}") for i in range(2)]
    hst = [P.dsem(f"hst{i}") for i in range(2)]

    def next_ps():
        p = psum[st["ps"] % NPS]
        st["ps"] += 1
        return p

    def job(W_ap, nk, rhs_fn, pe_waits=()):
        b = st["wb"] % NWB
        st["wb"] += 1
        wb = wbuf[b]
        pr = next_ps()
        src = W_ap.rearrange("(k p) f -> p k f", p=128)
        tok = P.dma("gpsimd", wsem[b], wb.t[:, 0:nk, :], src, waits=wb.ww())
        wb.wrote(tok)
        waits = wb.rw() + pr.ww() + list(pe_waits) + ctok
        last = None
        for k in range(nk):
            for hf in range(2):
                fin = (k == nk - 1 and hf == 1)
                fn = (lambda e, k=k, hf=hf: e.matmul(pr.t[:, hf * 512:(hf + 1) * 512], wb.t[:, k, :], rhs_fn(k, hf),
                                                     start=(k == 0), stop=(k == nk - 1)))
                last = P.op("tensor", fn, waits=waits if (k == 0 and hf == 0) else (), count=fin)
        wb.read(last)
        pr.wrote(last)
        return pr

    xld = P.dsem("xld")
    xsts = [P.dsem(f"xsts{i}") for i in range(2)]

    def stage_T0():
        cnt = 0
        for t in range(NT):
            tok = P.dma("sync", xld, xin.t[:], x[t * 128:(t + 1) * 128, :], waits=xin.ww())
            xin.wrote(tok)
            for kg in range(KC // 4):
                pr = next_ps()
                last = None
                for kk in range(4):
                    k = kg * 4 + kk
                    last = P.op("tensor", lambda e, k=k, kk=kk, pr=pr: e.transpose(pr.t[:, kk * 128:(kk + 1) * 128],
                                                                                   xin.t[:, k * 128:(k + 1) * 128], ident[:]),
                                waits=(xin.rw() + pr.ww() + ctok) if kk == 0 else (), count=(kk == 3))
                pr.wrote(last)
                xin.read(last)
                si = cnt % 2
                cnt += 1
                sb = xst[si]
                tk = P.op("vector", lambda e, pr=pr, sb=sb: e.tensor_copy(out=sb.t[:].rearrange("p a b -> p (a b)"), in_=pr.t[:, 0:512]),
                          waits=pr.rw() + sb.ww())
                pr.read(tk)
                sb.wrote(tk)
                dst = hT[kg * 4:(kg + 1) * 4, :, t * 128:(t + 1) * 128].rearrange("k p j -> p k j")
                tk2 = P.dma("sync", xsts[si], dst, sb.t[:], waits=sb.rw())
                sb.read(tk2)
                for k in range(kg * 4, kg * 4 + 4):
                    HT[k].wrote_more(tk2)

    def load_h(k):
        b = st["ld"] % 2
        st["ld"] += 1
        hb = hold[b]
        tok = P.dma("sync", hld[b], hb.t[:], hT[k], waits=hb.ww() + HT[k].rw())
        hb.wrote(tok)
        HT[k].read(tok)
        return hb

    def norm_stats():
        pr = next_ps()
        last = None
        for k in range(KC):
            hb = load_h(k)
            sb = stmp[st["tmp"] % 2]
            st["tmp"] += 1
            tk = P.op("scalar", lambda e, hb=hb, sb=sb: e.activation(out=sb.t[:], in_=hb.t[:], func=AF.Square),
                      waits=hb.rw() + sb.ww())
            hb.read(tk)
            sb.wrote(tk)
            for hf in range(2):
                last = P.op("tensor", lambda e, k=k, hf=hf, sb=sb, pr=pr: e.matmul(pr.t[:, hf * 512:(hf + 1) * 512], ones_bf[:],
                                                                                   sb.t[:, hf * 512:(hf + 1) * 512],
                                                                                   start=(k == 0), stop=(k == KC - 1)),
                            waits=(sb.rw() + (pr.ww() + ctok if k == 0 else [])) if hf == 0 else (), count=(hf == 1))
            sb.read(last)
        pr.wrote(last)
        t1 = P.op("vector", lambda e: e.tensor_scalar(out=rstd.t[:], in0=pr.t[:], scalar1=1.0 / D, scalar2=EPS, op0=ALU.mult, op1=ALU.add),
                  waits=pr.rw() + rstd.ww())
        pr.read(t1)
        t2 = P.op("scalar", lambda e: e.sqrt(out=rstd.t[:], in_=rstd.t[:]), waits=[t1])
        t3 = P.op("vector", lambda e: e.reciprocal(out=rstd.t[:], in_=rstd.t[:]), waits=[t2])
        rstd.wrote(t3)

    def norm_stage(gi):
        norm_stats()
        first = True
        for k in range(KC):
            hb = load_h(k)
            tk = P.op("vector", lambda e, k=k, hb=hb: e.scalar_tensor_tensor(out=actA[:, k, :], in0=hb.t[:], scalar=gam[:, gi, k:k + 1], in1=rstd.t[:],
                                                                             op0=ALU.mult, op1=ALU.mult),
                      waits=hb.rw() + rstd.rw() + (A.ww() if first else []))
            first = False
            hb.read(tk)
            rstd.read(tk)
            if k == 0:
                A.wrote(tk)
            else:
                A.wrote_more(tk)

    def resid_update(i, pd, scale):
        hb = load_h(i)
        nb = hnew[st["st"] % 2]
        ssem = hst[st["st"] % 2]
        st["st"] += 1
        tk = P.op("vector", lambda e: e.scalar_tensor_tensor(out=nb.t[:], in0=pd.t[:], scalar=scale, in1=hb.t[:],
                                                             op0=ALU.mult, op1=ALU.add),
                  waits=pd.rw() + hb.rw() + nb.ww())
        pd.read(tk)
        hb.read(tk)
        nb.wrote(tk)
        tk2 = P.dma("sync", ssem, hT[i], nb.t[:], waits=nb.rw() + HT[i].ww())
        nb.read(tk2)
        HT[i].wrote(tk2)

    def ffn(gi, Wg, Wu, Wd):
        norm_stage(gi)
        groups = [(0, 29), (29, 29), (58, 28)]
        for gidx, (f0, G) in enumerate(groups):
            for jj in range(G):
                j = f0 + jj
                pg = job(Wg[:, j * 128:(j + 1) * 128], KC, lambda k, hf: actA[:, k, hf * 512:(hf + 1) * 512], pe_waits=A.rw())
                pu = job(Wu[:, j * 128:(j + 1) * 128], KC, lambda k, hf: actA[:, k, hf * 512:(hf + 1) * 512])
                A.read(pu.ready[0])
                sb = stmp[st["tmp"] % 2]
                st["tmp"] += 1
                tk = P.op("scalar", lambda e, pg=pg, sb=sb: e.activation(out=sb.t[:], in_=pg.t[:], func=AF.Silu),
                          waits=pg.rw() + sb.ww())
                pg.read(tk)
                sb.wrote(tk)
                tk2 = P.op("vector", lambda e, pu=pu, sb=sb, jj=jj: e.tensor_tensor(out=actT[:, jj, :], in0=pu.t[:], in1=sb.t[:], op=ALU.mult),
                           waits=pu.rw() + sb.rw() + (ACT.ww() if jj == 0 else []))
                pu.read(tk2)
                sb.read(tk2)
                if jj == 0:
                    ACT.wrote(tk2)
                else:
                    ACT.wrote_more(tk2)
            for i in range(KC):
                pd = job(Wd[f0 * 128:(f0 + G) * 128, i * 128:(i + 1) * 128], G,
                         lambda k, hf: actT[:, k, hf * 512:(hf + 1) * 512], pe_waits=ACT.rw())
                ACT.read(pd.ready[0])
                resid_update(i, pd, 0.5)

    osem = P.dsem("osem")

    def final_stage():
        norm_stats()
        cnt = 0
        for k in range(KC):
            hb = load_h(k)
            nb = hnew[st["st"] % 2]
            st["st"] += 1
            tk = P.op("vector", lambda e, k=k, hb=hb, nb=nb: e.scalar_tensor_tensor(out=nb.t[:], in0=hb.t[:], scalar=gam[:, 3, k:k + 1], in1=rstd.t[:],
                                                                                    op0=ALU.mult, op1=ALU.mult),
                      waits=hb.rw() + rstd.rw() + nb.ww())
            hb.read(tk)
            rstd.read(tk)
            nb.wrote(tk)
            for tg in range(2):
                pr = next_ps()
                last = None
                for tt in range(4):
                    t = tg * 4 + tt
                    last = P.op("tensor", lambda e, t=t, tt=tt, pr=pr, nb=nb: e.transpose(pr.t[:, tt * 128:(tt + 1) * 128],
                                                                                          nb.t[:, t * 128:(t + 1) * 128], ident[:]),
                                waits=(nb.rw() + pr.ww() + ctok) if tt == 0 else (), count=(tt == 3))
                pr.wrote(last)
                nb.read(last)
                sb = xst[cnt % 2]
                cnt += 1
                tk = P.op("vector", lambda e, pr=pr, sb=sb: e.tensor_copy(out=sb.t[:].rearrange("p a b -> p (a b)"), in_=pr.t[:, 0:512]),
                          waits=pr.rw() + sb.ww())
                pr.read(tk)
                sb.wrote(tk)
                dst = out[tg * 512:(tg + 1) * 512, k * 128:(k + 1) * 128].rearrange("(tt p) f -> p tt f", p=128)
                tk2 = P.dma("sync", xsts[(cnt - 1) % 2], dst, sb.t[:], waits=sb.rw())
                sb.read(tk2)

    if MIX:
        class Flow:
            def __init__(self, start=()):
                self.tk = None
                self.start = list(start)

            def op(self, eng, fn, extra=()):
                self.tk = P.op(eng, fn, waits=[self.tk] + self.start + list(extra))
                self.start = []
                return self.tk

            def dma(self, eng, sem, out_, in_, extra=()):
                self.tk = P.dma(eng, sem, out_, in_, waits=[self.tk] + self.start + list(extra))
                self.start = []
                return self.tk

        def interleave(gens):
            gens = list(gens)
            while gens:
                nxt = []
                for g in gens:
                    try:
                        next(g)
                        nxt.append(g)
                    except StopIteration:
                        pass
                gens = nxt

        def raw_inc(eng, fn, waits, sem, amt):
            w = P._waits(eng, waits)
            sem.n += amt
            P.ops[eng].append((w, fn, (sem.h, amt)))
            return (sem, sem.n)

        rhsA = lambda k, hf: actA[:, k, hf * 512:(hf + 1) * 512]
        T_U, T_NEG, T_STR, T_ONE, T_DTB, T_ALOG, T_GOUT, T_SEL = range(8)
        S_G, S_BETA, S_GC, S_GL, S_EG, S_EGL, S_EKD, S_BG, S_NEGA, S_TMP = range(10)
        GROUPS4 = [[0, 1, 2, 3], [4, 5, 6, 7]]
        msem = [P.dsem(f"msem{i}") for i in range(8)]
        ccsem = P.dsem("ccsem")

        def mixer_front():
            norm_stage(1)
            f = Flow(start=A.rw() + ctok)
            f.dma("sync", msem[0], tab[:], tab_in)
            f.dma("sync", msem[0], cw[:], cw_in)
            f.dma("gpsimd", msem[1], wab[:], w_in[:, 14336:14368].rearrange("(k p) f -> p k f", p=128))
            for t in range(NT):
                slot = pss[t // 4][:, (t % 4) * 128:(t % 4) * 128 + 32]
                for k in range(KC):
                    tk = P.op("tensor", lambda e, k=k, t=t, slot=slot: e.matmul(slot, actA[:, k, t * 128:(t + 1) * 128], wab[:, k, :],
                                                                                start=(k == 0), stop=(k == KC - 1)),
                              waits=[f.tk] if k == 0 else (), count=(k == KC - 1))
                f.tk = tk
                f.op("vector", lambda e, t=t, slot=slot: e.tensor_copy(out=sc[:, S_G, t * 16:(t + 1) * 16], in_=slot[:, 0:16]))
                f.op("vector", lambda e, t=t, slot=slot: e.tensor_copy(out=sc[:, S_BETA, t * 16:(t + 1) * 16], in_=slot[:, 16:32]))
            f.op("vector", lambda e: e.tensor_tensor(out=sc[:, S_TMP, :], in0=sc[:, S_G, :], in1=tab[:, T_DTB, :], op=ALU.add))
            f.op("scalar", lambda e: e.activation(out=sc[:, S_TMP, :], in_=sc[:, S_TMP, :], func=AF.Exp))
            f.op("scalar", lambda e: e.activation(out=sc[:, S_TMP, :], in_=sc[:, S_TMP, :], func=AF.Ln, bias=1.0))
            f.op("scalar", lambda e: e.activation(out=sc[:, S_NEGA, :], in_=tab[:, T_ALOG, :], func=AF.Exp))
            f.op("vector", lambda e: e.tensor_scalar(out=sc[:, S_NEGA, :], in0=sc[:, S_NEGA, :], scalar1=-1.0, scalar2=None, op0=ALU.mult))
            f.op("vector", lambda e: e.tensor_tensor(out=sc[:, S_G, :], in0=sc[:, S_TMP, :], in1=sc[:, S_NEGA, :], op=ALU.mult))
            f.op("scalar", lambda e: e.activation(out=sc[:, S_BETA, :], in_=sc[:, S_BETA, :], func=AF.Sigmoid))
            f.op("tensor", lambda e: e.matmul(pss[0][:, 0:128], tab[:, T_U, :], sc[:, S_G, :], start=True, stop=True))
            f.op("tensor", lambda e: e.matmul(pss[0][:, 128:256], tab[:, T_ONE, :], sc[:, S_G, :], start=True, stop=True))
            f.op("vector", lambda e: e.tensor_copy(out=sc[:, S_GC, :], in_=pss[0][:, 0:128]))
            f.op("vector", lambda e: e.tensor_copy(out=sc[:, S_GL, :], in_=pss[0][:, 128:256]))
            f.op("scalar", lambda e: e.activation(out=sc[:, S_EG, :], in_=sc[:, S_GC, :], func=AF.Exp))
            f.op("scalar", lambda e: e.activation(out=sc[:, S_EGL, :], in_=sc[:, S_GL, :], func=AF.Exp))
            f.op("vector", lambda e: e.tensor_tensor(out=sc[:, S_TMP, :], in0=sc[:, S_GL, :], in1=sc[:, S_GC, :], op=ALU.subtract))
            f.op("scalar", lambda e: e.activation(out=sc[:, S_EKD, :], in_=sc[:, S_TMP, :], func=AF.Exp))
            f.op("vector", lambda e: e.tensor_tensor(out=sc[:, S_BG, :], in0=sc[:, S_BETA, :], in1=sc[:, S_EG, :], op=ALU.mult))
            sc_tok = f.tk

            def spill(pbuf_fn, dst, hidx, ssem_i):
                nb = hnew[st["st"] % 2]
                ssem = hst[st["st"] % 2]
                st["st"] += 1
                tk = pbuf_fn(nb)
                nb.wrote(tk)
                tk2 = P.dma("sync", ssem, dst, nb.t[:], waits=nb.rw())
                nb.read(tk2)
                tk3 = P.op("vector", lambda e, nb=nb, hidx=hidx: e.tensor_copy(out=hsb[:, hidx * 3:(hidx + 1) * 3], in_=nb.t[:, TOK - 3:TOK]),
                           waits=nb.rw())
                nb.read(tk3)
                return tk3

            last_h = None
            for i in range(16):
                pC = job(w_in[:, 2048 + i * 128:2048 + (i + 1) * 128], KC, rhsA, pe_waits=A.rw())
                px = job(w_in[:, 4096 + i * 128:4096 + (i + 1) * 128], KC, rhsA)
                hb = hold[st["ld"] % 2]
                st["ld"] += 1
                tk = P.op("scalar", lambda e, pC=pC, hb=hb: e.activation(out=hb.t[:], in_=pC.t[:], func=AF.Copy), waits=pC.rw() + hb.ww())
                pC.read(tk)
                hb.wrote(tk)

                def mk(nb, px=px, hb=hb):
                    t_ = P.op("vector", lambda e: e.tensor_tensor(out=nb.t[:], in0=px.t[:], in1=hb.t[:], op=ALU.mult),
                              waits=px.rw() + hb.rw() + nb.ww())
                    px.read(t_)
                    hb.read(t_)
                    return t_
                last_h = spill(mk, pc_d[i], i, 0)
            for c in range(48):
                pq = job(w_in[:, 6144 + c * 128:6144 + (c + 1) * 128], KC, rhsA)

                def mk(nb, pq=pq, c=c):
                    if c % 2 == 0:
                        t_ = P.op("scalar", lambda e: e.activation(out=nb.t[:], in_=pq.t[:], func=AF.Copy), waits=pq.rw() + nb.ww())
                    else:
                        t_ = P.op("vector", lambda e: e.tensor_copy(out=nb.t[:], in_=pq.t[:]), waits=pq.rw() + nb.ww())
                    pq.read(t_)
                    return t_
                last_h = spill(mk, pqkv_d[c], 16 + c, 0)
            for h in range(HEADS):
                pz = job(w_in[:, 12288 + h * 128:12288 + (h + 1) * 128], KC, rhsA)
                sb = stmp[st["tmp"] % 2]
                ssem = msem[2 + st["tmp"] % 2]
                st["tmp"] += 1
                tk = P.op("scalar", lambda e, pz=pz, sb=sb: e.activation(out=sb.t[:], in_=pz.t[:], func=AF.Silu), waits=pz.rw() + sb.ww())
                pz.read(tk)
                sb.wrote(tk)
                tk2 = P.dma("sync", ssem, sz_d[h], sb.t[:], waits=sb.rw())
                sb.read(tk2)
            f = Flow(start=[last_h, sc_tok])
            f.dma("sync", msem[4], hsend_t.ap(), hsb[:])
            raw_inc("gpsimd", lambda e: e.collective_compute("AllGather", ALU.bypass, replica_groups=GROUPS4,
                                                             ins=[hsend_t.ap().opt()], outs=[hall_t.ap().opt()]),
                    [f.tk], ccsem, 1)
            f.tk = (ccsem, ccsem.n)
            f.dma("sync", msem[4], hal[:], hall_t.ap().rearrange("(j p) f -> p j f", p=128))
            f.op("vector", lambda e: e.tensor_scalar(out=halo[:], in0=hal[:, 0, :], scalar1=tab[:, T_SEL, 0:1], scalar2=None, op0=ALU.mult))
            for j in range(1, 4):
                f.op("vector", lambda e, j=j: e.scalar_tensor_tensor(out=halo[:], in0=hal[:, j, :], scalar=tab[:, T_SEL, j:j + 1], in1=halo[:],
                                                                      op0=ALU.mult, op1=ALU.add))
            return f.tk

        def conv_taps(f, eng, y, xr, wcol0, K):
            for j in range(K):
                off = 3 - (K - 1) + j
                if j == 0:
                    f.op(eng, lambda e, off=off, j=j: e.tensor_scalar(out=y, in0=xr[:, off:off + TOK], scalar1=cw[:, wcol0 + j:wcol0 + j + 1],
                                                                      scalar2=None, op0=ALU.mult))
                else:
                    f.op(eng, lambda e, off=off, j=j: e.scalar_tensor_tensor(out=y, in0=xr[:, off:off + TOK], scalar=cw[:, wcol0 + j:wcol0 + j + 1],
                                                                             in1=y, op0=ALU.mult, op1=ALU.add))

        AR2 = hold[0].t

        def conv_branch(halo_tok):
            a2 = SB_BASE + 65536 * 2 + 32768
            sets = [(nc.alloc_sbuf_tensor_at("cxr0", [128, TOK + 3], F32, offset=a2), nc.alloc_sbuf_tensor_at("cy0", [128, TOK], F32, offset=a2 + 8192)),
                    (nc.alloc_sbuf_tensor_at("cxr1", [128, TOK + 3], F32, offset=a2 + 12288), nc.alloc_sbuf_tensor_at("cy1", [128, TOK], F32, offset=a2 + 20480))]
            flows = [Flow(start=[halo_tok]), Flow(start=[halo_tok])]
            ytoks = []
            for i in range(16):
                pB = job(w_in[:, i * 128:(i + 1) * 128], KC, rhsA)
                xr, y = sets[i % 2]
                f = flows[i % 2]
                f.dma("sync", msem[5 + i % 2], xr[:, 3:TOK + 3], pc_d[i])
                f.op("vector", lambda e, xr=xr, i=i: e.tensor_copy(out=xr[:, 0:3], in_=halo[:, i * 3:(i + 1) * 3]))
                conv_taps(f, "vector", y[:], xr, i * 3, 3)
                tk = f.op("vector", lambda e, y=y, pB=pB, i=i: e.tensor_tensor(out=yconv[:, i, :], in0=pB.t[:], in1=y[:], op=ALU.mult), extra=pB.rw())
                pB.read(tk)
                ytoks.append(tk)
            return ytoks[-2:]

        def delta_phase(dmode=None):
            Xraw = dalloc("Xraw", [128, 3, TOK + 3], F32)
            Y3 = dalloc("Y3", [128, 3, TOK], F32)
            kq_bf = dalloc("kq_bf", [128, 2, TOK], BF16)
            sqb = dalloc("sqb", [128, TOK], BF16)
            rinv = dalloc("rinv", [128, TOK], F32)
            HBb = dalloc("HBb", [128, NT, 4, 128], BF16)
            HBu = dalloc("HBu", [128, NT, 128], F32)
            szb = dalloc("szb", [128, TOK], BF16)
            NW = 2
            UgIb = [dalloc(f"UgIb{w}", [128, 256], F32) for w in range(NW)]
            GB = [dalloc(f"GB{w}", [128, 256], F32) for w in range(NW)]
            Dm = [dalloc(f"Dm{w}", [128, 128], F32) for w in range(NW)]
            Ebc = [dalloc(f"Ebc{w}", [128, 128], F32) for w in range(NW)]
            Mm = [dalloc(f"Mm{w}", [128, 128], F32) for w in range(NW)]
            MT = [dalloc(f"MT{w}", [128, 128], F32) for w in range(NW)]
            Rr = [dalloc(f"Rr{w}", [128, 128], F32) for w in range(NW)]
            PQ = [dalloc(f"PQ{w}", [128, 4, 128], F32) for w in range(NW)]
            SBm = [dalloc(f"SBm{w}", [128, 128], F32) for w in range(NW)]
            Tt = [dalloc(f"Tt{w}", [128, 128], BF16) for w in range(NW)]
            kbg = [dalloc(f"kbg{w}", [128, 128], BF16) for w in range(NW)]
            vbb = [dalloc(f"vbb{w}", [128, 128], BF16) for w in range(NW)]
            Xs = dalloc("Xs", [128, 256], F32)
            Xbf = dalloc("Xbf", [128, 256], BF16)
            vn = dalloc("vn", [128, 256], BF16)
            onb = dalloc("onb", [128, 128], F32)
            ssq = dalloc("ssq", [128, 8], F32)
            AB = dalloc("AB", [128, 256], F32)
            ATs = dalloc("ATs", [128, 128], F32)
            dsm = [P.dsem(f"dsm{i}") for i in range(4)]
            idn = ident[:]

            def colf(s_, th):
                return sc[:, s_, th:th + 1]

            def bulk(f, h):
                for c, idx in enumerate((h, 16 + h, 32 + h)):
                    f.dma("sync", dsm[0], Xraw[:, c, 3:TOK + 3], pqkv_d[idx])
                    f.op("vector", lambda e, c=c, idx=idx: e.tensor_copy(out=Xraw[:, c, 0:3], in_=halo[:, (16 + idx) * 3:(16 + idx) * 3 + 3]))
                    conv_taps(f, "vector", Y3[:, c, :], Xraw[:, c, :], 48 + idx * 4, 4)
                    f.op("scalar", lambda e, c=c: e.activation(out=Y3[:, c, :], in_=Y3[:, c, :], func=AF.Silu))
                    yield
                for c in (0, 1):
                    f.op("scalar", lambda e, c=c: e.activation(out=sqb[:], in_=Y3[:, c, :], func=AF.Square))
                    pr = next_ps()
                    for hf in range(2):
                        tk = f.op("tensor", lambda e, hf=hf, pr=pr: e.matmul(pr.t[:, hf * 512:(hf + 1) * 512], ones_bf[:], sqb[:, hf * 512:(hf + 1) * 512],
                                                                             start=True, stop=True), extra=pr.ww())
                    pr.wrote(tk)
                    if c == 0:
                        f.op("vector", lambda e, pr=pr: e.tensor_scalar(out=rinv[:], in0=pr.t[:], scalar1=EPS, scalar2=128.0, op0=ALU.add, op1=ALU.mult))
                    else:
                        f.op("vector", lambda e, pr=pr: e.tensor_scalar(out=rinv[:], in0=pr.t[:], scalar1=EPS, scalar2=None, op0=ALU.add))
                    pr.read(f.tk)
                    f.op("scalar", lambda e: e.sqrt(out=rinv[:], in_=rinv[:]))
                    f.op("vector", lambda e: e.reciprocal(out=rinv[:], in_=rinv[:]))
                    f.op("vector", lambda e, c=c: e.tensor_tensor(out=Y3[:, c, :], in0=Y3[:, c, :], in1=rinv[:], op=ALU.mult))
                    f.op("scalar", lambda e, c=c: e.activation(out=kq_bf[:, 1 - c, :], in_=Y3[:, c, :], func=AF.Copy))
                    yield

            def tile_flow(f, h, t, w):
                th = t * 16 + h
                tl = slice(t * 128, (t + 1) * 128)
                pab = pss[w][:, 0:256]
                pa = pss[w][:, 0:128]
                pb = pss[w][:, 128:256]
                f.op("vector", lambda e: e.tensor_scalar(out=UgIb[w][:, 0:128], in0=tab[:, T_U, :], scalar1=colf(S_G, th), scalar2=None, op0=ALU.mult))
                f.op("vector", lambda e: e.tensor_scalar(out=UgIb[w][:, 128:256], in0=idn, scalar1=colf(S_BETA, th), scalar2=None, op0=ALU.mult))
                f.op("tensor", lambda e: e.matmul(pab, tab[:, T_ONE, :], UgIb[w][:], start=True, stop=True))
                f.op("scalar", lambda e: e.activation(out=GB[w][:], in_=pab, func=AF.Copy))
                yield
                f.op("vector", lambda e: e.scalar_tensor_tensor(out=Dm[w][:], in0=GB[w][:, 0:128], scalar=colf(S_GC, th), in1=tab[:, T_NEG, :],
                                                                op0=ALU.subtract, op1=ALU.add))
                f.op("scalar", lambda e: e.activation(out=Dm[w][:], in_=Dm[w][:], func=AF.Exp))
                f.op("scalar", lambda e: e.activation(out=Ebc[w][:], in_=GB[w][:, 0:128], func=AF.Exp))
                f.op("tensor", lambda e: e.matmul(pab.rearrange("p (a b) -> p a b", a=2), kq_bf[:, 0, tl], kq_bf[:, :, tl], start=True, stop=True))
                yield
                f.op("vector", lambda e: e.tensor_tensor(out=HBb[:, t, 0, :], in0=pb, in1=Dm[w][:], op=ALU.mult))
                f.op("vector", lambda e: e.tensor_tensor(out=SBm[w][:], in0=GB[w][:, 128:256], in1=tab[:, T_STR, :], op=ALU.mult))
                f.op("vector", lambda e: e.tensor_tensor(out=Mm[w][:], in0=pa, in1=Dm[w][:], op=ALU.mult))
                f.op("vector", lambda e: e.tensor_tensor(out=Mm[w][:], in0=Mm[w][:], in1=SBm[w][:], op=ALU.mult))
                yield
                f.op("tensor", lambda e: e.transpose(pa, Mm[w][:], idn))
                f.op("scalar", lambda e: e.activation(out=MT[w][:], in_=pa, func=AF.Copy))
                f.op("vector", lambda e: e.scalar_tensor_tensor(out=Rr[w][:], in0=Mm[w][:], scalar=-1.0, in1=idn, op0=ALU.mult, op1=ALU.add))
                yield
                Pc, Qc = Mm[w][:], MT[w][:]
                for k in range(6):
                    Pn = PQ[w][:, (k % 2) * 2, :]
                    Qn = PQ[w][:, (k % 2) * 2 + 1, :]
                    f.op("tensor", lambda e, Pc=Pc, Qc=Qc: e.matmul(pa, Qc, Pc, start=True, stop=True))
                    f.op("tensor", lambda e, Pc=Pc, Qc=Qc: e.matmul(pb, Pc, Qc, start=True, stop=True))
                    f.op("scalar", lambda e, Pn=Pn: e.activation(out=Pn, in_=pa, func=AF.Copy))
                    f.op("vector", lambda e, Qn=Qn: e.tensor_copy(out=Qn, in_=pb))
                    yield
                    f.op("tensor", lambda e, Qn=Qn: e.matmul(pa, Qn, Rr[w][:], start=True, stop=True))
                    f.op("vector", lambda e: e.tensor_tensor(out=Rr[w][:], in0=pa, in1=Rr[w][:], op=ALU.add))
                    Pc, Qc = Pn, Qn
                    yield
                f.op("vector", lambda e: e.tensor_copy(out=Tt[w][:], in_=Rr[w][:]))
                f.op("tensor", lambda e: e.transpose(pa, Y3[:, 1, tl], idn))
                f.op("tensor", lambda e: e.transpose(pb, Y3[:, 2, tl], idn))
                f.op("scalar", lambda e: e.activation(out=kbg[w][:], in_=pa, func=AF.Copy, scale=colf(S_BG, th)))
                yield
                f.op("vector", lambda e: e.tensor_scalar(out=HBb[:, t, 1, :], in0=pa, scalar1=colf(S_EKD, th), scalar2=None, op0=ALU.mult))
                f.op("scalar", lambda e: e.activation(out=vbb[w][:], in_=pb, func=AF.Copy, scale=colf(S_BETA, th)))
                f.op("tensor", lambda e: e.matmul(pa, Tt[w][:], vbb[w][:], start=True, stop=True))
                f.op("tensor", lambda e: e.matmul(pb, kbg[w][:], Tt[w][:], start=True, stop=True))
                yield
                f.op("scalar", lambda e: e.activation(out=HBu[:, t, :], in_=pa, func=AF.Copy))
                f.op("vector", lambda e: e.tensor_copy(out=HBb[:, t, 2, :], in_=pb))
                f.op("vector", lambda e: e.tensor_tensor(out=HBb[:, t, 3, :], in0=kq_bf[:, 1, tl], in1=Ebc[w][:], op=ALU.mult))
                yield

            def drain(g):
                for _ in g:
                    pass

            pab0 = pss[0][:, 0:256]
            gs_toks = []
            prev = []
            NHD = 1 if dmode in ("delta1", "delta1c", "delta1x") else HEADS
            for h in range(NHD):
                fb = Flow(start=prev)
                drain(bulk(fb, h))
                flows = [Flow(start=[fb.tk]) for _ in range(NW)]
                for t0 in range(0, NT, NW):
                    interleave([tile_flow(flows[w], h, t0 + w, w) for w in range(NW)])
                f = Flow(start=[fl.tk for fl in flows])
                f.op("vector", lambda e: e.memset(Xs[:, 0:128], 0.0))
                f.op("vector", lambda e: e.tensor_copy(out=Xs[:, 128:256], in_=idn))
                f.op("vector", lambda e: e.tensor_copy(out=Xbf[:], in_=Xs[:]))
                for t in range(NT):
                    th = t * 16 + h
                    f.op("tensor", lambda e, t=t: e.matmul(pab0, HBb[:, t, 2, :], Xbf[:], start=True, stop=True))
                    f.op("vector", lambda e, t=t: e.tensor_tensor(out=vn[:, 0:128], in0=HBu[:, t, :], in1=pab0[:, 0:128], op=ALU.subtract))
                    f.op("vector", lambda e: e.tensor_scalar(out=vn[:, 128:256], in0=pab0[:, 128:256], scalar1=-1.0, scalar2=None, op0=ALU.mult))
                    f.op("tensor", lambda e, t=t: e.matmul(pab0, HBb[:, t, 1, :], vn[:], start=True, stop=True))
                    f.op("vector", lambda e, th=th: e.scalar_tensor_tensor(out=Xs[:], in0=Xs[:], scalar=colf(S_EGL, th), in1=pab0, op0=ALU.mult, op1=ALU.add))
                    f.op("scalar", lambda e: e.activation(out=Xbf[:], in_=Xs[:], func=AF.Copy))
                f.dma("sync", dsm[1], gsend_t[h // 4].ap()[(h % 4) * 128:(h % 4 + 1) * 128, :], Xs[:])
                gs_toks.append(f.tk)
                f.dma("sync", dsm[1], spb_d[h], HBb[:].rearrange("p a b c -> p (a b c)"))
                f.dma("sync", dsm[1], spu_d[h], HBu[:].rearrange("p a b -> p (a b)"))
                prev = [f.tk]
            if dmode == "delta1":
                return
            cc_toks = []
            for q in range(4):
                if dmode == "delta1x":
                    cc_toks.append(prev[0])
                    continue
                raw_inc("gpsimd", lambda e, q=q: e.collective_compute("AllGather", ALU.bypass, replica_groups=GROUPS4,
                                                                      ins=[gsend_t[q].ap().opt()], outs=[gall_t[q].ap().opt()]),
                        gs_toks + prev, ccsem, 1)
                cc_toks.append((ccsem, ccsem.n))
            pa = pss[0][:, 0:128]
            pb = pss[0][:, 128:256]
            Ss = Xs[:, 0:128]
            Sbf = Xbf[:, 0:128]
            for h in range(NHD):
                f = Flow(start=[cc_toks[h // 4]] + prev)
                f.op("vector", lambda e: e.memset(Ss, 0.0))
                for j in range(3):
                    f.dma("sync", dsm[2], AB[:], gall_t[h // 4].ap()[(j * 4 + h % 4) * 128:(j * 4 + h % 4 + 1) * 128, :])
                    f.op("tensor", lambda e: e.transpose(pa, AB[:, 128:256], idn))
                    f.op("scalar", lambda e: e.activation(out=ATs[:], in_=pa, func=AF.Copy))
                    f.op("tensor", lambda e: e.matmul(pb, ATs[:], Ss, start=True, stop=True))
                    f.op("vector", lambda e: e.tensor_tensor(out=ATs[:], in0=pb, in1=AB[:, 0:128], op=ALU.add))
                    f.op("vector", lambda e: e.tensor_tensor(out=ATs[:], in0=ATs[:], in1=Ss, op=ALU.subtract))
                    f.op("vector", lambda e, j=j: e.scalar_tensor_tensor(out=Ss, in0=ATs[:], scalar=tab[:, T_SEL, 4 + j:5 + j], in1=Ss,
                                                                          op0=ALU.mult, op1=ALU.add))
                f.op("vector", lambda e: e.tensor_copy(out=Sbf, in_=Ss))
                f.dma("sync", dsm[3], HBb[:].rearrange("p a b c -> p (a b c)"), spb_d[h])
                f.dma("sync", dsm[3], HBu[:].rearrange("p a b -> p (a b)"), spu_d[h])
                f.dma("sync", dsm[3], szb[:], sz_d[h])
                for t in range(NT):
                    th = t * 16 + h
                    tl = slice(t * 128, (t + 1) * 128)
                    f.op("tensor", lambda e, t=t: e.matmul(pa, HBb[:, t, 2, :], Sbf, start=True, stop=True))
                    f.op("vector", lambda e, t=t: e.tensor_tensor(out=vn[:, 0:128], in0=HBu[:, t, :], in1=pa, op=ALU.subtract))
                    f.op("tensor", lambda e, t=t: e.matmul(pb, HBb[:, t, 3, :], Sbf, start=True, stop=False))
                    f.op("tensor", lambda e, t=t: e.matmul(pb, HBb[:, t, 0, :], vn[:, 0:128], start=False, stop=True))
                    f.op("tensor", lambda e, t=t: e.matmul(pa, HBb[:, t, 1, :], vn[:, 0:128], start=True, stop=True))
                    f.op("vector", lambda e, th=th: e.scalar_tensor_tensor(out=Ss, in0=Ss, scalar=colf(S_EGL, th), in1=pa, op0=ALU.mult, op1=ALU.add))
                    f.op("scalar", lambda e: e.activation(out=Sbf, in_=Ss, func=AF.Copy))
                    f.op("vector", lambda e: e.memset(ssq[:, 0:1], 0.0))
                    f.op("scalar", lambda e: e.activation(out=onb[:], in_=pb, func=AF.Square, accum_out=ssq[:, 0:1]))
                    f.op("vector", lambda e: e.tensor_scalar(out=ssq[:, 0:1], in0=ssq[:, 0:1], scalar1=1.0 / 128, scalar2=EPS, op0=ALU.mult, op1=ALU.add))
                    f.op("scalar", lambda e: e.sqrt(out=ssq[:, 0:1], in_=ssq[:, 0:1]))
                    f.op("vector", lambda e: e.reciprocal(out=ssq[:, 0:1], in_=ssq[:, 0:1]))
                    f.op("vector", lambda e: e.scalar_tensor_tensor(out=onb[:], in0=pb, scalar=ssq[:, 0:1], in1=tab[:, T_GOUT, :], op0=ALU.mult, op1=ALU.mult))
                    f.op("tensor", lambda e: e.transpose(pa, onb[:], idn))
                    f.op("vector", lambda e, tl=tl, h=h: e.tensor_tensor(out=ydn[:, h, tl], in0=pa, in1=szb[:, tl], op=ALU.mult))
                prev = [f.tk]

        def merge_and_out():
            norm_stage(1)
            for i in range(KC):
                pgc = job(w_in[:, 14368 + i * 128:14368 + (i + 1) * 128], KC, rhsA, pe_waits=A.rw())
                hb1 = hold[st["ld"] % 2]; st["ld"] += 1
                tk = P.op("scalar", lambda e, pgc=pgc, hb1=hb1: e.activation(out=hb1.t[:], in_=pgc.t[:], func=AF.Sigmoid), waits=pgc.rw() + hb1.ww())
                pgc.read(tk); hb1.wrote(tk)
                pcp = job(w_cb[:, i * 128:(i + 1) * 128], 16, lambda k, hf: yconv[:, k, hf * 512:(hf + 1) * 512])
                nb = hnew[st["st"] % 2]; st["st"] += 1
                tk = P.op("vector", lambda e, pcp=pcp, hb1=hb1, nb=nb: e.tensor_tensor(out=nb.t[:], in0=pcp.t[:], in1=hb1.t[:], op=ALU.mult),
                          waits=pcp.rw() + hb1.rw() + nb.ww())
                pcp.read(tk); hb1.read(tk); nb.wrote(tk)
                pgd = job(w_in[:, 18464 + i * 128:18464 + (i + 1) * 128], KC, rhsA)
                hb2 = hold[st["ld"] % 2]; st["ld"] += 1
                tk = P.op("scalar", lambda e, pgd=pgd, hb2=hb2: e.activation(out=hb2.t[:], in_=pgd.t[:], func=AF.Sigmoid), waits=pgd.rw() + hb2.ww())
                pgd.read(tk); hb2.wrote(tk)
                pdp = job(w_db[:, i * 128:(i + 1) * 128], 16, lambda k, hf: ydn[:, k, hf * 512:(hf + 1) * 512])
                tk = P.op("vector", lambda e, pdp=pdp, hb2=hb2: e.tensor_tensor(out=hb2.t[:], in0=pdp.t[:], in1=hb2.t[:], op=ALU.mult),
                          waits=pdp.rw() + hb2.rw())
                pdp.read(tk); hb2.wrote(tk)
                sb = stmp[st["tmp"] % 2]
                ssem = msem[2 + st["tmp"] % 2]
                st["tmp"] += 1
                tk = P.op("vector", lambda e, sb=sb, nb=nb, hb2=hb2: e.tensor_tensor(out=sb.t[:], in0=nb.t[:], in1=hb2.t[:], op=ALU.add),
                          waits=nb.rw() + hb2.rw() + sb.ww())
                nb.read(tk); hb2.read(tk); sb.wrote(tk)
                tk2 = P.dma("sync", ssem, mrg_d[i], sb.t[:], waits=sb.rw())
                sb.read(tk2)
            P.barrier()
            toks = []
            for q4 in range(4):
                toks.append(P.dma("sync", msem[4 + q4 % 2], actA[:, q4 * 8:(q4 + 1) * 8, :], mrg_d[q4 * 8:(q4 + 1) * 8].rearrange("k p t -> p k t")))
            A.wrote(toks[0])
            for tk in toks[1:]:
                A.wrote_more(tk)
            for i in range(KC):
                po = job(w_out[:, i * 128:(i + 1) * 128], KC, rhsA, pe_waits=A.rw())
                resid_update(i, po, 1.0)


    if MIX and DELTA:
        P.dma("sync", msem[0], tab[:], tab_in)
        P.dma("sync", msem[0], cw[:], cw_in)
        P.dma("sync", msem[1], sc[:].rearrange("p a b -> p (a b)"), sc_in)
        P.dma("sync", msem[4], halo[:], halo_in)
        P.barrier()
        delta_phase(debug)
        P.barrier()
        dbg = nc.dram_tensor("dbg", [128, 16 * TOK], BF16, kind="ExternalOutput").ap()
        P.dma("sync", msem[0], dbg, ydn[:].rearrange("p a b -> p (a b)"))
        P.barrier()
        with nc.Block() as block:
            P.replay(block)
        return nc, stack
    stage_T0()
    P.barrier()
    if debug not in ("mixA", "mixB"):
        ffn(0, w1g, w1u, w1d)
        P.barrier()
    if debug != "hT":
        htok = mixer_front()
        P.barrier()
        conv_branch(htok)
        P.barrier()
        if debug != "mixA":
            delta_phase()
            P.barrier()
        if debug in ("mixA", "mixB"):
            dbg = nc.dram_tensor("dbg", [128, 16 * TOK], BF16, kind="ExternalOutput").ap()
            dbg2 = nc.dram_tensor("dbg2", [128, 12 * 128], F32, kind="ExternalOutput").ap()
            dbg3 = nc.dram_tensor("dbg3", [128, 192], F32, kind="ExternalOutput").ap()
            src = yconv if debug == "mixA" else ydn
            P.dma("sync", msem[0], dbg, src[:].rearrange("p a b -> p (a b)"))
            P.dma("sync", msem[1], dbg2, sc[:].rearrange("p a b -> p (a b)"))
            P.dma("sync", msem[4], dbg3, halo[:])
        else:
            merge_and_out()
            P.barrier()
            if debug != "h2":
                ffn(2, w2g, w2u, w2d)
                P.barrier()
                final_stage()
    P.barrier()

    with nc.Block() as block:
        P.replay(block)
    return nc, stack


def make_in_maps(inputs, debug=None):
    f = lambda k: np.asarray(inputs[k], dtype=np.float32)
    x = f("x").reshape(NCORES, TOK, D)
    gl = lambda v: np.asarray(v, np.float32).reshape(KC, 128).T
    gam = np.ascontiguousarray(np.concatenate([gl(f("ffn1_norm")[0]), gl(f("mix_norm")[0]),
                                               gl(f("ffn2_norm")[0]), gl(f("final_norm"))], axis=1))
    common = {
        "gam": gam,
        "w1g": f("ffn1_w_gate")[0], "w1u": f("ffn1_w_up")[0], "w1d": f("ffn1_w_down")[0],
        "ident": np.eye(128, dtype=np.float32),
    }
    if debug != "hT":
        common.update({
            "w2g": f("ffn2_w_gate")[0], "w2u": f("ffn2_w_up")[0], "w2d": f("ffn2_w_down")[0],
            "w_in": f("w_in")[0], "w_cb": f("w_conv_branch")[0], "w_db": f("w_dn_branch")[0], "w_out": f("w_out")[0],
        })
        cw = np.zeros((128, 240), np.float32)
        cw[:, 0:48] = f("conv_mixer_w")[0].reshape(16, 128, 3).transpose(1, 0, 2).reshape(128, 48)
        cw[:, 48:240] = f("dn_conv_w")[0].reshape(48, 128, 4).transpose(1, 0, 2).reshape(128, 192)
        common["cw"] = cw
        s_ = np.arange(128)[:, None]
        c_ = np.arange(128)[None, :]
        tab = np.zeros((128, 9, 128), np.float32)
        tab[:, 0] = (s_ <= c_)
        tab[:, 1] = np.where(c_ >= s_, 0.0, -30000.0)
        tab[:, 2] = (c_ > s_)
        tab[:, 3] = 1.0
        tab[:, 4] = np.tile(f("dn_dt_bias")[0], NT)[None, :]
        tab[:, 5] = np.tile(f("dn_a_log")[0], NT)[None, :]
        tab[:, 6] = f("dn_out_norm")[0][None, :]
    maps = []
    for c in range(NCORES):
        m = dict(common)
        m["x"] = np.ascontiguousarray(x[c])
        if debug != "hT":
            t = tab.copy()
            r = c % 4
            if r > 0:
                t[:, 7, r - 1] = 1.0
            for j in range(3):
                if j < r:
                    t[:, 7, 4 + j] = 1.0
            m["tab"] = t
        maps.append(m)
    return maps


def kernel(**inputs):
    nc, stack = build()
    in_maps = make_in_maps(inputs)
    res = run_bass_kernel_spmd(nc, in_maps, core_ids=list(range(NCORES)))
    outs = [np.asarray(r["out"]) for r in res.results]
    return np.stack(outs, 0).reshape(2, 4096, D).astype(np.float32)
```

```python
import contextlib
import numpy as np
import concourse.bass as bass
import concourse.mybir as mybir
from concourse.bass_utils import run_bass_kernel_spmd

F32 = mybir.dt.float32
BF16 = mybir.dt.bfloat16
AF = mybir.ActivationFunctionType
ALU = mybir.AluOpType

NCORES = 8
TOK = 1024
D = 4096
KC = 32
DFF = 11008
NIN = 22560
EPS = 1e-6
HEADS = 16
NT = 8
ENGS = ("sync", "scalar", "vector", "gpsimd", "tensor")


class Sem:
    def __init__(self, h, name):
        self.h = h
        self.n = 0
        self.name = name


class Prog:
    def __init__(self, nc, stack):
        self.nc = nc
        self.stack = stack
        self.ops = {e: [] for e in ENGS}
        self.nsem = 0
        self.done = {e: self.sem("dn_" + e) for e in ENGS}
        self.seen = {e: {} for e in ENGS}
        self.dma_sems = []

    def sem(self, name):
        self.nsem += 1
        return Sem(self.stack.enter_context(self.nc.semaphore(name)), name)

    def dsem(self, name):
        s = self.sem(name)
        self.dma_sems.append(s)
        return s

    def _waits(self, eng, waits):
        best = {}
        for tok in waits:
            if tok is None:
                continue
            s, v = tok
            if v > best.get(s, (None, 0))[1]:
                best[s] = (s, v)
        out = []
        seen = self.seen[eng]
        for s, v in best.values():
            if seen.get(s, 0) >= v:
                continue
            seen[s] = v
            out.append((s.h, v))
        return out

    def op(self, eng, fn, waits=(), count=True):
        w = self._waits(eng, waits)
        inc = None
        tok = None
        if count:
            s = self.done[eng]
            if s.n >= 30000:
                s = self.done[eng] = self.sem("dn_" + eng + str(self.nsem))
            s.n += 1
            inc = (s.h, 1)
            tok = (s, s.n)
        self.ops[eng].append((w, fn, inc))
        return tok

    def dma(self, eng, sem, out, in_, waits=()):
        w = self._waits(eng, waits)
        sem.n += 16
        self.ops[eng].append((w, lambda e: e.dma_start(out=out, in_=in_), (sem.h, 16)))
        return (sem, sem.n)

    def wait_only(self, eng, waits):
        w = self._waits(eng, waits)
        if w:
            self.ops[eng].append((w, None, None))

    def barrier(self):
        toks = [(s, s.n) for s in self.done.values()] + [(s, s.n) for s in self.dma_sems]
        for en in ENGS:
            self.wait_only(en, toks)

    def replay(self, block):
        def mk(lst):
            def run(e):
                for w, fn, inc in lst:
                    for h, v in w:
                        e.wait_ge(h, v)
                    if fn is None:
                        continue
                    ins = fn(e)
                    if inc is not None:
                        ins.then_inc(inc[0], inc[1])
            return run
        for en in ENGS:
            if self.ops[en]:
                getattr(block, en)(mk(self.ops[en]))


class Buf:
    def __init__(self, t):
        self.t = t
        self.ready = []
        self.readers = []

    def ww(self):
        return self.ready + self.readers

    def wrote(self, tok):
        self.ready = [tok]
        self.readers = []

    def wrote_more(self, tok):
        self.ready.append(tok)

    def rw(self):
        return list(self.ready)

    def read(self, tok):
        self.readers.append(tok)


def build(debug=None):
    nc = bass.Bass("TRN2", target_bir_lowering=False)
    stack = contextlib.ExitStack()
    P = Prog(nc, stack)

    def din(name, shape):
        return nc.dram_tensor(name, list(shape), F32, kind="ExternalInput").ap()

    x = din("x", [TOK, D])
    gam_in = din("gam", [128, 4 * KC])
    DBGM = debug in ("mixA", "mixB") or (isinstance(debug, str) and debug.startswith("delta"))
    if not DBGM:
        w1g = din("w1g", [D, DFF]); w1u = din("w1u", [D, DFF]); w1d = din("w1d", [DFF, D])
    if debug != "hT" and not (isinstance(debug, str) and debug.startswith("delta")):
        w_in = din("w_in", [D, NIN])
        if not DBGM:
            w2g = din("w2g", [D, DFF]); w2u = din("w2u", [D, DFF]); w2d = din("w2d", [DFF, D])
            w_cb = din("w_cb", [2048, D]); w_db = din("w_db", [2048, D]); w_out = din("w_out", [D, D])
    ident_in = din("ident", [128, 128])
    MIX = debug != "hT"
    if MIX:
        cw_in = din("cw", [128, 240])
        tab_in = din("tab", [128, 9, 128])
        def dscr(name, shape, dt=F32):
            return nc.dram_tensor(name, list(shape), dt).ap()
        DELTA = isinstance(debug, str) and debug.startswith("delta")
        pc_d = dscr("pc_d", [16, 128, TOK])
        if DELTA:
            pqkv_d = din("pqkv_in", [48, 128, TOK])
            sz_d = nc.dram_tensor("sz_in", [16, 128, TOK], BF16, kind="ExternalInput").ap()
            sc_in = din("sc_in", [128, 12 * 128]); halo_in = din("halo_in", [128, 192])
        else:
            pqkv_d = dscr("pqkv_d", [48, 128, TOK])
            sz_d = dscr("sz_d", [16, 128, TOK], BF16)
        spb_d = dscr("spb_d", [16, 128, 8 * 4 * 128], BF16); spu_d = dscr("spu_d", [16, 128, 8 * 128])
        gsend_t = [nc.dram_tensor(f"gsend{q}", [4 * 128, 256], F32) for q in range(4)]
        gall_t = [nc.dram_tensor(f"gall{q}", [4 * 4 * 128, 256], F32) for q in range(4)]
        hsend_t = nc.dram_tensor("hsend", [128, 192], F32); hall_t = nc.dram_tensor("hall", [4 * 128, 192], F32)
        mrg_d = dscr("mrg_d", [KC, 128, TOK], BF16)
    out = nc.dram_tensor("out", [TOK, D], F32, kind="ExternalOutput").ap()
    hT = nc.dram_tensor("hT", [KC, 128, TOK], F32,
                        kind=("ExternalOutput" if debug in ("hT", "h2") else "Internal")).ap()

    SB_BASE = 16512
    SB_END = 229375
    cur = [SB_BASE]

    def alloc(name, shape, dt, at=None):
        n = int(np.prod(shape[1:])) * (4 if dt == F32 else 2)
        if at is None:
            at = cur[0]
            cur[0] += (n + 31) // 32 * 32
        return nc.alloc_sbuf_tensor_at(name, list(shape), dt, offset=at)

    actA = alloc("actA", [128, KC, TOK], BF16)
    regB = cur[0]
    actT = alloc("actT", [128, 29, TOK], BF16)
    cur[0] = regB + 64 * 1024
    NWB = 4
    wbuf = [Buf(alloc(f"wb{i}", [128, KC, 128], BF16)) for i in range(NWB)]
    hold = [Buf(alloc(f"hold{i}", [128, TOK], F32)) for i in range(2)]
    hnew = [Buf(alloc(f"hnew{i}", [128, TOK], F32)) for i in range(2)]
    stmp = [Buf(alloc(f"stmp{i}", [128, TOK], BF16)) for i in range(2)]
    rstd = Buf(alloc("rstd", [128, TOK], F32))
    ident = alloc("ident", [128, 128], F32)
    ones_bf = alloc("ones_bf", [128, 128], BF16)
    gam = alloc("gam", [128, 4, KC], F32)
    xin = Buf(alloc("xin", [128, D], F32, at=regB))
    xst = [Buf(alloc(f"xst{i}", [128, 4, 128], F32, at=regB + 16384 + 2048 * i)) for i in range(2)]
    yconv = alloc("yconv", [128, 16, TOK], BF16, at=regB)
    ydn = alloc("ydn", [128, 16, TOK], BF16, at=regB + 32768)
    tab = alloc("tab", [128, 9, 128], F32)
    cw = alloc("cw", [128, 240], F32)
    sc = alloc("sc", [128, 12, 128], F32)
    hsb = alloc("hsb", [128, 192], F32)
    hal = alloc("hal", [128, 4, 192], F32)
    halo = alloc("halo", [128, 192], F32)
    wab = alloc("wab", [128, KC, 32], BF16)
    assert cur[0] <= SB_END, cur[0]
    dcur = [SB_BASE]
    def dalloc(name, shape, dt):
        n = int(np.prod(shape[1:])) * (4 if dt == F32 else 2)
        sz_ = (n + 31) // 32 * 32
        if dcur[0] + sz_ > SB_BASE + 65536 and dcur[0] <= SB_BASE + 65536:
            dcur[0] = SB_BASE + 163840
        at = dcur[0]
        dcur[0] += sz_
        assert dcur[0] <= SB_BASE + 65536 or (at >= SB_BASE + 163840 and dcur[0] <= SB_BASE + 163840 + 24576), (name, dcur[0])
        return nc.alloc_sbuf_tensor_at(name, list(shape), dt, offset=at)

    NPS = 3
    psum = [Buf(stack.enter_context(nc.psum_tensor(f"ps{i}", [128, TOK], F32))) for i in range(NPS)]
    pss = [stack.enter_context(nc.psum_tensor(f"pss{i}", [128, 512], F32)) for i in range(2)]
    A = Buf(actA)
    ACT = Buf(actT)
    HT = [Buf(None) for _ in range(KC)]

    c_ld = P.dsem("c_ld")
    ctok = [P.dma("sync", c_ld, ident[:], ident_in),
            P.dma("sync", c_ld, gam[:].rearrange("p a b -> p (a b)"), gam_in),
            P.op("vector", lambda e: e.memset(ones_bf[:], 1.0))]

    st = {"wb": 0, "ps": 0, "ld": 0, "st": 0, "tmp": 0}
    wsem = [P.dsem(f"wld{i}") for i in range(NWB)]
    hld = [P.dsem(f"hld{i}") for i in range(2)]
    hst = [P.dsem(f"hst{i}") for i in range(2)]

    def next_ps():
        p = psum[st["ps"] % NPS]
        st["ps"] += 1
        return p

    def job(W_ap, nk, rhs_fn, pe_waits=()):
        b = st["wb"] % NWB
        st["wb"] += 1
        wb = wbuf[b]
        pr = next_ps()
        src = W_ap.rearrange("(k p) f -> p k f", p=128)
        tok = P.dma("gpsimd", wsem[b], wb.t[:, 0:nk, :], src, waits=wb.ww())
        wb.wrote(tok)
        waits = wb.rw() + pr.ww() + list(pe_waits) + ctok
        last = None
        for k in range(nk):
            for hf in range(2):
                fin = (k == nk - 1 and hf == 1)
                fn = (lambda e, k=k, hf=hf: e.matmul(pr.t[:, hf * 512:(hf + 1) * 512], wb.t[:, k, :], rhs_fn(k, hf),
                                                     start=(k == 0), stop=(k == nk - 1)))
                last = P.op("tensor", fn, waits=waits if (k == 0 and hf == 0) else (), count=fin)
        wb.read(last)
        pr.wrote(last)
        return pr

    xld = P.dsem("xld")
    xsts = [P.dsem(f"xsts{i}") for i in range(2)]

    def stage_T0():
        cnt = 0
        for t in range(NT):
            tok = P.dma("sync", xld, xin.t[:], x[t * 128:(t + 1) * 128, :], waits=xin.ww())
            xin.wrote(tok)
            for kg in range(KC // 4):
                pr = next_ps()
                last = None
                for kk in range(4):
                    k = kg * 4 + kk
                    last = P.op("tensor", lambda e, k=k, kk=kk, pr=pr: e.transpose(pr.t[:, kk * 128:(kk + 1) * 128],
                                                                                   xin.t[:, k * 128:(k + 1) * 128], ident[:]),
                                waits=(xin.rw() + pr.ww() + ctok) if kk == 0 else (), count=(kk == 3))
                pr.wrote(last)
                xin.read(last)
                si = cnt % 2
                cnt += 1
                sb = xst[si]
                tk = P.op("vector", lambda e, pr=pr, sb=sb: e.tensor_copy(out=sb.t[:].rearrange("p a b -> p (a b)"), in_=pr.t[:, 0:512]),
                          waits=pr.rw() + sb.ww())
                pr.read(tk)
                sb.wrote(tk)
                dst = hT[kg * 4:(kg + 1) * 4, :, t * 128:(t + 1) * 128].rearrange("k p j -> p k j")
                tk2 = P.dma("sync", xsts[si], dst, sb.t[:], waits=sb.rw())
                sb.read(tk2)
                for k in range(kg * 4, kg * 4 + 4):
                    HT[k].wrote_more(tk2)

    def load_h(k):
        b = st["ld"] % 2
        st["ld"] += 1
        hb = hold[b]
        tok = P.dma("sync", hld[b], hb.t[:], hT[k], waits=hb.ww() + HT[k].rw())
        hb.wrote(tok)
        HT[k].read(tok)
        return hb

    def norm_stats():
        pr = next_ps()
        last = None
        for k in range(KC):
            hb = load_h(k)
            sb = stmp[st["tmp"] % 2]
            st["tmp"] += 1
            tk = P.op("scalar", lambda e, hb=hb, sb=sb: e.activation(out=sb.t[:], in_=hb.t[:], func=AF.Square),
                      waits=hb.rw() + sb.ww())
            hb.read(tk)
            sb.wrote(tk)
            for hf in range(2):
                last = P.op("tensor", lambda e, k=k, hf=hf, sb=sb, pr=pr: e.matmul(pr.t[:, hf * 512:(hf + 1) * 512], ones_bf[:],
                                                                                   sb.t[:, hf * 512:(hf + 1) * 512],
                                                                                   start=(k == 0), stop=(k == KC - 1)),
                            waits=(sb.rw() + (pr.ww() + ctok if k == 0 else [])) if hf == 0 else (), count=(hf == 1))
            sb.read(last)
        pr.wrote(last)
        t1 = P.op("vector", lambda e: e.tensor_scalar(out=rstd.t[:], in0=pr.t[:], scalar1=1.0 / D, scalar2=EPS, op0=ALU.mult, op1=ALU.add),
                  waits=pr.rw() + rstd.ww())
        pr.read(t1)
        t2 = P.op("scalar", lambda e: e.sqrt(out=rstd.t[:], in_=rstd.t[:]), waits=[t1])
        t3 = P.op("vector", lambda e: e.reciprocal(out=rstd.t[:], in_=rstd.t[:]), waits=[t2])
        rstd.wrote(t3)

    def norm_stage(gi):
        norm_stats()
        first = True
        for k in range(KC):
            hb = load_h(k)
            tk = P.op("vector", lambda e, k=k, hb=hb: e.scalar_tensor_tensor(out=actA[:, k, :], in0=hb.t[:], scalar=gam[:, gi, k:k + 1], in1=rstd.t[:],
                                                                             op0=ALU.mult, op1=ALU.mult),
                      waits=hb.rw() + rstd.rw() + (A.ww() if first else []))
            first = False
            hb.read(tk)
            rstd.read(tk)
            if k == 0:
                A.wrote(tk)
            else:
                A.wrote_more(tk)

    def resid_update(i, pd, scale):
        hb = load_h(i)
        nb = hnew[st["st"] % 2]
        ssem = hst[st["st"] % 2]
        st["st"] += 1
        tk = P.op("vector", lambda e: e.scalar_tensor_tensor(out=nb.t[:], in0=pd.t[:], scalar=scale, in1=hb.t[:],
                                                             op0=ALU.mult, op1=ALU.add),
                  waits=pd.rw() + hb.rw() + nb.ww())
        pd.read(tk)
        hb.read(tk)
        nb.wrote(tk)
        tk2 = P.dma("sync", ssem, hT[i], nb.t[:], waits=nb.rw() + HT[i].ww())
        nb.read(tk2)
        HT[i].wrote(tk2)

    def ffn(gi, Wg, Wu, Wd):
        norm_stage(gi)
        groups = [(0, 29), (29, 29), (58, 28)]
        for gidx, (f0, G) in enumerate(groups):
            for jj in range(G):
                j = f0 + jj
                pg = job(Wg[:, j * 128:(j + 1) * 128], KC, lambda k, hf: actA[:, k, hf * 512:(hf + 1) * 512], pe_waits=A.rw())
                pu = job(Wu[:, j * 128:(j + 1) * 128], KC, lambda k, hf: actA[:, k, hf * 512:(hf + 1) * 512])
                A.read(pu.ready[0])
                sb = stmp[st["tmp"] % 2]
                st["tmp"] += 1
                tk = P.op("scalar", lambda e, pg=pg, sb=sb: e.activation(out=sb.t[:], in_=pg.t[:], func=AF.Silu),
                          waits=pg.rw() + sb.ww())
                pg.read(tk)
                sb.wrote(tk)
                tk2 = P.op("vector", lambda e, pu=pu, sb=sb, jj=jj: e.tensor_tensor(out=actT[:, jj, :], in0=pu.t[:], in1=sb.t[:], op=ALU.mult),
                           waits=pu.rw() + sb.rw() + (ACT.ww() if jj == 0 else []))
                pu.read(tk2)
                sb.read(tk2)
                if jj == 0:
                    ACT.wrote(tk2)
                else:
                    ACT.wrote_more(tk2)
            for i in range(KC):
                pd = job(Wd[f0 * 128:(f0 + G) * 128, i * 128:(i + 1) * 128], G,
                         lambda k, hf: actT[:, k, hf * 512:(hf + 1) * 512], pe_waits=ACT.rw())
                ACT.read(pd.ready[0])
                resid_update(i, pd, 0.5)

    osem = P.dsem("osem")

    def final_stage():
        norm_stats()
        cnt = 0
        for k in range(KC):
            hb = load_h(k)
            nb = hnew[st["st"] % 2]
            st["st"] += 1
            tk = P.op("vector", lambda e, k=k, hb=hb, nb=nb: e.scalar_tensor_tensor(out=nb.t[:], in0=hb.t[:], scalar=gam[:, 3, k:k + 1], in1=rstd.t[:],
                                                                                    op0=ALU.mult, op1=ALU.mult),
                      waits=hb.rw() + rstd.rw() + nb.ww())
            hb.read(tk)
            rstd.read(tk)
            nb.wrote(tk)
            for tg in range(2):
                pr = next_ps()
                last = None
                for tt in range(4):
                    t = tg * 4 + tt
                    last = P.op("tensor", lambda e, t=t, tt=tt, pr=pr, nb=nb: e.transpose(pr.t[:, tt * 128:(tt + 1) * 128],
                                                                                          nb.t[:, t * 128:(t + 1) * 128], ident[:]),
                                waits=(nb.rw() + pr.ww() + ctok) if tt == 0 else (), count=(tt == 3))
                pr.wrote(last)
                nb.read(last)
                sb = xst[cnt % 2]
                cnt += 1
                tk = P.op("vector", lambda e, pr=pr, sb=sb: e.tensor_copy(out=sb.t[:].rearrange("p a b -> p (a b)"), in_=pr.t[:, 0:512]),
                          waits=pr.rw() + sb.ww())
                pr.read(tk)
                sb.wrote(tk)
                dst = out[tg * 512:(tg + 1) * 512, k * 128:(k + 1) * 128].rearrange("(tt p) f -> p tt f", p=128)
                tk2 = P.dma("sync", xsts[(cnt - 1) % 2], dst, sb.t[:], waits=sb.rw())
                sb.read(tk2)

    if MIX:
        class Flow:
            def __init__(self, start=()):
                self.tk = None
                self.start = list(start)
                self.q = []
                self.lazy = False

            def op(self, eng, fn, extra=()):
                if self.lazy:
                    self.q.append(("op", eng, fn, list(extra)))
                    return None
                self.tk = P.op(eng, fn, waits=[self.tk] + self.start + list(extra))
                self.start = []
                return self.tk

            def dma(self, eng, sem, out_, in_, extra=()):
                if self.lazy:
                    self.q.append(("dma", eng, sem, out_, in_, list(extra)))
                    return None
                self.tk = P.dma(eng, sem, out_, in_, waits=[self.tk] + self.start + list(extra))
                self.start = []
                return self.tk

            def call(self, fn):
                if self.lazy:
                    self.q.append(("call", fn))
                else:
                    fn()

            def emit_one(self):
                it = self.q.pop(0)
                self.lazy = False
                if it[0] == "op":
                    self.op(it[1], it[2], it[3])
                elif it[0] == "dma":
                    self.dma(it[1], it[2], it[3], it[4], it[5])
                else:
                    it[1]()
                self.lazy = True

        def run_flows(pairs):
            flows = []
            for f, g in pairs:
                f.lazy = True
                for _ in g:
                    pass
                flows.append(f)
            lens = [len(f.q) for f in flows]
            R = max(lens) if lens else 0
            done = [0] * len(flows)
            for r in range(R):
                for i, f in enumerate(flows):
                    tgt = ((r + 1) * lens[i] + R - 1) // R
                    while done[i] < tgt and f.q:
                        f.emit_one()
                        done[i] += 1
            for f in flows:
                while f.q:
                    f.emit_one()
                f.lazy = False

        def interleave(gens):
            gens = list(gens)
            while gens:
                nxt = []
                for g in gens:
                    try:
                        next(g)
                        nxt.append(g)
                    except StopIteration:
                        pass
                gens = nxt

        def raw_inc(eng, fn, waits, sem, amt):
            w = P._waits(eng, waits)
            sem.n += amt
            P.ops[eng].append((w, fn, (sem.h, amt)))
            return (sem, sem.n)

        rhsA = lambda k, hf: actA[:, k, hf * 512:(hf + 1) * 512]
        T_U, T_NEG, T_STR, T_ONE, T_DTB, T_ALOG, T_GOUT, T_SEL = range(8)
        S_G, S_BETA, S_GC, S_GL, S_EG, S_EGL, S_EKD, S_BG, S_NEGA, S_TMP = range(10)
        GROUPS4 = [[0, 1, 2, 3], [4, 5, 6, 7]]
        msem = [P.dsem(f"msem{i}") for i in range(8)]
        ccsem = P.dsem("ccsem")

        def mixer_front():
            norm_stage(1)
            f = Flow(start=A.rw() + ctok)
            f.dma("sync", msem[0], tab[:], tab_in)
            f.dma("sync", msem[0], cw[:], cw_in)
            f.dma("gpsimd", msem[1], wab[:], w_in[:, 14336:14368].rearrange("(k p) f -> p k f", p=128))
            for t in range(NT):
                slot = pss[t // 4][:, (t % 4) * 128:(t % 4) * 128 + 32]
                for k in range(KC):
                    tk = P.op("tensor", lambda e, k=k, t=t, slot=slot: e.matmul(slot, actA[:, k, t * 128:(t + 1) * 128], wab[:, k, :],
                                                                                start=(k == 0), stop=(k == KC - 1)),
                              waits=[f.tk] if k == 0 else (), count=(k == KC - 1))
                f.tk = tk
                f.op("vector", lambda e, t=t, slot=slot: e.tensor_copy(out=sc[:, S_G, t * 16:(t + 1) * 16], in_=slot[:, 0:16]))
                f.op("vector", lambda e, t=t, slot=slot: e.tensor_copy(out=sc[:, S_BETA, t * 16:(t + 1) * 16], in_=slot[:, 16:32]))
            f.op("vector", lambda e: e.tensor_tensor(out=sc[:, S_TMP, :], in0=sc[:, S_G, :], in1=tab[:, T_DTB, :], op=ALU.add))
            f.op("scalar", lambda e: e.activation(out=sc[:, S_TMP, :], in_=sc[:, S_TMP, :], func=AF.Exp))
            f.op("scalar", lambda e: e.activation(out=sc[:, S_TMP, :], in_=sc[:, S_TMP, :], func=AF.Ln, bias=1.0))
            f.op("scalar", lambda e: e.activation(out=sc[:, S_NEGA, :], in_=tab[:, T_ALOG, :], func=AF.Exp))
            f.op("vector", lambda e: e.tensor_scalar(out=sc[:, S_NEGA, :], in0=sc[:, S_NEGA, :], scalar1=-1.0, scalar2=None, op0=ALU.mult))
            f.op("vector", lambda e: e.tensor_tensor(out=sc[:, S_G, :], in0=sc[:, S_TMP, :], in1=sc[:, S_NEGA, :], op=ALU.mult))
            f.op("scalar", lambda e: e.activation(out=sc[:, S_BETA, :], in_=sc[:, S_BETA, :], func=AF.Sigmoid))
            f.op("tensor", lambda e: e.matmul(pss[0][:, 0:128], tab[:, T_U, :], sc[:, S_G, :], start=True, stop=True))
            f.op("tensor", lambda e: e.matmul(pss[0][:, 128:256], tab[:, T_ONE, :], sc[:, S_G, :], start=True, stop=True))
            f.op("vector", lambda e: e.tensor_copy(out=sc[:, S_GC, :], in_=pss[0][:, 0:128]))
            f.op("vector", lambda e: e.tensor_copy(out=sc[:, S_GL, :], in_=pss[0][:, 128:256]))
            f.op("scalar", lambda e: e.activation(out=sc[:, S_EG, :], in_=sc[:, S_GC, :], func=AF.Exp))
            f.op("scalar", lambda e: e.activation(out=sc[:, S_EGL, :], in_=sc[:, S_GL, :], func=AF.Exp))
            f.op("vector", lambda e: e.tensor_tensor(out=sc[:, S_TMP, :], in0=sc[:, S_GL, :], in1=sc[:, S_GC, :], op=ALU.subtract))
            f.op("scalar", lambda e: e.activation(out=sc[:, S_EKD, :], in_=sc[:, S_TMP, :], func=AF.Exp))
            f.op("vector", lambda e: e.tensor_tensor(out=sc[:, S_BG, :], in0=sc[:, S_BETA, :], in1=sc[:, S_EG, :], op=ALU.mult))
            sc_tok = f.tk

            def spill(pbuf_fn, dst, hidx, ssem_i):
                nb = hnew[st["st"] % 2]
                ssem = hst[st["st"] % 2]
                st["st"] += 1
                tk = pbuf_fn(nb)
                nb.wrote(tk)
                tk2 = P.dma("sync", ssem, dst, nb.t[:], waits=nb.rw())
                nb.read(tk2)
                tk3 = P.op("vector", lambda e, nb=nb, hidx=hidx: e.tensor_copy(out=hsb[:, hidx * 3:(hidx + 1) * 3], in_=nb.t[:, TOK - 3:TOK]),
                           waits=nb.rw())
                nb.read(tk3)
                return tk3

            last_h = None
            for i in range(16):
                pC = job(w_in[:, 2048 + i * 128:2048 + (i + 1) * 128], KC, rhsA, pe_waits=A.rw())
                px = job(w_in[:, 4096 + i * 128:4096 + (i + 1) * 128], KC, rhsA)
                hb = hold[st["ld"] % 2]
                st["ld"] += 1
                tk = P.op("scalar", lambda e, pC=pC, hb=hb: e.activation(out=hb.t[:], in_=pC.t[:], func=AF.Copy), waits=pC.rw() + hb.ww())
                pC.read(tk)
                hb.wrote(tk)

                def mk(nb, px=px, hb=hb):
                    t_ = P.op("vector", lambda e: e.tensor_tensor(out=nb.t[:], in0=px.t[:], in1=hb.t[:], op=ALU.mult),
                              waits=px.rw() + hb.rw() + nb.ww())
                    px.read(t_)
                    hb.read(t_)
                    return t_
                last_h = spill(mk, pc_d[i], i, 0)
            for c in range(48):
                pq = job(w_in[:, 6144 + c * 128:6144 + (c + 1) * 128], KC, rhsA)

                def mk(nb, pq=pq, c=c):
                    if c % 2 == 0:
                        t_ = P.op("scalar", lambda e: e.activation(out=nb.t[:], in_=pq.t[:], func=AF.Copy), waits=pq.rw() + nb.ww())
                    else:
                        t_ = P.op("vector", lambda e: e.tensor_copy(out=nb.t[:], in_=pq.t[:]), waits=pq.rw() + nb.ww())
                    pq.read(t_)
                    return t_
                last_h = spill(mk, pqkv_d[c], 16 + c, 0)
            for h in range(HEADS):
                pz = job(w_in[:, 12288 + h * 128:12288 + (h + 1) * 128], KC, rhsA)
                sb = stmp[st["tmp"] % 2]
                ssem = msem[2 + st["tmp"] % 2]
                st["tmp"] += 1
                tk = P.op("scalar", lambda e, pz=pz, sb=sb: e.activation(out=sb.t[:], in_=pz.t[:], func=AF.Silu), waits=pz.rw() + sb.ww())
                pz.read(tk)
                sb.wrote(tk)
                tk2 = P.dma("sync", ssem, sz_d[h], sb.t[:], waits=sb.rw())
                sb.read(tk2)
            f = Flow(start=[last_h, sc_tok])
            f.dma("sync", msem[4], hsend_t.ap(), hsb[:])
            raw_inc("gpsimd", lambda e: e.collective_compute("AllGather", ALU.bypass, replica_groups=GROUPS4,
                                                             ins=[hsend_t.ap().opt()], outs=[hall_t.ap().opt()]),
                    [f.tk], ccsem, 1)
            f.tk = (ccsem, ccsem.n)
            f.dma("sync", msem[4], hal[:], hall_t.ap().rearrange("(j p) f -> p j f", p=128))
            f.op("vector", lambda e: e.tensor_scalar(out=halo[:], in0=hal[:, 0, :], scalar1=tab[:, T_SEL, 0:1], scalar2=None, op0=ALU.mult))
            for j in range(1, 4):
                f.op("vector", lambda e, j=j: e.scalar_tensor_tensor(out=halo[:], in0=hal[:, j, :], scalar=tab[:, T_SEL, j:j + 1], in1=halo[:],
                                                                      op0=ALU.mult, op1=ALU.add))
            return f.tk

        def conv_taps(f, eng, y, xr, wcol0, K):
            for j in range(K):
                off = 3 - (K - 1) + j
                if j == 0:
                    f.op(eng, lambda e, off=off, j=j: e.tensor_scalar(out=y, in0=xr[:, off:off + TOK], scalar1=cw[:, wcol0 + j:wcol0 + j + 1],
                                                                      scalar2=None, op0=ALU.mult))
                else:
                    f.op(eng, lambda e, off=off, j=j: e.scalar_tensor_tensor(out=y, in0=xr[:, off:off + TOK], scalar=cw[:, wcol0 + j:wcol0 + j + 1],
                                                                             in1=y, op0=ALU.mult, op1=ALU.add))

        AR2 = hold[0].t

        def conv_branch(halo_tok):
            a2 = SB_BASE + 65536 * 2 + 32768
            sets = [(nc.alloc_sbuf_tensor_at("cxr0", [128, TOK + 3], F32, offset=a2), nc.alloc_sbuf_tensor_at("cy0", [128, TOK], F32, offset=a2 + 8192)),
                    (nc.alloc_sbuf_tensor_at("cxr1", [128, TOK + 3], F32, offset=a2 + 12288), nc.alloc_sbuf_tensor_at("cy1", [128, TOK], F32, offset=a2 + 20480))]
            flows = [Flow(start=[halo_tok]), Flow(start=[halo_tok])]
            ytoks = []
            for i in range(16):
                pB = job(w_in[:, i * 128:(i + 1) * 128], KC, rhsA)
                xr, y = sets[i % 2]
                f = flows[i % 2]
                f.dma("sync", msem[5 + i % 2], xr[:, 3:TOK + 3], pc_d[i])
                f.op("vector", lambda e, xr=xr, i=i: e.tensor_copy(out=xr[:, 0:3], in_=halo[:, i * 3:(i + 1) * 3]))
                conv_taps(f, "vector", y[:], xr, i * 3, 3)
                tk = f.op("vector", lambda e, y=y, pB=pB, i=i: e.tensor_tensor(out=yconv[:, i, :], in0=pB.t[:], in1=y[:], op=ALU.mult), extra=pB.rw())
                pB.read(tk)
                ytoks.append(tk)
            return ytoks[-2:]

        def delta_phase(dmode=None):
            A2 = SB_BASE + 131072
            A2_END = A2 + 57344
            dc = [SB_BASE]

            def dal(name, shape, dt):
                n = int(np.prod(shape[1:])) * (4 if dt == F32 else 2)
                sz_ = (n + 31) // 32 * 32
                if dc[0] < A2 and dc[0] + sz_ > SB_BASE + 65536:
                    dc[0] = A2
                at = dc[0]
                dc[0] += sz_
                assert dc[0] <= SB_BASE + 65536 or (at >= A2 and dc[0] <= A2_END), (name, dc[0])
                return nc.alloc_sbuf_tensor_at(name, list(shape), dt, offset=at)

            NW = 4
            Xraw = dal("Xraw", [128, 3, TOK + 3], F32)
            Y3s = [dal(f"Y3_{i}", [128, 3, TOK], F32) for i in range(2)]
            kqs = [dal(f"kq_{i}", [128, 2, TOK], BF16) for i in range(2)]
            sqb = dal("sqb", [128, TOK], BF16)
            rinv = dal("rinv", [128, TOK], F32)

            def flowbufs(w):
                d = {}
                d["GB"] = dal(f"GB{w}", [128, 256], F32)
                d["Ebc"] = dal(f"Ebc{w}", [128, 128], F32)
                d["Mm"] = dal(f"Mm{w}", [128, 128], F32)
                d["MT"] = dal(f"MT{w}", [128, 128], F32)
                d["Rr"] = dal(f"Rr{w}", [128, 128], F32)
                base = dc[0]
                d["PQ"] = dal(f"PQ{w}", [128, 4, 128], F32)
                d["UgIb"] = nc.alloc_sbuf_tensor_at(f"UgIb{w}", [128, 256], F32, offset=base)
                d["SBm"] = nc.alloc_sbuf_tensor_at(f"SBm{w}", [128, 128], F32, offset=base + 1024)
                d["Dm"] = nc.alloc_sbuf_tensor_at(f"Dm{w}", [128, 128], F32, offset=base + 1536)
                d["Tt"] = dal(f"Tt{w}", [128, 128], BF16)
                d["kbg"] = dal(f"kbg{w}", [128, 128], BF16)
                d["vbb"] = dal(f"vbb{w}", [128, 128], BF16)
                return d
            FB = [flowbufs(0), flowbufs(1)]
            Xs = dal("Xs", [128, 256], F32)
            Xbf = dal("Xbf", [128, 256], BF16)
            vn1 = dal("vn1", [128, 256], BF16)
            HBbs = [dal(f"HBb{i}", [128, NT, 4, 128], BF16) for i in range(2)]
            HBus = [dal(f"HBu{i}", [128, NT, 128], F32) for i in range(2)]
            FB += [flowbufs(2), flowbufs(3)]
            dsm = [P.dsem(f"dsm{i}") for i in range(8)]
            idn = ident[:]
            fbank = [pss[0], pss[1], psum[1].t[:, 0:512], psum[1].t[:, 512:1024]]
            p1bank = psum[2].t[:, 0:512]
            bpsum = psum[0].t

            def colf(s_, th):
                return sc[:, s_, th:th + 1]

            def bulk(f, h, Y3, kq_bf):
                for c, idx in enumerate((h, 16 + h, 32 + h)):
                    f.dma("sync", dsm[0], Xraw[:, c, 3:TOK + 3], pqkv_d[idx])
                    f.op("vector", lambda e, c=c, idx=idx: e.tensor_copy(out=Xraw[:, c, 0:3], in_=halo[:, (16 + idx) * 3:(16 + idx) * 3 + 3]))
                    yield
                    conv_taps(f, "vector", Y3[:, c, :], Xraw[:, c, :], 48 + idx * 4, 4)
                    yield
                    f.op("scalar", lambda e, c=c: e.activation(out=Y3[:, c, :], in_=Y3[:, c, :], func=AF.Silu))
                    yield
                for c in (0, 1):
                    f.op("scalar", lambda e, c=c: e.activation(out=sqb[:], in_=Y3[:, c, :], func=AF.Square))
                    for hf in range(2):
                        f.op("tensor", lambda e, hf=hf: e.matmul(bpsum[:, hf * 512:(hf + 1) * 512], ones_bf[:], sqb[:, hf * 512:(hf + 1) * 512],
                                                                 start=True, stop=True))
                    yield
                    if c == 0:
                        f.op("vector", lambda e: e.tensor_scalar(out=rinv[:], in0=bpsum[:], scalar1=EPS, scalar2=128.0, op0=ALU.add, op1=ALU.mult))
                    else:
                        f.op("vector", lambda e: e.tensor_scalar(out=rinv[:], in0=bpsum[:], scalar1=EPS, scalar2=None, op0=ALU.add))
                    f.op("scalar", lambda e: e.sqrt(out=rinv[:], in_=rinv[:]))
                    yield
                    f.op("vector", lambda e: e.reciprocal(out=rinv[:], in_=rinv[:]))
                    f.op("vector", lambda e, c=c: e.tensor_tensor(out=Y3[:, c, :], in0=Y3[:, c, :], in1=rinv[:], op=ALU.mult))
                    f.op("scalar", lambda e, c=c: e.activation(out=kq_bf[:, 1 - c, :], in_=Y3[:, c, :], func=AF.Copy))
                    yield

            def tile_flow(f, h, t, w, Y3, kq_bf, HBb, HBu):
                th = t * 16 + h
                tl = slice(t * 128, (t + 1) * 128)
                B = FB[w]
                GB, Ebc, Mm, MT, Rr, PQ, UgIb, SBm, Dm, Tt, kbg, vbb = (B[k] for k in ("GB", "Ebc", "Mm", "MT", "Rr", "PQ", "UgIb", "SBm", "Dm", "Tt", "kbg", "vbb"))
                pab = fbank[w][:, 0:256]
                pa = fbank[w][:, 0:128]
                pb = fbank[w][:, 128:256]
                f.op("vector", lambda e: e.tensor_scalar(out=UgIb[:, 0:128], in0=tab[:, T_U, :], scalar1=colf(S_G, th), scalar2=None, op0=ALU.mult))
                f.op("vector", lambda e: e.tensor_scalar(out=UgIb[:, 128:256], in0=idn, scalar1=colf(S_BETA, th), scalar2=None, op0=ALU.mult))
                f.op("tensor", lambda e: e.matmul(pab, tab[:, T_ONE, :], UgIb[:], start=True, stop=True))
                f.op("scalar", lambda e: e.activation(out=GB[:], in_=pab, func=AF.Copy))
                yield
                f.op("vector", lambda e: e.scalar_tensor_tensor(out=Dm[:], in0=GB[:, 0:128], scalar=colf(S_GC, th), in1=tab[:, T_NEG, :],
                                                                op0=ALU.subtract, op1=ALU.add))
                f.op("scalar", lambda e: e.activation(out=Dm[:], in_=Dm[:], func=AF.Exp))
                f.op("scalar", lambda e: e.activation(out=Ebc[:], in_=GB[:, 0:128], func=AF.Exp))
                f.op("tensor", lambda e: e.matmul(pab.rearrange("p (a b) -> p a b", a=2), kq_bf[:, 0, tl], kq_bf[:, :, tl], start=True, stop=True))
                yield
                f.op("vector", lambda e: e.tensor_tensor(out=HBb[:, t, 0, :], in0=pb, in1=Dm[:], op=ALU.mult))
                f.op("vector", lambda e: e.tensor_tensor(out=SBm[:], in0=GB[:, 128:256], in1=tab[:, T_STR, :], op=ALU.mult))
                f.op("vector", lambda e: e.tensor_tensor(out=Mm[:], in0=pa, in1=Dm[:], op=ALU.mult))
                f.op("vector", lambda e: e.tensor_tensor(out=Mm[:], in0=Mm[:], in1=SBm[:], op=ALU.mult))
                yield
                f.op("tensor", lambda e: e.transpose(pa, Mm[:], idn))
                f.op("scalar", lambda e: e.activation(out=MT[:], in_=pa, func=AF.Copy))
                f.op("vector", lambda e: e.scalar_tensor_tensor(out=Rr[:], in0=Mm[:], scalar=-1.0, in1=idn, op0=ALU.mult, op1=ALU.add))
                yield
                Pc, Qc = Mm[:], MT[:]
                for k in range(6):
                    Pn = PQ[:, (k % 2) * 2, :]
                    Qn = PQ[:, (k % 2) * 2 + 1, :]
                    f.op("tensor", lambda e, Pc=Pc, Qc=Qc: e.matmul(pa, Qc, Pc, start=True, stop=True))
                    f.op("tensor", lambda e, Pc=Pc, Qc=Qc: e.matmul(pb, Pc, Qc, start=True, stop=True))
                    f.op("scalar", lambda e, Pn=Pn: e.activation(out=Pn, in_=pa, func=AF.Copy))
                    f.op("vector", lambda e, Qn=Qn: e.tensor_copy(out=Qn, in_=pb))
                    yield
                    f.op("tensor", lambda e, Qn=Qn: e.matmul(pa, Qn, Rr[:], start=True, stop=True))
                    f.op("vector", lambda e: e.tensor_tensor(out=Rr[:], in0=pa, in1=Rr[:], op=ALU.add))
                    Pc, Qc = Pn, Qn
                    yield
                f.op("vector", lambda e: e.tensor_copy(out=Tt[:], in_=Rr[:]))
                f.op("tensor", lambda e: e.transpose(pa, Y3[:, 1, tl], idn))
                f.op("tensor", lambda e: e.transpose(pb, Y3[:, 2, tl], idn))
                f.op("scalar", lambda e: e.activation(out=kbg[:], in_=pa, func=AF.Copy, scale=colf(S_BG, th)))
                yield
                f.op("vector", lambda e: e.tensor_scalar(out=HBb[:, t, 1, :], in0=pa, scalar1=colf(S_EKD, th), scalar2=None, op0=ALU.mult))
                f.op("scalar", lambda e: e.activation(out=vbb[:], in_=pb, func=AF.Copy, scale=colf(S_BETA, th)))
                f.op("tensor", lambda e: e.matmul(pa, Tt[:], vbb[:], start=True, stop=True))
                f.op("tensor", lambda e: e.matmul(pb, kbg[:], Tt[:], start=True, stop=True))
                yield
                f.op("scalar", lambda e: e.activation(out=HBu[:, t, :], in_=pa, func=AF.Copy))
                f.op("vector", lambda e: e.tensor_copy(out=HBb[:, t, 2, :], in_=pb))
                f.op("vector", lambda e: e.tensor_tensor(out=HBb[:, t, 3, :], in0=kq_bf[:, 1, tl], in1=Ebc[:], op=ALU.mult))
                yield

            def tile_seq(f, h, w, Y3, kq_bf, HBb, HBu):
                for t in range(w, NT, NW):
                    yield from tile_flow(f, h, t, w, Y3, kq_bf, HBb, HBu)

            def pass1(f, h, HBb, HBu):
                pab0 = p1bank[:, 0:256]
                f.op("vector", lambda e: e.memset(Xs[:, 0:128], 0.0))
                f.op("vector", lambda e: e.tensor_copy(out=Xs[:, 128:256], in_=idn))
                f.op("vector", lambda e: e.tensor_copy(out=Xbf[:], in_=Xs[:]))
                yield
                for t in range(NT):
                    th = t * 16 + h
                    f.op("tensor", lambda e, t=t: e.matmul(pab0, HBb[:, t, 2, :], Xbf[:], start=True, stop=True))
                    f.op("vector", lambda e, t=t: e.tensor_tensor(out=vn1[:, 0:128], in0=HBu[:, t, :], in1=pab0[:, 0:128], op=ALU.subtract))
                    f.op("vector", lambda e: e.tensor_scalar(out=vn1[:, 128:256], in0=pab0[:, 128:256], scalar1=-1.0, scalar2=None, op0=ALU.mult))
                    yield
                    f.op("tensor", lambda e, t=t: e.matmul(pab0, HBb[:, t, 1, :], vn1[:], start=True, stop=True))
                    f.op("vector", lambda e, th=th: e.scalar_tensor_tensor(out=Xs[:], in0=Xs[:], scalar=colf(S_EGL, th), in1=pab0, op0=ALU.mult, op1=ALU.add))
                    f.op("scalar", lambda e: e.activation(out=Xbf[:], in_=Xs[:], func=AF.Copy))
                    yield
                f.dma("sync", dsm[1], gsend_t[h // 4].ap()[(h % 4) * 128:(h % 4 + 1) * 128, :], Xs[:])
                f.call(lambda h=h: gs_toks.__setitem__(h, f.tk))
                f.dma("sync", dsm[1], spb_d[h], HBb[:].rearrange("p a b c -> p (a b c)"))
                f.dma("sync", dsm[1], spu_d[h], HBu[:].rearrange("p a b -> p (a b)"))
                yield

            NHD = 1 if dmode in ("delta1", "delta1c", "delta1x") else HEADS
            gs_toks = [None] * HEADS
            bulk_tok = {}
            tiles_tok = {}
            p1_tok = {}
            cc_toks = []
            for step in range(NHD + 2):
                hb, ht, hp = step, step - 1, step - 2
                gens = []
                fb = fl = fp = None
                if hb < NHD:
                    fb = Flow(start=[bulk_tok.get(hb - 1)] + tiles_tok.get(hb - 2, []))
                    gens.append((fb, bulk(fb, hb, Y3s[hb % 2], kqs[hb % 2])))
                if 0 <= ht < NHD:
                    fl = [Flow(start=[bulk_tok[ht], p1_tok.get(ht - 2)] + tiles_tok.get(ht - 1, [])) for _ in range(NW)]
                    gens += [(fl[w], tile_seq(fl[w], ht, w, Y3s[ht % 2], kqs[ht % 2], HBbs[ht % 2], HBus[ht % 2])) for w in range(NW)]
                if 0 <= hp < NHD:
                    fp = Flow(start=tiles_tok[hp] + [p1_tok.get(hp - 1)])
                    gens.append((fp, pass1(fp, hp, HBbs[hp % 2], HBus[hp % 2])))
                run_flows(gens)
                if fb is not None:
                    bulk_tok[hb] = fb.tk
                if fl is not None:
                    tiles_tok[ht] = [f_.tk for f_ in fl]
                if fp is not None:
                    p1_tok[hp] = fp.tk
                    if hp % 4 == 3 and dmode not in ("delta1", "delta1x"):
                        raw_inc("gpsimd", lambda e, q=hp // 4: e.collective_compute("AllGather", ALU.bypass, replica_groups=GROUPS4,
                                                                                    ins=[gsend_t[q].ap().opt()], outs=[gall_t[q].ap().opt()]),
                                [gs_toks[hh] for hh in range(hp - 3, hp + 1)], ccsem, 1)
                        cc_toks.append((ccsem, ccsem.n))
            if dmode == "delta1":
                return
            while len(cc_toks) < 4:
                if dmode == "delta1c" and not cc_toks:
                    raw_inc("gpsimd", lambda e: e.collective_compute("AllGather", ALU.bypass, replica_groups=GROUPS4,
                                                                     ins=[gsend_t[0].ap().opt()], outs=[gall_t[0].ap().opt()]),
                            [gs_toks[0]], ccsem, 1)
                    cc_toks.append((ccsem, ccsem.n))
                else:
                    cc_toks.append(p1_tok[NHD - 1])
            P.barrier()
            dc[0] = SB_BASE

            def p2bufs(i):
                d = {}
                d["HBb"] = dal(f"q_HBb{i}", [128, NT, 4, 128], BF16)
                d["HBu"] = dal(f"q_HBu{i}", [128, NT, 128], F32)
                d["szb"] = dal(f"q_szb{i}", [128, TOK], BF16)
                d["S"] = dal(f"q_S{i}", [128, 128], F32)
                d["Sbf"] = dal(f"q_Sbf{i}", [128, 128], BF16)
                d["vn"] = dal(f"q_vn{i}", [128, 128], BF16)
                d["onb"] = dal(f"q_onb{i}", [128, 128], F32)
                d["ssq"] = dal(f"q_ssq{i}", [128, 8], F32)
                d["AB"] = dal(f"q_AB{i}", [128, 256], F32)
                d["ATs"] = dal(f"q_ATs{i}", [128, 128], F32)
                return d
            PB = [p2bufs(i) for i in range(4)]

            def pass2(f, h, i):
                B = PB[i]
                HBb, HBu, szb, Ss, Sbf, vn, onb, ssq, AB, ATs = (B[k] for k in ("HBb", "HBu", "szb", "S", "Sbf", "vn", "onb", "ssq", "AB", "ATs"))
                pa = fbank[i][:, 0:128]
                pb = fbank[i][:, 128:256]
                f.dma("sync", dsm[4 + i], HBb[:].rearrange("p a b c -> p (a b c)"), spb_d[h])
                f.dma("sync", dsm[4 + i], HBu[:].rearrange("p a b -> p (a b)"), spu_d[h])
                f.dma("sync", dsm[4 + i], szb[:], sz_d[h])
                f.op("vector", lambda e: e.memset(Ss[:], 0.0))
                yield
                f.call(lambda h=h: f.start.append(cc_toks[h // 4]))
                for j in range(3):
                    f.dma("sync", dsm[4 + i], AB[:], gall_t[h // 4].ap()[(j * 4 + h % 4) * 128:(j * 4 + h % 4 + 1) * 128, :])
                    f.op("tensor", lambda e: e.transpose(pa, AB[:, 128:256], idn))
                    f.op("scalar", lambda e: e.activation(out=ATs[:], in_=pa, func=AF.Copy))
                    yield
                    f.op("tensor", lambda e: e.matmul(pb, ATs[:], Ss[:], start=True, stop=True))
                    f.op("vector", lambda e: e.tensor_tensor(out=ATs[:], in0=pb, in1=AB[:, 0:128], op=ALU.add))
                    f.op("vector", lambda e: e.tensor_tensor(out=ATs[:], in0=ATs[:], in1=Ss[:], op=ALU.subtract))
                    f.op("vector", lambda e, j=j: e.scalar_tensor_tensor(out=Ss[:], in0=ATs[:], scalar=tab[:, T_SEL, 4 + j:5 + j], in1=Ss[:],
                                                                          op0=ALU.mult, op1=ALU.add))
                    yield
                f.op("vector", lambda e: e.tensor_copy(out=Sbf[:], in_=Ss[:]))
                for t in range(NT):
                    th = t * 16 + h
                    tl = slice(t * 128, (t + 1) * 128)
                    f.op("tensor", lambda e, t=t: e.matmul(pa, HBb[:, t, 2, :], Sbf[:], start=True, stop=True))
                    f.op("vector", lambda e, t=t: e.tensor_tensor(out=vn[:], in0=HBu[:, t, :], in1=pa, op=ALU.subtract))
                    yield
                    f.op("tensor", lambda e, t=t: e.matmul(pb, HBb[:, t, 3, :], Sbf[:], start=True, stop=False))
                    f.op("tensor", lambda e, t=t: e.matmul(pb, HBb[:, t, 0, :], vn[:], start=False, stop=True))
                    f.op("tensor", lambda e, t=t: e.matmul(pa, HBb[:, t, 1, :], vn[:], start=True, stop=True))
                    yield
                    f.op("vector", lambda e, th=th: e.scalar_tensor_tensor(out=Ss[:], in0=Ss[:], scalar=colf(S_EGL, th), in1=pa, op0=ALU.mult, op1=ALU.add))
                    f.op("scalar", lambda e: e.activation(out=Sbf[:], in_=Ss[:], func=AF.Copy))
                    f.op("vector", lambda e: e.memset(ssq[:, 0:1], 0.0))
                    yield
                    f.op("scalar", lambda e: e.activation(out=onb[:], in_=pb, func=AF.Square, accum_out=ssq[:, 0:1]))
                    f.op("vector", lambda e: e.tensor_scalar(out=ssq[:, 0:1], in0=ssq[:, 0:1], scalar1=1.0 / 128, scalar2=EPS, op0=ALU.mult, op1=ALU.add))
                    f.op("scalar", lambda e: e.sqrt(out=ssq[:, 0:1], in_=ssq[:, 0:1]))
                    yield
                    f.op("vector", lambda e: e.reciprocal(out=ssq[:, 0:1], in_=ssq[:, 0:1]))
                    f.op("vector", lambda e: e.scalar_tensor_tensor(out=onb[:], in0=pb, scalar=ssq[:, 0:1], in1=tab[:, T_GOUT, :], op0=ALU.mult, op1=ALU.mult))
                    f.op("tensor", lambda e: e.transpose(pa, onb[:], idn))
                    f.op("vector", lambda e, tl=tl, h=h: e.tensor_tensor(out=ydn[:, h, tl], in0=pa, in1=szb[:, tl], op=ALU.mult))
                    yield

            prevs = [None] * 4
            for h0 in range(0, NHD, 4):
                fls = []
                gens = []
                for i in range(min(4, NHD - h0)):
                    f_ = Flow(start=[prevs[i]])
                    fls.append(f_)
                    gens.append((f_, pass2(f_, h0 + i, i)))
                run_flows(gens)
                for i, f_ in enumerate(fls):
                    prevs[i] = f_.tk

        def merge_and_out():
            norm_stage(1)
            for i in range(KC):
                pgc = job(w_in[:, 14368 + i * 128:14368 + (i + 1) * 128], KC, rhsA, pe_waits=A.rw())
                hb1 = hold[st["ld"] % 2]; st["ld"] += 1
                tk = P.op("scalar", lambda e, pgc=pgc, hb1=hb1: e.activation(out=hb1.t[:], in_=pgc.t[:], func=AF.Sigmoid), waits=pgc.rw() + hb1.ww())
                pgc.read(tk); hb1.wrote(tk)
                pcp = job(w_cb[:, i * 128:(i + 1) * 128], 16, lambda k, hf: yconv[:, k, hf * 512:(hf + 1) * 512])
                nb = hnew[st["st"] % 2]; st["st"] += 1
                tk = P.op("vector", lambda e, pcp=pcp, hb1=hb1, nb=nb: e.tensor_tensor(out=nb.t[:], in0=pcp.t[:], in1=hb1.t[:], op=ALU.mult),
                          waits=pcp.rw() + hb1.rw() + nb.ww())
                pcp.read(tk); hb1.read(tk); nb.wrote(tk)
                pgd = job(w_in[:, 18464 + i * 128:18464 + (i + 1) * 128], KC, rhsA)
                hb2 = hold[st["ld"] % 2]; st["ld"] += 1
                tk = P.op("scalar", lambda e, pgd=pgd, hb2=hb2: e.activation(out=hb2.t[:], in_=pgd.t[:], func=AF.Sigmoid), waits=pgd.rw() + hb2.ww())
                pgd.read(tk); hb2.wrote(tk)
                pdp = job(w_db[:, i * 128:(i + 1) * 128], 16, lambda k, hf: ydn[:, k, hf * 512:(hf + 1) * 512])
                tk = P.op("vector", lambda e, pdp=pdp, hb2=hb2: e.tensor_tensor(out=hb2.t[:], in0=pdp.t[:], in1=hb2.t[:], op=ALU.mult),
                          waits=pdp.rw() + hb2.rw())
                pdp.read(tk); hb2.wrote(tk)
                sb = stmp[st["tmp"] % 2]
                ssem = msem[2 + st["tmp"] % 2]
                st["tmp"] += 1
                tk = P.op("vector", lambda e, sb=sb, nb=nb, hb2=hb2: e.tensor_tensor(out=sb.t[:], in0=nb.t[:], in1=hb2.t[:], op=ALU.add),
                          waits=nb.rw() + hb2.rw() + sb.ww())
                nb.read(tk); hb2.read(tk); sb.wrote(tk)
                tk2 = P.dma("sync", ssem, mrg_d[i], sb.t[:], waits=sb.rw())
                sb.read(tk2)
            P.barrier()
            toks = []
            for q4 in range(4):
                toks.append(P.dma("sync", msem[4 + q4 % 2], actA[:, q4 * 8:(q4 + 1) * 8, :], mrg_d[q4 * 8:(q4 + 1) * 8].rearrange("k p t -> p k t")))
            A.wrote(toks[0])
            for tk in toks[1:]:
                A.wrote_more(tk)
            for i in range(KC):
                po = job(w_out[:, i * 128:(i + 1) * 128], KC, rhsA, pe_waits=A.rw())
                resid_update(i, po, 1.0)


    if MIX and DELTA:
        P.dma("sync", msem[0], tab[:], tab_in)
        P.dma("sync", msem[0], cw[:], cw_in)
        P.dma("sync", msem[1], sc[:].rearrange("p a b -> p (a b)"), sc_in)
        P.dma("sync", msem[4], halo[:], halo_in)
        P.barrier()
        delta_phase(debug)
        P.barrier()
        dbg = nc.dram_tensor("dbg", [128, 16 * TOK], BF16, kind="ExternalOutput").ap()
        P.dma("sync", msem[0], dbg, ydn[:].rearrange("p a b -> p (a b)"))
        P.barrier()
        with nc.Block() as block:
            P.replay(block)
        return nc, stack
    stage_T0()
    P.barrier()
    if debug not in ("mixA", "mixB"):
        ffn(0, w1g, w1u, w1d)
        P.barrier()
    if debug != "hT":
        htok = mixer_front()
        P.barrier()
        conv_branch(htok)
        P.barrier()
        if debug != "mixA":
            delta_phase()
            P.barrier()
        if debug in ("mixA", "mixB"):
            dbg = nc.dram_tensor("dbg", [128, 16 * TOK], BF16, kind="ExternalOutput").ap()
            dbg2 = nc.dram_tensor("dbg2", [128, 12 * 128], F32, kind="ExternalOutput").ap()
            dbg3 = nc.dram_tensor("dbg3", [128, 192], F32, kind="ExternalOutput").ap()
            src = yconv if debug == "mixA" else ydn
            P.dma("sync", msem[0], dbg, src[:].rearrange("p a b -> p (a b)"))
            P.dma("sync", msem[1], dbg2, sc[:].rearrange("p a b -> p (a b)"))
            P.dma("sync", msem[4], dbg3, halo[:])
        else:
            merge_and_out()
            P.barrier()
            if debug != "h2":
                ffn(2, w2g, w2u, w2d)
                P.barrier()
                final_stage()
    P.barrier()

    with nc.Block() as block:
        P.replay(block)
    return nc, stack


def make_in_maps(inputs, debug=None):
    f = lambda k: np.asarray(inputs[k], dtype=np.float32)
    x = f("x").reshape(NCORES, TOK, D)
    gl = lambda v: np.asarray(v, np.float32).reshape(KC, 128).T
    gam = np.ascontiguousarray(np.concatenate([gl(f("ffn1_norm")[0]), gl(f("mix_norm")[0]),
                                               gl(f("ffn2_norm")[0]), gl(f("final_norm"))], axis=1))
    common = {
        "gam": gam,
        "w1g": f("ffn1_w_gate")[0], "w1u": f("ffn1_w_up")[0], "w1d": f("ffn1_w_down")[0],
        "ident": np.eye(128, dtype=np.float32),
    }
    if debug != "hT":
        common.update({
            "w2g": f("ffn2_w_gate")[0], "w2u": f("ffn2_w_up")[0], "w2d": f("ffn2_w_down")[0],
            "w_in": f("w_in")[0], "w_cb": f("w_conv_branch")[0], "w_db": f("w_dn_branch")[0], "w_out": f("w_out")[0],
        })
        cw = np.zeros((128, 240), np.float32)
        cw[:, 0:48] = f("conv_mixer_w")[0].reshape(16, 128, 3).transpose(1, 0, 2).reshape(128, 48)
        cw[:, 48:240] = f("dn_conv_w")[0].reshape(48, 128, 4).transpose(1, 0, 2).reshape(128, 192)
        common["cw"] = cw
        s_ = np.arange(128)[:, None]
        c_ = np.arange(128)[None, :]
        tab = np.zeros((128, 9, 128), np.float32)
        tab[:, 0] = (s_ <= c_)
        tab[:, 1] = np.where(c_ >= s_, 0.0, -30000.0)
        tab[:, 2] = (c_ > s_)
        tab[:, 3] = 1.0
        tab[:, 4] = np.tile(f("dn_dt_bias")[0], NT)[None, :]
        tab[:, 5] = np.tile(f("dn_a_log")[0], NT)[None, :]
        tab[:, 6] = f("dn_out_norm")[0][None, :]
    maps = []
    for c in range(NCORES):
        m = dict(common)
        m["x"] = np.ascontiguousarray(x[c])
        if debug != "hT":
            t = tab.copy()
            r = c % 4
            if r > 0:
                t[:, 7, r - 1] = 1.0
            for j in range(3):
                if j < r:
                    t[:, 7, 4 + j] = 1.0
            m["tab"] = t
        maps.append(m)
    return maps


def kernel(**inputs):
    nc, stack = build()
    in_maps = make_in_maps(inputs)
    res = run_bass_kernel_spmd(nc, in_maps, core_ids=list(range(NCORES)))
    outs = [np.asarray(r["out"]) for r in res.results]
    return np.stack(outs, 0).reshape(2, 4096, D).astype(np.float32)
```

```python
import contextlib
import numpy as np
import concourse.bass as bass
import concourse.mybir as mybir
from concourse.bass_utils import run_bass_kernel_spmd

F32 = mybir.dt.float32
BF16 = mybir.dt.bfloat16
AF = mybir.ActivationFunctionType
ALU = mybir.AluOpType

NCORES = 8
TOK = 1024
D = 4096
KC = 32
DFF = 11008
NIN = 22560
EPS = 1e-6
HEADS = 16
NT = 8
ENGS = ("sync", "scalar", "vector", "gpsimd", "tensor")


class Sem:
    def __init__(self, h, name):
        self.h = h
        self.n = 0
        self.name = name


class Prog:
    def __init__(self, nc, stack):
        self.nc = nc
        self.stack = stack
        self.ops = {e: [] for e in ENGS}
        self.nsem = 0
        self.done = {e: self.sem("dn_" + e) for e in ENGS}
        self.seen = {e: {} for e in ENGS}
        self.dma_sems = []

    def sem(self, name):
        self.nsem += 1
        return Sem(self.stack.enter_context(self.nc.semaphore(name)), name)

    def dsem(self, name):
        s = self.sem(name)
        self.dma_sems.append(s)
        return s

    def _waits(self, eng, waits):
        best = {}
        for tok in waits:
            if tok is None:
                continue
            s, v = tok
            if v > best.get(s, (None, 0))[1]:
                best[s] = (s, v)
        out = []
        seen = self.seen[eng]
        for s, v in best.values():
            if seen.get(s, 0) >= v:
                continue
            seen[s] = v
            out.append((s.h, v))
        return out

    def op(self, eng, fn, waits=(), count=True):
        w = self._waits(eng, waits)
        inc = None
        tok = None
        if count:
            s = self.done[eng]
            if s.n >= 30000:
                s = self.done[eng] = self.sem("dn_" + eng + str(self.nsem))
            s.n += 1
            inc = (s.h, 1)
            tok = (s, s.n)
        self.ops[eng].append((w, fn, inc))
        return tok

    def dma(self, eng, sem, out, in_, waits=()):
        w = self._waits(eng, waits)
        sem.n += 16
        self.ops[eng].append((w, lambda e: e.dma_start(out=out, in_=in_), (sem.h, 16)))
        return (sem, sem.n)

    def wait_only(self, eng, waits):
        w = self._waits(eng, waits)
        if w:
            self.ops[eng].append((w, None, None))

    def barrier(self):
        toks = [(s, s.n) for s in self.done.values()] + [(s, s.n) for s in self.dma_sems]
        for en in ENGS:
            self.wait_only(en, toks)

    def replay(self, block):
        def mk(lst):
            def run(e):
                for w, fn, inc in lst:
                    for h, v in w:
                        e.wait_ge(h, v)
                    if fn is None:
                        continue
                    ins = fn(e)
                    if inc is not None:
                        ins.then_inc(inc[0], inc[1])
            return run
        for en in ENGS:
            if self.ops[en]:
                getattr(block, en)(mk(self.ops[en]))


class Buf:
    def __init__(self, t):
        self.t = t
        self.ready = []
        self.readers = []

    def ww(self):
        return self.ready + self.readers

    def wrote(self, tok):
        self.ready = [tok]
        self.readers = []

    def wrote_more(self, tok):
        self.ready.append(tok)

    def rw(self):
        return list(self.ready)

    def read(self, tok):
        self.readers.append(tok)


def build(debug=None):
    nc = bass.Bass("TRN2", target_bir_lowering=False)
    stack = contextlib.ExitStack()
    P = Prog(nc, stack)

    def din(name, shape):
        return nc.dram_tensor(name, list(shape), F32, kind="ExternalInput").ap()

    x = din("x", [TOK, D])
    gam_in = din("gam", [128, 4 * KC])
    DBGM = debug in ("mixA", "mixB") or (isinstance(debug, str) and debug.startswith("delta"))
    if not DBGM:
        w1g = din("w1g", [D, DFF]); w1u = din("w1u", [D, DFF]); w1d = din("w1d", [DFF, D])
    if debug != "hT" and not (isinstance(debug, str) and debug.startswith("delta")):
        w_in = din("w_in", [D, NIN])
        if not DBGM:
            w2g = din("w2g", [D, DFF]); w2u = din("w2u", [D, DFF]); w2d = din("w2d", [DFF, D])
            w_cb = din("w_cb", [2048, D]); w_db = din("w_db", [2048, D]); w_out = din("w_out", [D, D])
    ident_in = din("ident", [128, 128])
    MIX = debug != "hT"
    if MIX:
        cw_in = din("cw", [128, 240])
        tab_in = din("tab", [128, 9, 128])
        def dscr(name, shape, dt=F32):
            return nc.dram_tensor(name, list(shape), dt).ap()
        DELTA = isinstance(debug, str) and debug.startswith("delta")
        pc_d = dscr("pc_d", [16, 128, TOK])
        if DELTA:
            pqkv_d = din("pqkv_in", [48, 128, TOK])
            sz_d = nc.dram_tensor("sz_in", [16, 128, TOK], BF16, kind="ExternalInput").ap()
            sc_in = din("sc_in", [128, 12 * 128]); halo_in = din("halo_in", [128, 192])
        else:
            pqkv_d = dscr("pqkv_d", [48, 128, TOK])
            sz_d = dscr("sz_d", [16, 128, TOK], BF16)
        spb_d = dscr("spb_d", [16, 128, 8 * 4 * 128], BF16); spu_d = dscr("spu_d", [16, 128, 8 * 128])
        gsend_t = [nc.dram_tensor(f"gsend{q}", [4 * 128, 256], F32) for q in range(4)]
        gall_t = [nc.dram_tensor(f"gall{q}", [4 * 4 * 128, 256], F32) for q in range(4)]
        hsend_t = nc.dram_tensor("hsend", [128, 192], F32); hall_t = nc.dram_tensor("hall", [4 * 128, 192], F32)
        mrg_d = dscr("mrg_d", [KC, 128, TOK], BF16)
    out = nc.dram_tensor("out", [TOK, D], F32, kind="ExternalOutput").ap()
    hT = nc.dram_tensor("hT", [KC, 128, TOK], F32,
                        kind=("ExternalOutput" if debug in ("hT", "h2") else "Internal")).ap()

    SB_BASE = 16512
    SB_END = 229375
    cur = [SB_BASE]

    def alloc(name, shape, dt, at=None):
        n = int(np.prod(shape[1:])) * (4 if dt == F32 else 2)
        if at is None:
            at = cur[0]
            cur[0] += (n + 31) // 32 * 32
        return nc.alloc_sbuf_tensor_at(name, list(shape), dt, offset=at)

    actA = alloc("actA", [128, KC, TOK], BF16)
    regB = cur[0]
    actT = alloc("actT", [128, 29, TOK], BF16)
    cur[0] = regB + 64 * 1024
    NWB = 4
    wbuf = [Buf(alloc(f"wb{i}", [128, KC, 128], BF16)) for i in range(NWB)]
    hold = [Buf(alloc(f"hold{i}", [128, TOK], F32)) for i in range(2)]
    hnew = [Buf(alloc(f"hnew{i}", [128, TOK], F32)) for i in range(2)]
    stmp = [Buf(alloc(f"stmp{i}", [128, TOK], BF16)) for i in range(2)]
    rstd = Buf(alloc("rstd", [128, TOK], F32))
    ident = alloc("ident", [128, 128], F32)
    ones_bf = alloc("ones_bf", [128, 128], BF16)
    gam = alloc("gam", [128, 4, KC], F32)
    xin = Buf(alloc("xin", [128, D], F32, at=regB))
    xst = [Buf(alloc(f"xst{i}", [128, 4, 128], F32, at=regB + 16384 + 2048 * i)) for i in range(2)]
    yconv = alloc("yconv", [128, 16, TOK], BF16, at=regB)
    ydn = alloc("ydn", [128, 16, TOK], BF16, at=regB + 32768)
    tab = alloc("tab", [128, 9, 128], F32)
    cw = alloc("cw", [128, 240], F32)
    sc = alloc("sc", [128, 12, 128], F32)
    hsb = alloc("hsb", [128, 192], F32)
    hal = alloc("hal", [128, 4, 192], F32)
    halo = alloc("halo", [128, 192], F32)
    wab = alloc("wab", [128, KC, 32], BF16)
    assert cur[0] <= SB_END, cur[0]
    dcur = [SB_BASE]
    def dalloc(name, shape, dt):
        n = int(np.prod(shape[1:])) * (4 if dt == F32 else 2)
        sz_ = (n + 31) // 32 * 32
        if dcur[0] + sz_ > SB_BASE + 65536 and dcur[0] <= SB_BASE + 65536:
            dcur[0] = SB_BASE + 163840
        at = dcur[0]
        dcur[0] += sz_
        assert dcur[0] <= SB_BASE + 65536 or (at >= SB_BASE + 163840 and dcur[0] <= SB_BASE + 163840 + 24576), (name, dcur[0])
        return nc.alloc_sbuf_tensor_at(name, list(shape), dt, offset=at)

    NPS = 3
    psum = [Buf(stack.enter_context(nc.psum_tensor(f"ps{i}", [128, TOK], F32))) for i in range(NPS)]
    pss = [stack.enter_context(nc.psum_tensor(f"pss{i}", [128, 512], F32)) for i in range(2)]
    A = Buf(actA)
    ACT = Buf(actT)
    HT = [Buf(None) for _ in range(KC)]

    c_ld = P.dsem("c_ld")
    ctok = [P.dma("sync", c_ld, ident[:], ident_in),
            P.dma("sync", c_ld, gam[:].rearrange("p a b -> p (a b)"), gam_in),
            P.op("vector", lambda e: e.memset(ones_bf[:], 1.0))]

    st = {"wb": 0, "ps": 0, "ld": 0, "ld4": 0, "st": 0, "tmp": 0}
    wsem = [P.dsem(f"wld{i}") for i in range(NWB)]
    hld = [P.dsem(f"hld{i}") for i in range(4)]
    hst = [P.dsem(f"hst{i}") for i in range(2)]

    def next_ps():
        p = psum[st["ps"] % NPS]
        st["ps"] += 1
        return p

    def job(W_ap, nk, rhs_fn, pe_waits=()):
        b = st["wb"] % NWB
        st["wb"] += 1
        wb = wbuf[b]
        pr = next_ps()
        src = W_ap.rearrange("(k p) f -> p k f", p=128)
        tok = P.dma("gpsimd", wsem[b], wb.t[:, 0:nk, :], src, waits=wb.ww())
        wb.wrote(tok)
        waits = wb.rw() + pr.ww() + list(pe_waits) + ctok
        last = None
        for k in range(nk):
            for hf in range(2):
                fin = (k == nk - 1 and hf == 1)
                fn = (lambda e, k=k, hf=hf: e.matmul(pr.t[:, hf * 512:(hf + 1) * 512], wb.t[:, k, :], rhs_fn(k, hf),
                                                     start=(k == 0), stop=(k == nk - 1)))
                last = P.op("tensor", fn, waits=waits if (k == 0 and hf == 0) else (), count=fin)
        wb.read(last)
        pr.wrote(last)
        return pr

    xld = P.dsem("xld")
    xsts = [P.dsem(f"xsts{i}") for i in range(2)]

    def stage_T0():
        cnt = 0
        for t in range(NT):
            tok = P.dma("sync", xld, xin.t[:], x[t * 128:(t + 1) * 128, :], waits=xin.ww())
            xin.wrote(tok)
            for kg in range(KC // 4):
                pr = next_ps()
                last = None
                for kk in range(4):
                    k = kg * 4 + kk
                    last = P.op("tensor", lambda e, k=k, kk=kk, pr=pr: e.transpose(pr.t[:, kk * 128:(kk + 1) * 128],
                                                                                   xin.t[:, k * 128:(k + 1) * 128], ident[:]),
                                waits=(xin.rw() + pr.ww() + ctok) if kk == 0 else (), count=(kk == 3))
                pr.wrote(last)
                xin.read(last)
                si = cnt % 2
                cnt += 1
                sb = xst[si]
                tk = P.op("vector", lambda e, pr=pr, sb=sb: e.tensor_copy(out=sb.t[:].rearrange("p a b -> p (a b)"), in_=pr.t[:, 0:512]),
                          waits=pr.rw() + sb.ww())
                pr.read(tk)
                sb.wrote(tk)
                dst = hT[kg * 4:(kg + 1) * 4, :, t * 128:(t + 1) * 128].rearrange("k p j -> p k j")
                tk2 = P.dma("sync", xsts[si], dst, sb.t[:], waits=sb.rw())
                sb.read(tk2)
                for k in range(kg * 4, kg * 4 + 4):
                    HT[k].wrote_more(tk2)

    def load_h(k, deep=False):
        if deep:
            b = st["ld4"] % 4
            st["ld4"] += 1
            hb = (hold + hnew)[b]
        else:
            b = st["ld"] % 2
            st["ld"] += 1
            hb = hold[b]
        tok = P.dma("sync", hld[b], hb.t[:], hT[k], waits=hb.ww() + HT[k].rw())
        hb.wrote(tok)
        HT[k].read(tok)
        return hb

    def norm_stats():
        pr = next_ps()
        last = None
        for k in range(KC):
            hb = load_h(k, deep=True)
            sb = stmp[st["tmp"] % 2]
            st["tmp"] += 1
            tk = P.op("scalar", lambda e, hb=hb, sb=sb: e.activation(out=sb.t[:], in_=hb.t[:], func=AF.Square),
                      waits=hb.rw() + sb.ww())
            hb.read(tk)
            sb.wrote(tk)
            for hf in range(2):
                last = P.op("tensor", lambda e, k=k, hf=hf, sb=sb, pr=pr: e.matmul(pr.t[:, hf * 512:(hf + 1) * 512], ones_bf[:],
                                                                                   sb.t[:, hf * 512:(hf + 1) * 512],
                                                                                   start=(k == 0), stop=(k == KC - 1)),
                            waits=(sb.rw() + (pr.ww() + ctok if k == 0 else [])) if hf == 0 else (), count=(hf == 1))
            sb.read(last)
        pr.wrote(last)
        t1 = P.op("vector", lambda e: e.tensor_scalar(out=rstd.t[:], in0=pr.t[:], scalar1=1.0 / D, scalar2=EPS, op0=ALU.mult, op1=ALU.add),
                  waits=pr.rw() + rstd.ww())
        pr.read(t1)
        t2 = P.op("scalar", lambda e: e.sqrt(out=rstd.t[:], in_=rstd.t[:]), waits=[t1])
        t3 = P.op("vector", lambda e: e.reciprocal(out=rstd.t[:], in_=rstd.t[:]), waits=[t2])
        rstd.wrote(t3)

    rs_d = nc.dram_tensor("rs_d", [128, TOK], F32).ap()
    rsem = P.dsem("rsem")
    rs_tok = []

    def norm_stage(gi, save=False, reuse=False):
        if reuse:
            tk = P.dma("sync", rsem, rstd.t[:], rs_d, waits=rstd.ww() + rs_tok)
            rstd.wrote(tk)
        else:
            norm_stats()
            if save:
                tk = P.dma("sync", rsem, rs_d, rstd.t[:], waits=rstd.rw())
                rstd.read(tk)
                rs_tok.append(tk)
        first = True
        for k in range(KC):
            hb = load_h(k, deep=True)
            tk = P.op("vector", lambda e, k=k, hb=hb: e.scalar_tensor_tensor(out=actA[:, k, :], in0=hb.t[:], scalar=gam[:, gi, k:k + 1], in1=rstd.t[:],
                                                                             op0=ALU.mult, op1=ALU.mult),
                      waits=hb.rw() + rstd.rw() + (A.ww() if first else []))
            first = False
            hb.read(tk)
            rstd.read(tk)
            if k == 0:
                A.wrote(tk)
            else:
                A.wrote_more(tk)

    def resid_update(i, pd, scale):
        hb = load_h(i)
        nb = hnew[st["st"] % 2]
        ssem = hst[st["st"] % 2]
        st["st"] += 1
        tk = P.op("vector", lambda e: e.scalar_tensor_tensor(out=nb.t[:], in0=pd.t[:], scalar=scale, in1=hb.t[:],
                                                             op0=ALU.mult, op1=ALU.add),
                  waits=pd.rw() + hb.rw() + nb.ww())
        pd.read(tk)
        hb.read(tk)
        nb.wrote(tk)
        tk2 = P.dma("sync", ssem, hT[i], nb.t[:], waits=nb.rw() + HT[i].ww())
        nb.read(tk2)
        HT[i].wrote(tk2)

    def ffn(gi, Wg, Wu, Wd):
        norm_stage(gi)
        groups = [(0, 29), (29, 29), (58, 28)]
        for gidx, (f0, G) in enumerate(groups):
            for jj in range(G):
                j = f0 + jj
                pg = job(Wg[:, j * 128:(j + 1) * 128], KC, lambda k, hf: actA[:, k, hf * 512:(hf + 1) * 512], pe_waits=A.rw())
                pu = job(Wu[:, j * 128:(j + 1) * 128], KC, lambda k, hf: actA[:, k, hf * 512:(hf + 1) * 512])
                A.read(pu.ready[0])
                sb = stmp[st["tmp"] % 2]
                st["tmp"] += 1
                tk = P.op("scalar", lambda e, pg=pg, sb=sb: e.activation(out=sb.t[:], in_=pg.t[:], func=AF.Silu),
                          waits=pg.rw() + sb.ww())
                pg.read(tk)
                sb.wrote(tk)
                tk2 = P.op("vector", lambda e, pu=pu, sb=sb, jj=jj: e.tensor_tensor(out=actT[:, jj, :], in0=pu.t[:], in1=sb.t[:], op=ALU.mult),
                           waits=pu.rw() + sb.rw() + (ACT.ww() if jj == 0 else []))
                pu.read(tk2)
                sb.read(tk2)
                if jj == 0:
                    ACT.wrote(tk2)
                else:
                    ACT.wrote_more(tk2)
            for i in range(KC):
                pd = job(Wd[f0 * 128:(f0 + G) * 128, i * 128:(i + 1) * 128], G,
                         lambda k, hf: actT[:, k, hf * 512:(hf + 1) * 512], pe_waits=ACT.rw())
                ACT.read(pd.ready[0])
                resid_update(i, pd, 0.5)

    osem = P.dsem("osem")

    def final_stage():
        norm_stats()
        cnt = 0
        for k in range(KC):
            hb = load_h(k)
            nb = hnew[st["st"] % 2]
            st["st"] += 1
            tk = P.op("vector", lambda e, k=k, hb=hb, nb=nb: e.scalar_tensor_tensor(out=nb.t[:], in0=hb.t[:], scalar=gam[:, 3, k:k + 1], in1=rstd.t[:],
                                                                                    op0=ALU.mult, op1=ALU.mult),
                      waits=hb.rw() + rstd.rw() + nb.ww())
            hb.read(tk)
            rstd.read(tk)
            nb.wrote(tk)
            for tg in range(2):
                pr = next_ps()
                last = None
                for tt in range(4):
                    t = tg * 4 + tt
                    last = P.op("tensor", lambda e, t=t, tt=tt, pr=pr, nb=nb: e.transpose(pr.t[:, tt * 128:(tt + 1) * 128],
                                                                                          nb.t[:, t * 128:(t + 1) * 128], ident[:]),
                                waits=(nb.rw() + pr.ww() + ctok) if tt == 0 else (), count=(tt == 3))
                pr.wrote(last)
                nb.read(last)
                sb = xst[cnt % 2]
                cnt += 1
                tk = P.op("vector", lambda e, pr=pr, sb=sb: e.tensor_copy(out=sb.t[:].rearrange("p a b -> p (a b)"), in_=pr.t[:, 0:512]),
                          waits=pr.rw() + sb.ww())
                pr.read(tk)
                sb.wrote(tk)
                dst = out[tg * 512:(tg + 1) * 512, k * 128:(k + 1) * 128].rearrange("(tt p) f -> p tt f", p=128)
                tk2 = P.dma("sync", xsts[(cnt - 1) % 2], dst, sb.t[:], waits=sb.rw())
                sb.read(tk2)

    if MIX:
        class Flow:
            def __init__(self, start=()):
                self.tk = None
                self.start = list(start)
                self.q = []
                self.lazy = False

            def op(self, eng, fn, extra=()):
                if self.lazy:
                    self.q.append(("op", eng, fn, list(extra)))
                    return None
                self.tk = P.op(eng, fn, waits=[self.tk] + self.start + list(extra))
                self.start = []
                return self.tk

            def dma(self, eng, sem, out_, in_, extra=()):
                if self.lazy:
                    self.q.append(("dma", eng, sem, out_, in_, list(extra)))
                    return None
                self.tk = P.dma(eng, sem, out_, in_, waits=[self.tk] + self.start + list(extra))
                self.start = []
                return self.tk

            def call(self, fn):
                if self.lazy:
                    self.q.append(("call", fn))
                else:
                    fn()

            def emit_one(self):
                it = self.q.pop(0)
                self.lazy = False
                if it[0] == "op":
                    self.op(it[1], it[2], it[3])
                elif it[0] == "dma":
                    self.dma(it[1], it[2], it[3], it[4], it[5])
                else:
                    it[1]()
                self.lazy = True

        def run_flows(pairs):
            flows = []
            for f, g in pairs:
                f.lazy = True
                for _ in g:
                    pass
                flows.append(f)
            lens = [len(f.q) for f in flows]
            R = max(lens) if lens else 0
            done = [0] * len(flows)
            for r in range(R):
                for i, f in enumerate(flows):
                    tgt = ((r + 1) * lens[i] + R - 1) // R
                    while done[i] < tgt and f.q:
                        f.emit_one()
                        done[i] += 1
            for f in flows:
                while f.q:
                    f.emit_one()
                f.lazy = False

        def interleave(gens):
            gens = list(gens)
            while gens:
                nxt = []
                for g in gens:
                    try:
                        next(g)
                        nxt.append(g)
                    except StopIteration:
                        pass
                gens = nxt

        def raw_inc(eng, fn, waits, sem, amt):
            w = P._waits(eng, waits)
            sem.n += amt
            P.ops[eng].append((w, fn, (sem.h, amt)))
            return (sem, sem.n)

        rhsA = lambda k, hf: actA[:, k, hf * 512:(hf + 1) * 512]
        T_U, T_NEG, T_STR, T_ONE, T_DTB, T_ALOG, T_GOUT, T_SEL = range(8)
        S_G, S_BETA, S_GC, S_GL, S_EG, S_EGL, S_EKD, S_BG, S_NEGA, S_TMP = range(10)
        GROUPS4 = [[0, 1, 2, 3], [4, 5, 6, 7]]
        msem = [P.dsem(f"msem{i}") for i in range(8)]
        ccsem = P.dsem("ccsem")

        def mixer_front():
            norm_stage(1, save=True)
            f = Flow(start=A.rw() + ctok)
            f.dma("sync", msem[0], tab[:], tab_in)
            f.dma("sync", msem[0], cw[:], cw_in)
            f.dma("gpsimd", msem[1], wab[:], w_in[:, 14336:14368].rearrange("(k p) f -> p k f", p=128))
            for t in range(NT):
                slot = pss[t // 4][:, (t % 4) * 128:(t % 4) * 128 + 32]
                for k in range(KC):
                    tk = P.op("tensor", lambda e, k=k, t=t, slot=slot: e.matmul(slot, actA[:, k, t * 128:(t + 1) * 128], wab[:, k, :],
                                                                                start=(k == 0), stop=(k == KC - 1)),
                              waits=[f.tk] if k == 0 else (), count=(k == KC - 1))
                f.tk = tk
                f.op("vector", lambda e, t=t, slot=slot: e.tensor_copy(out=sc[:, S_G, t * 16:(t + 1) * 16], in_=slot[:, 0:16]))
                f.op("vector", lambda e, t=t, slot=slot: e.tensor_copy(out=sc[:, S_BETA, t * 16:(t + 1) * 16], in_=slot[:, 16:32]))
            f.op("vector", lambda e: e.tensor_tensor(out=sc[:, S_TMP, :], in0=sc[:, S_G, :], in1=tab[:, T_DTB, :], op=ALU.add))
            f.op("scalar", lambda e: e.activation(out=sc[:, S_TMP, :], in_=sc[:, S_TMP, :], func=AF.Exp))
            f.op("scalar", lambda e: e.activation(out=sc[:, S_TMP, :], in_=sc[:, S_TMP, :], func=AF.Ln, bias=1.0))
            f.op("scalar", lambda e: e.activation(out=sc[:, S_NEGA, :], in_=tab[:, T_ALOG, :], func=AF.Exp))
            f.op("vector", lambda e: e.tensor_scalar(out=sc[:, S_NEGA, :], in0=sc[:, S_NEGA, :], scalar1=-1.0, scalar2=None, op0=ALU.mult))
            f.op("vector", lambda e: e.tensor_tensor(out=sc[:, S_G, :], in0=sc[:, S_TMP, :], in1=sc[:, S_NEGA, :], op=ALU.mult))
            f.op("scalar", lambda e: e.activation(out=sc[:, S_BETA, :], in_=sc[:, S_BETA, :], func=AF.Sigmoid))
            f.op("tensor", lambda e: e.matmul(pss[0][:, 0:128], tab[:, T_U, :], sc[:, S_G, :], start=True, stop=True))
            f.op("tensor", lambda e: e.matmul(pss[0][:, 128:256], tab[:, T_ONE, :], sc[:, S_G, :], start=True, stop=True))
            f.op("vector", lambda e: e.tensor_copy(out=sc[:, S_GC, :], in_=pss[0][:, 0:128]))
            f.op("vector", lambda e: e.tensor_copy(out=sc[:, S_GL, :], in_=pss[0][:, 128:256]))
            f.op("scalar", lambda e: e.activation(out=sc[:, S_EG, :], in_=sc[:, S_GC, :], func=AF.Exp))
            f.op("scalar", lambda e: e.activation(out=sc[:, S_EGL, :], in_=sc[:, S_GL, :], func=AF.Exp))
            f.op("vector", lambda e: e.tensor_tensor(out=sc[:, S_TMP, :], in0=sc[:, S_GL, :], in1=sc[:, S_GC, :], op=ALU.subtract))
            f.op("scalar", lambda e: e.activation(out=sc[:, S_EKD, :], in_=sc[:, S_TMP, :], func=AF.Exp))
            f.op("vector", lambda e: e.tensor_tensor(out=sc[:, S_BG, :], in0=sc[:, S_BETA, :], in1=sc[:, S_EG, :], op=ALU.mult))
            sc_tok = f.tk

            def spill(pbuf_fn, dst, hidx, ssem_i):
                nb = hnew[st["st"] % 2]
                ssem = hst[st["st"] % 2]
                st["st"] += 1
                tk = pbuf_fn(nb)
                nb.wrote(tk)
                tk2 = P.dma("sync", ssem, dst, nb.t[:], waits=nb.rw())
                nb.read(tk2)
                tk3 = P.op("vector", lambda e, nb=nb, hidx=hidx: e.tensor_copy(out=hsb[:, hidx * 3:(hidx + 1) * 3], in_=nb.t[:, TOK - 3:TOK]),
                           waits=nb.rw())
                nb.read(tk3)
                return tk3

            last_h = None
            for i in range(16):
                pC = job(w_in[:, 2048 + i * 128:2048 + (i + 1) * 128], KC, rhsA, pe_waits=A.rw())
                px = job(w_in[:, 4096 + i * 128:4096 + (i + 1) * 128], KC, rhsA)
                hb = hold[st["ld"] % 2]
                st["ld"] += 1
                tk = P.op("scalar", lambda e, pC=pC, hb=hb: e.activation(out=hb.t[:], in_=pC.t[:], func=AF.Copy), waits=pC.rw() + hb.ww())
                pC.read(tk)
                hb.wrote(tk)

                def mk(nb, px=px, hb=hb):
                    t_ = P.op("vector", lambda e: e.tensor_tensor(out=nb.t[:], in0=px.t[:], in1=hb.t[:], op=ALU.mult),
                              waits=px.rw() + hb.rw() + nb.ww())
                    px.read(t_)
                    hb.read(t_)
                    return t_
                last_h = spill(mk, pc_d[i], i, 0)
            for c in range(48):
                pq = job(w_in[:, 6144 + c * 128:6144 + (c + 1) * 128], KC, rhsA)

                def mk(nb, pq=pq, c=c):
                    if c % 2 == 0:
                        t_ = P.op("scalar", lambda e: e.activation(out=nb.t[:], in_=pq.t[:], func=AF.Copy), waits=pq.rw() + nb.ww())
                    else:
                        t_ = P.op("vector", lambda e: e.tensor_copy(out=nb.t[:], in_=pq.t[:]), waits=pq.rw() + nb.ww())
                    pq.read(t_)
                    return t_
                last_h = spill(mk, pqkv_d[c], 16 + c, 0)
            for h in range(HEADS):
                pz = job(w_in[:, 12288 + h * 128:12288 + (h + 1) * 128], KC, rhsA)
                sb = stmp[st["tmp"] % 2]
                ssem = msem[2 + st["tmp"] % 2]
                st["tmp"] += 1
                tk = P.op("scalar", lambda e, pz=pz, sb=sb: e.activation(out=sb.t[:], in_=pz.t[:], func=AF.Silu), waits=pz.rw() + sb.ww())
                pz.read(tk)
                sb.wrote(tk)
                tk2 = P.dma("sync", ssem, sz_d[h], sb.t[:], waits=sb.rw())
                sb.read(tk2)
            f = Flow(start=[last_h, sc_tok])
            f.dma("sync", msem[4], hsend_t.ap(), hsb[:])
            raw_inc("gpsimd", lambda e: e.collective_compute("AllGather", ALU.bypass, replica_groups=GROUPS4,
                                                             ins=[hsend_t.ap().opt()], outs=[hall_t.ap().opt()]),
                    [f.tk], ccsem, 1)
            f.tk = (ccsem, ccsem.n)
            f.dma("sync", msem[4], hal[:], hall_t.ap().rearrange("(j p) f -> p j f", p=128))
            f.op("vector", lambda e: e.tensor_scalar(out=halo[:], in0=hal[:, 0, :], scalar1=tab[:, T_SEL, 0:1], scalar2=None, op0=ALU.mult))
            for j in range(1, 4):
                f.op("vector", lambda e, j=j: e.scalar_tensor_tensor(out=halo[:], in0=hal[:, j, :], scalar=tab[:, T_SEL, j:j + 1], in1=halo[:],
                                                                      op0=ALU.mult, op1=ALU.add))
            return f.tk

        def conv_taps(f, eng, y, xr, wcol0, K):
            for j in range(K):
                off = 3 - (K - 1) + j
                if j == 0:
                    f.op(eng, lambda e, off=off, j=j: e.tensor_scalar(out=y, in0=xr[:, off:off + TOK], scalar1=cw[:, wcol0 + j:wcol0 + j + 1],
                                                                      scalar2=None, op0=ALU.mult))
                else:
                    f.op(eng, lambda e, off=off, j=j: e.scalar_tensor_tensor(out=y, in0=xr[:, off:off + TOK], scalar=cw[:, wcol0 + j:wcol0 + j + 1],
                                                                             in1=y, op0=ALU.mult, op1=ALU.add))

        AR2 = hold[0].t

        def conv_branch(halo_tok):
            a2 = SB_BASE + 65536 * 2 + 32768
            sets = [(nc.alloc_sbuf_tensor_at("cxr0", [128, TOK + 3], F32, offset=a2), nc.alloc_sbuf_tensor_at("cy0", [128, TOK], F32, offset=a2 + 8192)),
                    (nc.alloc_sbuf_tensor_at("cxr1", [128, TOK + 3], F32, offset=a2 + 12288), nc.alloc_sbuf_tensor_at("cy1", [128, TOK], F32, offset=a2 + 20480))]
            flows = [Flow(start=[halo_tok]), Flow(start=[halo_tok])]
            ytoks = []
            for i in range(16):
                pB = job(w_in[:, i * 128:(i + 1) * 128], KC, rhsA)
                xr, y = sets[i % 2]
                f = flows[i % 2]
                f.dma("sync", msem[5 + i % 2], xr[:, 3:TOK + 3], pc_d[i])
                f.op("vector", lambda e, xr=xr, i=i: e.tensor_copy(out=xr[:, 0:3], in_=halo[:, i * 3:(i + 1) * 3]))
                conv_taps(f, "vector", y[:], xr, i * 3, 3)
                tk = f.op("vector", lambda e, y=y, pB=pB, i=i: e.tensor_tensor(out=yconv[:, i, :], in0=pB.t[:], in1=y[:], op=ALU.mult), extra=pB.rw())
                pB.read(tk)
                ytoks.append(tk)
            return ytoks[-2:]

        def delta_phase(dmode=None):
            A2 = SB_BASE + 131072
            A2_END = A2 + 57344
            dc = [SB_BASE]

            def dal(name, shape, dt):
                n = int(np.prod(shape[1:])) * (4 if dt == F32 else 2)
                sz_ = (n + 31) // 32 * 32
                if dc[0] < A2 and dc[0] + sz_ > SB_BASE + 65536:
                    dc[0] = A2
                at = dc[0]
                dc[0] += sz_
                assert dc[0] <= SB_BASE + 65536 or (at >= A2 and dc[0] <= A2_END), (name, dc[0])
                return nc.alloc_sbuf_tensor_at(name, list(shape), dt, offset=at)

            NW = 4
            Xraw = dal("Xraw", [128, 3, TOK + 3], F32)
            Y3s = [dal(f"Y3_{i}", [128, 3, TOK], F32) for i in range(2)]
            kqs = [dal(f"kq_{i}", [128, 2, TOK], BF16) for i in range(2)]
            sqb = dal("sqb", [128, TOK], BF16)
            rinv = dal("rinv", [128, TOK], F32)

            def flowbufs(w):
                d = {}
                d["GB"] = dal(f"GB{w}", [128, 256], F32)
                d["Ebc"] = dal(f"Ebc{w}", [128, 128], F32)
                d["Mm"] = dal(f"Mm{w}", [128, 128], F32)
                d["MT"] = dal(f"MT{w}", [128, 128], F32)
                d["Rr"] = dal(f"Rr{w}", [128, 128], F32)
                base = dc[0]
                d["PQ"] = dal(f"PQ{w}", [128, 4, 128], F32)
                d["UgIb"] = nc.alloc_sbuf_tensor_at(f"UgIb{w}", [128, 256], F32, offset=base)
                d["SBm"] = nc.alloc_sbuf_tensor_at(f"SBm{w}", [128, 128], F32, offset=base + 1024)
                d["Dm"] = nc.alloc_sbuf_tensor_at(f"Dm{w}", [128, 128], F32, offset=base + 1536)
                d["Tt"] = dal(f"Tt{w}", [128, 128], BF16)
                d["kbg"] = dal(f"kbg{w}", [128, 128], BF16)
                d["vbb"] = dal(f"vbb{w}", [128, 128], BF16)
                return d
            FB = [flowbufs(0), flowbufs(1)]
            Xs = dal("Xs", [128, 256], F32)
            Xbf = dal("Xbf", [128, 256], BF16)
            vn1 = dal("vn1", [128, 256], BF16)
            HBbs = [dal(f"HBb{i}", [128, NT, 4, 128], BF16) for i in range(2)]
            HBus = [dal(f"HBu{i}", [128, NT, 128], F32) for i in range(2)]
            FB += [flowbufs(2), flowbufs(3)]
            dsm = [P.dsem(f"dsm{i}") for i in range(8)]
            idn = ident[:]
            fbank = [pss[0], pss[1], psum[1].t[:, 0:512], psum[1].t[:, 512:1024]]
            p1bank = psum[2].t[:, 0:512]
            bpsum = psum[0].t

            def colf(s_, th):
                return sc[:, s_, th:th + 1]

            def bulk(f, h, Y3, kq_bf):
                for c, idx in enumerate((h, 16 + h, 32 + h)):
                    f.dma("sync", dsm[0], Xraw[:, c, 3:TOK + 3], pqkv_d[idx])
                    f.op("vector", lambda e, c=c, idx=idx: e.tensor_copy(out=Xraw[:, c, 0:3], in_=halo[:, (16 + idx) * 3:(16 + idx) * 3 + 3]))
                    yield
                    conv_taps(f, "vector", Y3[:, c, :], Xraw[:, c, :], 48 + idx * 4, 4)
                    yield
                    f.op("scalar", lambda e, c=c: e.activation(out=Y3[:, c, :], in_=Y3[:, c, :], func=AF.Silu))
                    yield
                for c in (0, 1):
                    f.op("scalar", lambda e, c=c: e.activation(out=sqb[:], in_=Y3[:, c, :], func=AF.Square))
                    for hf in range(2):
                        f.op("tensor", lambda e, hf=hf: e.matmul(bpsum[:, hf * 512:(hf + 1) * 512], ones_bf[:], sqb[:, hf * 512:(hf + 1) * 512],
                                                                 start=True, stop=True))
                    yield
                    if c == 0:
                        f.op("vector", lambda e: e.tensor_scalar(out=rinv[:], in0=bpsum[:], scalar1=EPS, scalar2=128.0, op0=ALU.add, op1=ALU.mult))
                    else:
                        f.op("vector", lambda e: e.tensor_scalar(out=rinv[:], in0=bpsum[:], scalar1=EPS, scalar2=None, op0=ALU.add))
                    f.op("scalar", lambda e: e.sqrt(out=rinv[:], in_=rinv[:]))
                    yield
                    f.op("vector", lambda e: e.reciprocal(out=rinv[:], in_=rinv[:]))
                    f.op("vector", lambda e, c=c: e.tensor_tensor(out=Y3[:, c, :], in0=Y3[:, c, :], in1=rinv[:], op=ALU.mult))
                    f.op("scalar", lambda e, c=c: e.activation(out=kq_bf[:, 1 - c, :], in_=Y3[:, c, :], func=AF.Copy))
                    yield

            def tile_flow(f, h, t, w, Y3, kq_bf, HBb, HBu):
                th = t * 16 + h
                tl = slice(t * 128, (t + 1) * 128)
                B = FB[w]
                GB, Ebc, Mm, MT, Rr, PQ, UgIb, SBm, Dm, Tt, kbg, vbb = (B[k] for k in ("GB", "Ebc", "Mm", "MT", "Rr", "PQ", "UgIb", "SBm", "Dm", "Tt", "kbg", "vbb"))
                pab = fbank[w][:, 0:256]
                pa = fbank[w][:, 0:128]
                pb = fbank[w][:, 128:256]
                f.op("vector", lambda e: e.tensor_scalar(out=UgIb[:, 0:128], in0=tab[:, T_U, :], scalar1=colf(S_G, th), scalar2=None, op0=ALU.mult))
                f.op("vector", lambda e: e.tensor_scalar(out=UgIb[:, 128:256], in0=idn, scalar1=colf(S_BETA, th), scalar2=None, op0=ALU.mult))
                f.op("tensor", lambda e: e.matmul(pab, tab[:, T_ONE, :], UgIb[:], start=True, stop=True))
                f.op("scalar", lambda e: e.activation(out=GB[:], in_=pab, func=AF.Copy))
                yield
                f.op("vector", lambda e: e.scalar_tensor_tensor(out=Dm[:], in0=GB[:, 0:128], scalar=colf(S_GC, th), in1=tab[:, T_NEG, :],
                                                                op0=ALU.subtract, op1=ALU.add))
                f.op("scalar", lambda e: e.activation(out=Dm[:], in_=Dm[:], func=AF.Exp))
                f.op("scalar", lambda e: e.activation(out=Ebc[:], in_=GB[:, 0:128], func=AF.Exp))
                f.op("tensor", lambda e: e.matmul(pab.rearrange("p (a b) -> p a b", a=2), kq_bf[:, 0, tl], kq_bf[:, :, tl], start=True, stop=True))
                yield
                f.op("vector", lambda e: e.tensor_tensor(out=HBb[:, t, 0, :], in0=pb, in1=Dm[:], op=ALU.mult))
                f.op("vector", lambda e: e.tensor_tensor(out=SBm[:], in0=GB[:, 128:256], in1=tab[:, T_STR, :], op=ALU.mult))
                f.op("vector", lambda e: e.tensor_tensor(out=Mm[:], in0=pa, in1=Dm[:], op=ALU.mult))
                f.op("vector", lambda e: e.tensor_tensor(out=Mm[:], in0=Mm[:], in1=SBm[:], op=ALU.mult))
                yield
                f.op("tensor", lambda e: e.transpose(pa, Mm[:], idn))
                f.op("scalar", lambda e: e.activation(out=MT[:], in_=pa, func=AF.Copy))
                f.op("vector", lambda e: e.scalar_tensor_tensor(out=Rr[:], in0=Mm[:], scalar=-1.0, in1=idn, op0=ALU.mult, op1=ALU.add))
                yield
                Pc, Qc = Mm[:], MT[:]
                for k in range(6):
                    Pn = PQ[:, (k % 2) * 2, :]
                    Qn = PQ[:, (k % 2) * 2 + 1, :]
                    f.op("tensor", lambda e, Pc=Pc, Qc=Qc: e.matmul(pa, Qc, Pc, start=True, stop=True))
                    f.op("tensor", lambda e, Pc=Pc, Qc=Qc: e.matmul(pb, Pc, Qc, start=True, stop=True))
                    f.op("scalar", lambda e, Pn=Pn: e.activation(out=Pn, in_=pa, func=AF.Copy))
                    f.op("vector", lambda e, Qn=Qn: e.tensor_copy(out=Qn, in_=pb))
                    yield
                    f.op("tensor", lambda e, Qn=Qn: e.matmul(pa, Qn, Rr[:], start=True, stop=True))
                    f.op("vector", lambda e: e.tensor_tensor(out=Rr[:], in0=pa, in1=Rr[:], op=ALU.add))
                    Pc, Qc = Pn, Qn
                    yield
                f.op("vector", lambda e: e.tensor_copy(out=Tt[:], in_=Rr[:]))
                f.op("tensor", lambda e: e.transpose(pa, Y3[:, 1, tl], idn))
                f.op("tensor", lambda e: e.transpose(pb, Y3[:, 2, tl], idn))
                f.op("scalar", lambda e: e.activation(out=kbg[:], in_=pa, func=AF.Copy, scale=colf(S_BG, th)))
                yield
                f.op("vector", lambda e: e.tensor_scalar(out=HBb[:, t, 1, :], in0=pa, scalar1=colf(S_EKD, th), scalar2=None, op0=ALU.mult))
                f.op("scalar", lambda e: e.activation(out=vbb[:], in_=pb, func=AF.Copy, scale=colf(S_BETA, th)))
                f.op("tensor", lambda e: e.matmul(pa, Tt[:], vbb[:], start=True, stop=True))
                f.op("tensor", lambda e: e.matmul(pb, kbg[:], Tt[:], start=True, stop=True))
                yield
                f.op("scalar", lambda e: e.activation(out=HBu[:, t, :], in_=pa, func=AF.Copy))
                f.op("vector", lambda e: e.tensor_copy(out=HBb[:, t, 2, :], in_=pb))
                f.op("vector", lambda e: e.tensor_tensor(out=HBb[:, t, 3, :], in0=kq_bf[:, 1, tl], in1=Ebc[:], op=ALU.mult))
                yield

            def tile_seq(f, h, w, Y3, kq_bf, HBb, HBu):
                for t in range(w, NT, NW):
                    yield from tile_flow(f, h, t, w, Y3, kq_bf, HBb, HBu)

            def pass1(f, h, HBb, HBu):
                pab0 = p1bank[:, 0:256]
                f.op("vector", lambda e: e.memset(Xs[:, 0:128], 0.0))
                f.op("vector", lambda e: e.tensor_copy(out=Xs[:, 128:256], in_=idn))
                f.op("vector", lambda e: e.tensor_copy(out=Xbf[:], in_=Xs[:]))
                yield
                for t in range(NT):
                    th = t * 16 + h
                    f.op("tensor", lambda e, t=t: e.matmul(pab0, HBb[:, t, 2, :], Xbf[:], start=True, stop=True))
                    f.op("vector", lambda e, t=t: e.tensor_tensor(out=vn1[:, 0:128], in0=HBu[:, t, :], in1=pab0[:, 0:128], op=ALU.subtract))
                    f.op("vector", lambda e: e.tensor_scalar(out=vn1[:, 128:256], in0=pab0[:, 128:256], scalar1=-1.0, scalar2=None, op0=ALU.mult))
                    yield
                    f.op("tensor", lambda e, t=t: e.matmul(pab0, HBb[:, t, 1, :], vn1[:], start=True, stop=True))
                    f.op("vector", lambda e, th=th: e.scalar_tensor_tensor(out=Xs[:], in0=Xs[:], scalar=colf(S_EGL, th), in1=pab0, op0=ALU.mult, op1=ALU.add))
                    f.op("scalar", lambda e: e.activation(out=Xbf[:], in_=Xs[:], func=AF.Copy))
                    yield
                f.dma("sync", dsm[1], gsend_t[h // 4].ap()[(h % 4) * 128:(h % 4 + 1) * 128, :], Xs[:])
                f.call(lambda h=h: gs_toks.__setitem__(h, f.tk))
                f.dma("sync", dsm[1], spb_d[h], HBb[:].rearrange("p a b c -> p (a b c)"))
                f.dma("sync", dsm[1], spu_d[h], HBu[:].rearrange("p a b -> p (a b)"))
                yield

            NHD = 1 if dmode in ("delta1", "delta1c", "delta1x") else HEADS
            gs_toks = [None] * HEADS
            bulk_tok = {}
            tiles_tok = {}
            p1_tok = {}
            cc_toks = []
            for step in range(NHD + 2):
                hb, ht, hp = step, step - 1, step - 2
                gens = []
                fb = fl = fp = None
                if hb < NHD:
                    fb = Flow(start=[bulk_tok.get(hb - 1)] + tiles_tok.get(hb - 2, []))
                    gens.append((fb, bulk(fb, hb, Y3s[hb % 2], kqs[hb % 2])))
                if 0 <= ht < NHD:
                    fl = [Flow(start=[bulk_tok[ht], p1_tok.get(ht - 2)] + tiles_tok.get(ht - 1, [])) for _ in range(NW)]
                    gens += [(fl[w], tile_seq(fl[w], ht, w, Y3s[ht % 2], kqs[ht % 2], HBbs[ht % 2], HBus[ht % 2])) for w in range(NW)]
                if 0 <= hp < NHD:
                    fp = Flow(start=tiles_tok[hp] + [p1_tok.get(hp - 1)])
                    gens.append((fp, pass1(fp, hp, HBbs[hp % 2], HBus[hp % 2])))
                run_flows(gens)
                if fb is not None:
                    bulk_tok[hb] = fb.tk
                if fl is not None:
                    tiles_tok[ht] = [f_.tk for f_ in fl]
                if fp is not None:
                    p1_tok[hp] = fp.tk
                    if hp % 4 == 3 and dmode not in ("delta1", "delta1x"):
                        raw_inc("gpsimd", lambda e, q=hp // 4: e.collective_compute("AllGather", ALU.bypass, replica_groups=GROUPS4,
                                                                                    ins=[gsend_t[q].ap().opt()], outs=[gall_t[q].ap().opt()]),
                                [gs_toks[hh] for hh in range(hp - 3, hp + 1)], ccsem, 1)
                        cc_toks.append((ccsem, ccsem.n))
            if dmode == "delta1":
                return
            while len(cc_toks) < 4:
                if dmode == "delta1c" and not cc_toks:
                    raw_inc("gpsimd", lambda e: e.collective_compute("AllGather", ALU.bypass, replica_groups=GROUPS4,
                                                                     ins=[gsend_t[0].ap().opt()], outs=[gall_t[0].ap().opt()]),
                            [gs_toks[0]], ccsem, 1)
                    cc_toks.append((ccsem, ccsem.n))
                else:
                    cc_toks.append(p1_tok[NHD - 1])
            P.barrier()
            dc[0] = SB_BASE

            def p2bufs(i):
                d = {}
                d["HBb"] = dal(f"q_HBb{i}", [128, NT, 4, 128], BF16)
                d["HBu"] = dal(f"q_HBu{i}", [128, NT, 128], F32)
                d["szb"] = dal(f"q_szb{i}", [128, TOK], BF16)
                d["S"] = dal(f"q_S{i}", [128, 128], F32)
                d["Sbf"] = dal(f"q_Sbf{i}", [128, 128], BF16)
                d["vn"] = dal(f"q_vn{i}", [128, 128], BF16)
                d["onb"] = dal(f"q_onb{i}", [128, 128], F32)
                d["ssq"] = dal(f"q_ssq{i}", [128, 8], F32)
                d["AB"] = dal(f"q_AB{i}", [128, 256], F32)
                d["ATs"] = dal(f"q_ATs{i}", [128, 128], F32)
                return d
            PB = [p2bufs(i) for i in range(4)]

            def pass2(f, h, i):
                B = PB[i]
                HBb, HBu, szb, Ss, Sbf, vn, onb, ssq, AB, ATs = (B[k] for k in ("HBb", "HBu", "szb", "S", "Sbf", "vn", "onb", "ssq", "AB", "ATs"))
                pa = fbank[i][:, 0:128]
                pb = fbank[i][:, 128:256]
                f.dma("sync", dsm[4 + i], HBb[:].rearrange("p a b c -> p (a b c)"), spb_d[h])
                f.dma("sync", dsm[4 + i], HBu[:].rearrange("p a b -> p (a b)"), spu_d[h])
                f.dma("sync", dsm[4 + i], szb[:], sz_d[h])
                f.op("vector", lambda e: e.memset(Ss[:], 0.0))
                yield
                f.call(lambda h=h: f.start.append(cc_toks[h // 4]))
                for j in range(3):
                    f.dma("sync", dsm[4 + i], AB[:], gall_t[h // 4].ap()[(j * 4 + h % 4) * 128:(j * 4 + h % 4 + 1) * 128, :])
                    f.op("tensor", lambda e: e.transpose(pa, AB[:, 128:256], idn))
                    f.op("scalar", lambda e: e.activation(out=ATs[:], in_=pa, func=AF.Copy))
                    yield
                    f.op("tensor", lambda e: e.matmul(pb, ATs[:], Ss[:], start=True, stop=True))
                    f.op("vector", lambda e: e.tensor_tensor(out=ATs[:], in0=pb, in1=AB[:, 0:128], op=ALU.add))
                    f.op("vector", lambda e: e.tensor_tensor(out=ATs[:], in0=ATs[:], in1=Ss[:], op=ALU.subtract))
                    f.op("vector", lambda e, j=j: e.scalar_tensor_tensor(out=Ss[:], in0=ATs[:], scalar=tab[:, T_SEL, 4 + j:5 + j], in1=Ss[:],
                                                                          op0=ALU.mult, op1=ALU.add))
                    yield
                f.op("vector", lambda e: e.tensor_copy(out=Sbf[:], in_=Ss[:]))
                for t in range(NT):
                    th = t * 16 + h
                    tl = slice(t * 128, (t + 1) * 128)
                    f.op("tensor", lambda e, t=t: e.matmul(pa, HBb[:, t, 2, :], Sbf[:], start=True, stop=True))
                    f.op("vector", lambda e, t=t: e.tensor_tensor(out=vn[:], in0=HBu[:, t, :], in1=pa, op=ALU.subtract))
                    yield
                    f.op("tensor", lambda e, t=t: e.matmul(pb, HBb[:, t, 3, :], Sbf[:], start=True, stop=False))
                    f.op("tensor", lambda e, t=t: e.matmul(pb, HBb[:, t, 0, :], vn[:], start=False, stop=True))
                    f.op("tensor", lambda e, t=t: e.matmul(pa, HBb[:, t, 1, :], vn[:], start=True, stop=True))
                    yield
                    f.op("vector", lambda e, th=th: e.scalar_tensor_tensor(out=Ss[:], in0=Ss[:], scalar=colf(S_EGL, th), in1=pa, op0=ALU.mult, op1=ALU.add))
                    f.op("scalar", lambda e: e.activation(out=Sbf[:], in_=Ss[:], func=AF.Copy))
                    f.op("vector", lambda e: e.memset(ssq[:, 0:1], 0.0))
                    yield
                    f.op("scalar", lambda e: e.activation(out=onb[:], in_=pb, func=AF.Square, accum_out=ssq[:, 0:1]))
                    f.op("vector", lambda e: e.tensor_scalar(out=ssq[:, 0:1], in0=ssq[:, 0:1], scalar1=1.0 / 128, scalar2=EPS, op0=ALU.mult, op1=ALU.add))
                    f.op("scalar", lambda e: e.sqrt(out=ssq[:, 0:1], in_=ssq[:, 0:1]))
                    yield
                    f.op("vector", lambda e: e.reciprocal(out=ssq[:, 0:1], in_=ssq[:, 0:1]))
                    f.op("vector", lambda e: e.scalar_tensor_tensor(out=onb[:], in0=pb, scalar=ssq[:, 0:1], in1=tab[:, T_GOUT, :], op0=ALU.mult, op1=ALU.mult))
                    f.op("tensor", lambda e: e.transpose(pa, onb[:], idn))
                    f.op("vector", lambda e, tl=tl, h=h: e.tensor_tensor(out=ydn[:, h, tl], in0=pa, in1=szb[:, tl], op=ALU.mult))
                    yield

            prevs = [None] * 4
            for h0 in range(0, NHD, 4):
                fls = []
                gens = []
                for i in range(min(4, NHD - h0)):
                    f_ = Flow(start=[prevs[i]])
                    fls.append(f_)
                    gens.append((f_, pass2(f_, h0 + i, i)))
                run_flows(gens)
                for i, f_ in enumerate(fls):
                    prevs[i] = f_.tk

        def merge_and_out():
            norm_stage(1, reuse=True)
            for i in range(KC):
                pgc = job(w_in[:, 14368 + i * 128:14368 + (i + 1) * 128], KC, rhsA, pe_waits=A.rw())
                hb1 = hold[st["ld"] % 2]; st["ld"] += 1
                tk = P.op("scalar", lambda e, pgc=pgc, hb1=hb1: e.activation(out=hb1.t[:], in_=pgc.t[:], func=AF.Sigmoid), waits=pgc.rw() + hb1.ww())
                pgc.read(tk); hb1.wrote(tk)
                pcp = job(w_cb[:, i * 128:(i + 1) * 128], 16, lambda k, hf: yconv[:, k, hf * 512:(hf + 1) * 512])
                nb = hnew[st["st"] % 2]; st["st"] += 1
                tk = P.op("vector", lambda e, pcp=pcp, hb1=hb1, nb=nb: e.tensor_tensor(out=nb.t[:], in0=pcp.t[:], in1=hb1.t[:], op=ALU.mult),
                          waits=pcp.rw() + hb1.rw() + nb.ww())
                pcp.read(tk); hb1.read(tk); nb.wrote(tk)
                pgd = job(w_in[:, 18464 + i * 128:18464 + (i + 1) * 128], KC, rhsA)
                hb2 = hold[st["ld"] % 2]; st["ld"] += 1
                tk = P.op("scalar", lambda e, pgd=pgd, hb2=hb2: e.activation(out=hb2.t[:], in_=pgd.t[:], func=AF.Sigmoid), waits=pgd.rw() + hb2.ww())
                pgd.read(tk); hb2.wrote(tk)
                pdp = job(w_db[:, i * 128:(i + 1) * 128], 16, lambda k, hf: ydn[:, k, hf * 512:(hf + 1) * 512])
                tk = P.op("vector", lambda e, pdp=pdp, hb2=hb2: e.tensor_tensor(out=hb2.t[:], in0=pdp.t[:], in1=hb2.t[:], op=ALU.mult),
                          waits=pdp.rw() + hb2.rw())
                pdp.read(tk); hb2.wrote(tk)
                sb = stmp[st["tmp"] % 2]
                ssem = msem[2 + st["tmp"] % 2]
                st["tmp"] += 1
                tk = P.op("vector", lambda e, sb=sb, nb=nb, hb2=hb2: e.tensor_tensor(out=sb.t[:], in0=nb.t[:], in1=hb2.t[:], op=ALU.add),
                          waits=nb.rw() + hb2.rw() + sb.ww())
                nb.read(tk); hb2.read(tk); sb.wrote(tk)
                tk2 = P.dma("sync", ssem, mrg_d[i], sb.t[:], waits=sb.rw())
                sb.read(tk2)
            P.barrier()
            toks = []
            for q4 in range(4):
                toks.append(P.dma("sync", msem[4 + q4 % 2], actA[:, q4 * 8:(q4 + 1) * 8, :], mrg_d[q4 * 8:(q4 + 1) * 8].rearrange("k p t -> p k t")))
            A.wrote(toks[0])
            for tk in toks[1:]:
                A.wrote_more(tk)
            for i in range(KC):
                po = job(w_out[:, i * 128:(i + 1) * 128], KC, rhsA, pe_waits=A.rw())
                resid_update(i, po, 1.0)


    if MIX and DELTA:
        P.dma("sync", msem[0], tab[:], tab_in)
        P.dma("sync", msem[0], cw[:], cw_in)
        P.dma("sync", msem[1], sc[:].rearrange("p a b -> p (a b)"), sc_in)
        P.dma("sync", msem[4], halo[:], halo_in)
        P.barrier()
        delta_phase(debug)
        P.barrier()
        dbg = nc.dram_tensor("dbg", [128, 16 * TOK], BF16, kind="ExternalOutput").ap()
        P.dma("sync", msem[0], dbg, ydn[:].rearrange("p a b -> p (a b)"))
        P.barrier()
        with nc.Block() as block:
            P.replay(block)
        return nc, stack
    stage_T0()
    P.barrier()
    if debug not in ("mixA", "mixB"):
        ffn(0, w1g, w1u, w1d)
        P.barrier()
    if debug != "hT":
        htok = mixer_front()
        P.barrier()
        conv_branch(htok)
        P.barrier()
        if debug != "mixA":
            delta_phase()
            P.barrier()
        if debug in ("mixA", "mixB"):
            dbg = nc.dram_tensor("dbg", [128, 16 * TOK], BF16, kind="ExternalOutput").ap()
            dbg2 = nc.dram_tensor("dbg2", [128, 12 * 128], F32, kind="ExternalOutput").ap()
            dbg3 = nc.dram_tensor("dbg3", [128, 192], F32, kind="ExternalOutput").ap()
            src = yconv if debug == "mixA" else ydn
            P.dma("sync", msem[0], dbg, src[:].rearrange("p a b -> p (a b)"))
            P.dma("sync", msem[1], dbg2, sc[:].rearrange("p a b -> p (a b)"))
            P.dma("sync", msem[4], dbg3, halo[:])
        else:
            merge_and_out()
            P.barrier()
            if debug != "h2":
                ffn(2, w2g, w2u, w2d)
                P.barrier()
                final_stage()
    P.barrier()

    with nc.Block() as block:
        P.replay(block)
    return nc, stack


def make_in_maps(inputs, debug=None):
    f = lambda k: np.asarray(inputs[k], dtype=np.float32)
    x = f("x").reshape(NCORES, TOK, D)
    gl = lambda v: np.asarray(v, np.float32).reshape(KC, 128).T
    gam = np.ascontiguousarray(np.concatenate([gl(f("ffn1_norm")[0]), gl(f("mix_norm")[0]),
                                               gl(f("ffn2_norm")[0]), gl(f("final_norm"))], axis=1))
    common = {
        "gam": gam,
        "w1g": f("ffn1_w_gate")[0], "w1u": f("ffn1_w_up")[0], "w1d": f("ffn1_w_down")[0],
        "ident": np.eye(128, dtype=np.float32),
    }
    if debug != "hT":
        common.update({
            "w2g": f("ffn2_w_gate")[0], "w2u": f("ffn2_w_up")[0], "w2d": f("ffn2_w_down")[0],
            "w_in": f("w_in")[0], "w_cb": f("w_conv_branch")[0], "w_db": f("w_dn_branch")[0], "w_out": f("w_out")[0],
        })
        cw = np.zeros((128, 240), np.float32)
        cw[:, 0:48] = f("conv_mixer_w")[0].reshape(16, 128, 3).transpose(1, 0, 2).reshape(128, 48)
        cw[:, 48:240] = f("dn_conv_w")[0].reshape(48, 128, 4).transpose(1, 0, 2).reshape(128, 192)
        common["cw"] = cw
        s_ = np.arange(128)[:, None]
        c_ = np.arange(128)[None, :]
        tab = np.zeros((128, 9, 128), np.float32)
        tab[:, 0] = (s_ <= c_)
        tab[:, 1] = np.where(c_ >= s_, 0.0, -30000.0)
        tab[:, 2] = (c_ > s_)
        tab[:, 3] = 1.0
        tab[:, 4] = np.tile(f("dn_dt_bias")[0], NT)[None, :]
        tab[:, 5] = np.tile(f("dn_a_log")[0], NT)[None, :]
        tab[:, 6] = f("dn_out_norm")[0][None, :]
    maps = []
    for c in range(NCORES):
        m = dict(common)
        m["x"] = np.ascontiguousarray(x[c])
        if debug != "hT":
            t = tab.copy()
            r = c % 4
            if r > 0:
                t[:, 7, r - 1] = 1.0
            for j in range(3):
                if j < r:
                    t[:, 7, 4 + j] = 1.0
            m["tab"] = t
        maps.append(m)
    return maps


def kernel(**inputs):
    nc, stack = build()
    in_maps = make_in_maps(inputs)
    res = run_bass_kernel_spmd(nc, in_maps, core_ids=list(range(NCORES)))
    outs = [np.asarray(r["out"]) for r in res.results]
    return np.stack(outs, 0).reshape(2, 4096, D).astype(np.float32)
```

```python
import contextlib
import numpy as np
import concourse.bass as bass
import concourse.mybir as mybir
from concourse.bass_utils import run_bass_kernel_spmd

F32 = mybir.dt.float32
BF16 = mybir.dt.bfloat16
AF = mybir.ActivationFunctionType
ALU = mybir.AluOpType

NCORES = 8
TOK = 1024
D = 4096
KC = 32
DFF = 11008
NIN = 22560
EPS = 1e-6
HEADS = 16
NT = 8
ENGS = ("sync", "scalar", "vector", "gpsimd", "tensor")


class Sem:
    def __init__(self, h, name):
        self.h = h
        self.n = 0
        self.name = name


class Prog:
    def __init__(self, nc, stack):
        self.nc = nc
        self.stack = stack
        self.ops = {e: [] for e in ENGS}
        self.nsem = 0
        self.done = {e: self.sem("dn_" + e) for e in ENGS}
        self.seen = {e: {} for e in ENGS}
        self.dma_sems = []

    def sem(self, name):
        self.nsem += 1
        return Sem(self.stack.enter_context(self.nc.semaphore(name)), name)

    def dsem(self, name):
        s = self.sem(name)
        self.dma_sems.append(s)
        return s

    def _waits(self, eng, waits):
        best = {}
        for tok in waits:
            if tok is None:
                continue
            s, v = tok
            if v > best.get(s, (None, 0))[1]:
                best[s] = (s, v)
        out = []
        seen = self.seen[eng]
        for s, v in best.values():
            if seen.get(s, 0) >= v:
                continue
            seen[s] = v
            out.append((s.h, v))
        return out

    def op(self, eng, fn, waits=(), count=True):
        w = self._waits(eng, waits)
        inc = None
        tok = None
        if count:
            s = self.done[eng]
            if s.n >= 30000:
                s = self.done[eng] = self.sem("dn_" + eng + str(self.nsem))
            s.n += 1
            inc = (s.h, 1)
            tok = (s, s.n)
        self.ops[eng].append((w, fn, inc))
        return tok

    def dma(self, eng, sem, out, in_, waits=()):
        w = self._waits(eng, waits)
        sem.n += 16
        self.ops[eng].append((w, lambda e: e.dma_start(out=out, in_=in_), (sem.h, 16)))
        return (sem, sem.n)

    def wait_only(self, eng, waits):
        w = self._waits(eng, waits)
        if w:
            self.ops[eng].append((w, None, None))

    def barrier(self):
        toks = [(s, s.n) for s in self.done.values()] + [(s, s.n) for s in self.dma_sems]
        for en in ENGS:
            self.wait_only(en, toks)

    def replay(self, block):
        def mk(lst):
            def run(e):
                for w, fn, inc in lst:
                    for h, v in w:
                        e.wait_ge(h, v)
                    if fn is None:
                        continue
                    ins = fn(e)
                    if inc is not None:
                        ins.then_inc(inc[0], inc[1])
            return run
        for en in ENGS:
            if self.ops[en]:
                getattr(block, en)(mk(self.ops[en]))


class Buf:
    def __init__(self, t):
        self.t = t
        self.ready = []
        self.readers = []

    def ww(self):
        return self.ready + self.readers

    def wrote(self, tok):
        self.ready = [tok]
        self.readers = []

    def wrote_more(self, tok):
        self.ready.append(tok)

    def rw(self):
        return list(self.ready)

    def read(self, tok):
        self.readers.append(tok)


def build(debug=None):
    nc = bass.Bass("TRN2", target_bir_lowering=False)
    stack = contextlib.ExitStack()
    P = Prog(nc, stack)

    def din(name, shape):
        return nc.dram_tensor(name, list(shape), F32, kind="ExternalInput").ap()

    x = din("x", [TOK, D])
    gam_in = din("gam", [128, 4 * KC])
    DBGM = debug in ("mixA", "mixB") or (isinstance(debug, str) and debug.startswith("delta"))
    if not DBGM:
        w1g = din("w1g", [D, DFF]); w1u = din("w1u", [D, DFF]); w1d = din("w1d", [DFF, D])
    if debug != "hT" and not (isinstance(debug, str) and debug.startswith("delta")):
        w_in = din("w_in", [D, NIN])
        if not DBGM:
            w2g = din("w2g", [D, DFF]); w2u = din("w2u", [D, DFF]); w2d = din("w2d", [DFF, D])
            w_cb = din("w_cb", [2048, D]); w_db = din("w_db", [2048, D]); w_out = din("w_out", [D, D])
    ident_in = din("ident", [128, 128])
    MIX = debug != "hT"
    if MIX:
        cw_in = din("cw", [128, 240])
        tab_in = din("tab", [128, 9, 128])
        def dscr(name, shape, dt=F32):
            return nc.dram_tensor(name, list(shape), dt).ap()
        DELTA = isinstance(debug, str) and debug.startswith("delta")
        pc_d = dscr("pc_d", [16, 128, TOK])
        if DELTA:
            pqkv_d = din("pqkv_in", [48, 128, TOK])
            sz_d = nc.dram_tensor("sz_in", [16, 128, TOK], BF16, kind="ExternalInput").ap()
            sc_in = din("sc_in", [128, 12 * 128]); halo_in = din("halo_in", [128, 192])
        else:
            pqkv_d = dscr("pqkv_d", [48, 128, TOK])
            sz_d = dscr("sz_d", [16, 128, TOK], BF16)
        spb_d = dscr("spb_d", [16, 128, 8 * 4 * 128], BF16); spu_d = dscr("spu_d", [16, 128, 8 * 128])
        gsend_t = [nc.dram_tensor(f"gsend{q}", [4 * 128, 256], F32) for q in range(4)]
        gall_t = [nc.dram_tensor(f"gall{q}", [4 * 4 * 128, 256], F32) for q in range(4)]
        hsend_t = nc.dram_tensor("hsend", [128, 192], F32); hall_t = nc.dram_tensor("hall", [4 * 128, 192], F32)
        mrg_d = dscr("mrg_d", [KC, 128, TOK], BF16)
    out = nc.dram_tensor("out", [TOK, D], F32, kind="ExternalOutput").ap()
    hT = nc.dram_tensor("hT", [KC, 128, TOK], F32,
                        kind=("ExternalOutput" if debug in ("hT", "h2") else "Internal")).ap()

    SB_BASE = 16512
    SB_END = 229375
    cur = [SB_BASE]

    def alloc(name, shape, dt, at=None):
        n = int(np.prod(shape[1:])) * (4 if dt == F32 else 2)
        if at is None:
            at = cur[0]
            cur[0] += (n + 31) // 32 * 32
        return nc.alloc_sbuf_tensor_at(name, list(shape), dt, offset=at)

    actA = alloc("actA", [128, KC, TOK], BF16)
    regB = cur[0]
    actT = alloc("actT", [128, 29, TOK], BF16)
    cur[0] = regB + 64 * 1024
    NWB = 4
    wbuf = [Buf(alloc(f"wb{i}", [128, KC, 128], BF16)) for i in range(NWB)]
    hold = [Buf(alloc(f"hold{i}", [128, TOK], F32)) for i in range(2)]
    hnew = [Buf(alloc(f"hnew{i}", [128, TOK], F32)) for i in range(2)]
    stmp = [Buf(alloc(f"stmp{i}", [128, TOK], BF16)) for i in range(2)]
    rstd = Buf(alloc("rstd", [128, TOK], F32))
    ident = alloc("ident", [128, 128], F32)
    ones_bf = alloc("ones_bf", [128, 128], BF16)
    gam = alloc("gam", [128, 4, KC], F32)
    xin = Buf(alloc("xin", [128, D], F32, at=regB))
    xst = [Buf(alloc(f"xst{i}", [128, 4, 128], F32, at=regB + 16384 + 2048 * i)) for i in range(2)]
    yconv = alloc("yconv", [128, 16, TOK], BF16, at=regB)
    ydn = alloc("ydn", [128, 16, TOK], BF16, at=regB + 32768)
    tab = alloc("tab", [128, 9, 128], F32)
    cw = alloc("cw", [128, 240], F32)
    sc = alloc("sc", [128, 12, 128], F32)
    hsb = alloc("hsb", [128, 192], F32)
    hal = alloc("hal", [128, 4, 192], F32)
    halo = alloc("halo", [128, 192], F32)
    wab = alloc("wab", [128, KC, 32], BF16)
    assert cur[0] <= SB_END, cur[0]
    dcur = [SB_BASE]
    def dalloc(name, shape, dt):
        n = int(np.prod(shape[1:])) * (4 if dt == F32 else 2)
        sz_ = (n + 31) // 32 * 32
        if dcur[0] + sz_ > SB_BASE + 65536 and dcur[0] <= SB_BASE + 65536:
            dcur[0] = SB_BASE + 163840
        at = dcur[0]
        dcur[0] += sz_
        assert dcur[0] <= SB_BASE + 65536 or (at >= SB_BASE + 163840 and dcur[0] <= SB_BASE + 163840 + 24576), (name, dcur[0])
        return nc.alloc_sbuf_tensor_at(name, list(shape), dt, offset=at)

    NPS = 3
    psum = [Buf(stack.enter_context(nc.psum_tensor(f"ps{i}", [128, TOK], F32))) for i in range(NPS)]
    pss = [stack.enter_context(nc.psum_tensor(f"pss{i}", [128, 512], F32)) for i in range(2)]
    A = Buf(actA)
    ACT = Buf(actT)
    HT = [Buf(None) for _ in range(KC)]

    c_ld = P.dsem("c_ld")
    ctok = [P.dma("sync", c_ld, ident[:], ident_in),
            P.dma("sync", c_ld, gam[:].rearrange("p a b -> p (a b)"), gam_in),
            P.op("vector", lambda e: e.memset(ones_bf[:], 1.0))]

    st = {"wb": 0, "ps": 0, "ld": 0, "ld4": 0, "st": 0, "tmp": 0}
    wsem = [P.dsem(f"wld{i}") for i in range(NWB)]
    hld = [P.dsem(f"hld{i}") for i in range(4)]
    hst = [P.dsem(f"hst{i}") for i in range(2)]

    def next_ps():
        p = psum[st["ps"] % NPS]
        st["ps"] += 1
        return p

    def job(W_ap, nk, rhs_fn, pe_waits=()):
        b = st["wb"] % NWB
        st["wb"] += 1
        wb = wbuf[b]
        pr = next_ps()
        src = W_ap.rearrange("(k p) f -> p k f", p=128)
        tok = P.dma("gpsimd", wsem[b], wb.t[:, 0:nk, :], src, waits=wb.ww())
        wb.wrote(tok)
        waits = wb.rw() + pr.ww() + list(pe_waits) + ctok
        last = None
        for k in range(nk):
            for hf in range(2):
                fin = (k == nk - 1 and hf == 1)
                fn = (lambda e, k=k, hf=hf: e.matmul(pr.t[:, hf * 512:(hf + 1) * 512], wb.t[:, k, :], rhs_fn(k, hf),
                                                     start=(k == 0), stop=(k == nk - 1)))
                last = P.op("tensor", fn, waits=waits if (k == 0 and hf == 0) else (), count=fin)
        wb.read(last)
        pr.wrote(last)
        return pr

    xld = P.dsem("xld")
    xsts = [P.dsem(f"xsts{i}") for i in range(2)]

    def stage_T0():
        cnt = 0
        for t in range(NT):
            tok = P.dma("sync", xld, xin.t[:], x[t * 128:(t + 1) * 128, :], waits=xin.ww())
            xin.wrote(tok)
            for kg in range(KC // 4):
                pr = next_ps()
                last = None
                for kk in range(4):
                    k = kg * 4 + kk
                    last = P.op("tensor", lambda e, k=k, kk=kk, pr=pr: e.transpose(pr.t[:, kk * 128:(kk + 1) * 128],
                                                                                   xin.t[:, k * 128:(k + 1) * 128], ident[:]),
                                waits=(xin.rw() + pr.ww() + ctok) if kk == 0 else (), count=(kk == 3))
                pr.wrote(last)
                xin.read(last)
                si = cnt % 2
                cnt += 1
                sb = xst[si]
                tk = P.op("vector", lambda e, pr=pr, sb=sb: e.tensor_copy(out=sb.t[:].rearrange("p a b -> p (a b)"), in_=pr.t[:, 0:512]),
                          waits=pr.rw() + sb.ww())
                pr.read(tk)
                sb.wrote(tk)
                dst = hT[kg * 4:(kg + 1) * 4, :, t * 128:(t + 1) * 128].rearrange("k p j -> p k j")
                tk2 = P.dma("sync", xsts[si], dst, sb.t[:], waits=sb.rw())
                sb.read(tk2)
                for k in range(kg * 4, kg * 4 + 4):
                    HT[k].wrote_more(tk2)

    def load_h(k, deep=False):
        if deep:
            b = st["ld4"] % 4
            st["ld4"] += 1
            hb = (hold + hnew)[b]
        else:
            b = st["ld"] % 2
            st["ld"] += 1
            hb = hold[b]
        tok = P.dma("sync", hld[b], hb.t[:], hT[k], waits=hb.ww() + HT[k].rw())
        hb.wrote(tok)
        HT[k].read(tok)
        return hb

    def norm_stats():
        pr = next_ps()
        last = None
        for k in range(KC):
            hb = load_h(k, deep=True)
            sb = stmp[st["tmp"] % 2]
            st["tmp"] += 1
            tk = P.op("scalar", lambda e, hb=hb, sb=sb: e.activation(out=sb.t[:], in_=hb.t[:], func=AF.Square),
                      waits=hb.rw() + sb.ww())
            hb.read(tk)
            sb.wrote(tk)
            for hf in range(2):
                last = P.op("tensor", lambda e, k=k, hf=hf, sb=sb, pr=pr: e.matmul(pr.t[:, hf * 512:(hf + 1) * 512], ones_bf[:],
                                                                                   sb.t[:, hf * 512:(hf + 1) * 512],
                                                                                   start=(k == 0), stop=(k == KC - 1)),
                            waits=(sb.rw() + (pr.ww() + ctok if k == 0 else [])) if hf == 0 else (), count=(hf == 1))
            sb.read(last)
        pr.wrote(last)
        t1 = P.op("vector", lambda e: e.tensor_scalar(out=rstd.t[:], in0=pr.t[:], scalar1=1.0 / D, scalar2=EPS, op0=ALU.mult, op1=ALU.add),
                  waits=pr.rw() + rstd.ww())
        pr.read(t1)
        t2 = P.op("scalar", lambda e: e.sqrt(out=rstd.t[:], in_=rstd.t[:]), waits=[t1])
        t3 = P.op("vector", lambda e: e.reciprocal(out=rstd.t[:], in_=rstd.t[:]), waits=[t2])
        rstd.wrote(t3)

    rs_d = nc.dram_tensor("rs_d", [128, TOK], F32).ap()
    rsem = P.dsem("rsem")
    rs_tok = []

    def norm_stage(gi, save=False, reuse=False):
        if reuse:
            tk = P.dma("sync", rsem, rstd.t[:], rs_d, waits=rstd.ww() + rs_tok)
            rstd.wrote(tk)
        else:
            norm_stats()
            if save:
                tk = P.dma("sync", rsem, rs_d, rstd.t[:], waits=rstd.rw())
                rstd.read(tk)
                rs_tok.append(tk)
        first = True
        for k in range(KC):
            hb = load_h(k, deep=True)
            tk = P.op("vector", lambda e, k=k, hb=hb: e.scalar_tensor_tensor(out=actA[:, k, :], in0=hb.t[:], scalar=gam[:, gi, k:k + 1], in1=rstd.t[:],
                                                                             op0=ALU.mult, op1=ALU.mult),
                      waits=hb.rw() + rstd.rw() + (A.ww() if first else []))
            first = False
            hb.read(tk)
            rstd.read(tk)
            if k == 0:
                A.wrote(tk)
            else:
                A.wrote_more(tk)

    def resid_update(i, pd, scale):
        hb = load_h(i)
        nb = hnew[st["st"] % 2]
        ssem = hst[st["st"] % 2]
        st["st"] += 1
        tk = P.op("vector", lambda e: e.scalar_tensor_tensor(out=nb.t[:], in0=pd.t[:], scalar=scale, in1=hb.t[:],
                                                             op0=ALU.mult, op1=ALU.add),
                  waits=pd.rw() + hb.rw() + nb.ww())
        pd.read(tk)
        hb.read(tk)
        nb.wrote(tk)
        tk2 = P.dma("sync", ssem, hT[i], nb.t[:], waits=nb.rw() + HT[i].ww())
        nb.read(tk2)
        HT[i].wrote(tk2)

    def ffn(gi, Wg, Wu, Wd):
        norm_stage(gi)
        groups = [(0, 29), (29, 29), (58, 28)]
        for gidx, (f0, G) in enumerate(groups):
            for jj in range(G):
                j = f0 + jj
                pg = job(Wg[:, j * 128:(j + 1) * 128], KC, lambda k, hf: actA[:, k, hf * 512:(hf + 1) * 512], pe_waits=A.rw())
                pu = job(Wu[:, j * 128:(j + 1) * 128], KC, lambda k, hf: actA[:, k, hf * 512:(hf + 1) * 512])
                A.read(pu.ready[0])
                sb = stmp[st["tmp"] % 2]
                st["tmp"] += 1
                tk = P.op("scalar", lambda e, pg=pg, sb=sb: e.activation(out=sb.t[:], in_=pg.t[:], func=AF.Silu),
                          waits=pg.rw() + sb.ww())
                pg.read(tk)
                sb.wrote(tk)
                tk2 = P.op("vector", lambda e, pu=pu, sb=sb, jj=jj: e.tensor_tensor(out=actT[:, jj, :], in0=pu.t[:], in1=sb.t[:], op=ALU.mult),
                           waits=pu.rw() + sb.rw() + (ACT.ww() if jj == 0 else []))
                pu.read(tk2)
                sb.read(tk2)
                if jj == 0:
                    ACT.wrote(tk2)
                else:
                    ACT.wrote_more(tk2)
            for i in range(KC):
                pd = job(Wd[f0 * 128:(f0 + G) * 128, i * 128:(i + 1) * 128], G,
                         lambda k, hf: actT[:, k, hf * 512:(hf + 1) * 512], pe_waits=ACT.rw())
                ACT.read(pd.ready[0])
                resid_update(i, pd, 0.5)

    osem = P.dsem("osem")

    def final_stage():
        norm_stats()
        cnt = 0
        hb_next = load_h(0)
        for k in range(KC):
            hb = hb_next
            nb = hnew[st["st"] % 2]
            st["st"] += 1
            tk = P.op("vector", lambda e, k=k, hb=hb, nb=nb: e.scalar_tensor_tensor(out=nb.t[:], in0=hb.t[:], scalar=gam[:, 3, k:k + 1], in1=rstd.t[:],
                                                                                    op0=ALU.mult, op1=ALU.mult),
                      waits=hb.rw() + rstd.rw() + nb.ww())
            hb.read(tk)
            rstd.read(tk)
            nb.wrote(tk)
            if k + 1 < KC:
                hb_next = load_h(k + 1)
            for tg in range(2):
                pr = next_ps()
                last = None
                for tt in range(4):
                    t = tg * 4 + tt
                    last = P.op("tensor", lambda e, t=t, tt=tt, pr=pr, nb=nb: e.transpose(pr.t[:, tt * 128:(tt + 1) * 128],
                                                                                          nb.t[:, t * 128:(t + 1) * 128], ident[:]),
                                waits=(nb.rw() + pr.ww() + ctok) if tt == 0 else (), count=(tt == 3))
                pr.wrote(last)
                nb.read(last)
                sb = xst[cnt % 2]
                cnt += 1
                tk = P.op("vector", lambda e, pr=pr, sb=sb: e.tensor_copy(out=sb.t[:].rearrange("p a b -> p (a b)"), in_=pr.t[:, 0:512]),
                          waits=pr.rw() + sb.ww())
                pr.read(tk)
                sb.wrote(tk)
                dst = out[tg * 512:(tg + 1) * 512, k * 128:(k + 1) * 128].rearrange("(tt p) f -> p tt f", p=128)
                tk2 = P.dma("sync", xsts[(cnt - 1) % 2], dst, sb.t[:], waits=sb.rw())
                sb.read(tk2)

    if MIX:
        class Flow:
            def __init__(self, start=()):
                self.tk = None
                self.start = list(start)
                self.q = []
                self.lazy = False

            def op(self, eng, fn, extra=()):
                if self.lazy:
                    self.q.append(("op", eng, fn, list(extra)))
                    return None
                self.tk = P.op(eng, fn, waits=[self.tk] + self.start + list(extra))
                self.start = []
                return self.tk

            def dma(self, eng, sem, out_, in_, extra=()):
                if self.lazy:
                    self.q.append(("dma", eng, sem, out_, in_, list(extra)))
                    return None
                self.tk = P.dma(eng, sem, out_, in_, waits=[self.tk] + self.start + list(extra))
                self.start = []
                return self.tk

            def call(self, fn):
                if self.lazy:
                    self.q.append(("call", fn))
                else:
                    fn()

            def emit_one(self):
                it = self.q.pop(0)
                self.lazy = False
                if it[0] == "op":
                    self.op(it[1], it[2], it[3])
                elif it[0] == "dma":
                    self.dma(it[1], it[2], it[3], it[4], it[5])
                else:
                    it[1]()
                self.lazy = True

        def run_flows(pairs):
            flows = []
            for f, g in pairs:
                f.lazy = True
                for _ in g:
                    pass
                flows.append(f)
            lens = [len(f.q) for f in flows]
            R = max(lens) if lens else 0
            done = [0] * len(flows)
            for r in range(R):
                for i, f in enumerate(flows):
                    tgt = ((r + 1) * lens[i] + R - 1) // R
                    while done[i] < tgt and f.q:
                        f.emit_one()
                        done[i] += 1
            for f in flows:
                while f.q:
                    f.emit_one()
                f.lazy = False

        def interleave(gens):
            gens = list(gens)
            while gens:
                nxt = []
                for g in gens:
                    try:
                        next(g)
                        nxt.append(g)
                    except StopIteration:
                        pass
                gens = nxt

        def raw_inc(eng, fn, waits, sem, amt):
            w = P._waits(eng, waits)
            sem.n += amt
            P.ops[eng].append((w, fn, (sem.h, amt)))
            return (sem, sem.n)

        rhsA = lambda k, hf: actA[:, k, hf * 512:(hf + 1) * 512]
        T_U, T_NEG, T_STR, T_ONE, T_DTB, T_ALOG, T_GOUT, T_SEL = range(8)
        S_G, S_BETA, S_GC, S_GL, S_EG, S_EGL, S_EKD, S_BG, S_NEGA, S_TMP = range(10)
        GROUPS4 = [[0, 1, 2, 3], [4, 5, 6, 7]]
        msem = [P.dsem(f"msem{i}") for i in range(8)]
        ccsem = P.dsem("ccsem")

        def mixer_front():
            norm_stage(1, save=True)
            f = Flow(start=A.rw() + ctok)
            f.dma("sync", msem[0], tab[:], tab_in)
            f.dma("sync", msem[0], cw[:], cw_in)
            f.dma("gpsimd", msem[1], wab[:], w_in[:, 14336:14368].rearrange("(k p) f -> p k f", p=128))
            for t in range(NT):
                slot = pss[t // 4][:, (t % 4) * 128:(t % 4) * 128 + 32]
                for k in range(KC):
                    tk = P.op("tensor", lambda e, k=k, t=t, slot=slot: e.matmul(slot, actA[:, k, t * 128:(t + 1) * 128], wab[:, k, :],
                                                                                start=(k == 0), stop=(k == KC - 1)),
                              waits=[f.tk] if k == 0 else (), count=(k == KC - 1))
                f.tk = tk
                f.op("vector", lambda e, t=t, slot=slot: e.tensor_copy(out=sc[:, S_G, t * 16:(t + 1) * 16], in_=slot[:, 0:16]))
                f.op("vector", lambda e, t=t, slot=slot: e.tensor_copy(out=sc[:, S_BETA, t * 16:(t + 1) * 16], in_=slot[:, 16:32]))
            f.op("vector", lambda e: e.tensor_tensor(out=sc[:, S_TMP, :], in0=sc[:, S_G, :], in1=tab[:, T_DTB, :], op=ALU.add))
            f.op("scalar", lambda e: e.activation(out=sc[:, S_TMP, :], in_=sc[:, S_TMP, :], func=AF.Exp))
            f.op("scalar", lambda e: e.activation(out=sc[:, S_TMP, :], in_=sc[:, S_TMP, :], func=AF.Ln, bias=1.0))
            f.op("scalar", lambda e: e.activation(out=sc[:, S_NEGA, :], in_=tab[:, T_ALOG, :], func=AF.Exp))
            f.op("vector", lambda e: e.tensor_scalar(out=sc[:, S_NEGA, :], in0=sc[:, S_NEGA, :], scalar1=-1.0, scalar2=None, op0=ALU.mult))
            f.op("vector", lambda e: e.tensor_tensor(out=sc[:, S_G, :], in0=sc[:, S_TMP, :], in1=sc[:, S_NEGA, :], op=ALU.mult))
            f.op("scalar", lambda e: e.activation(out=sc[:, S_BETA, :], in_=sc[:, S_BETA, :], func=AF.Sigmoid))
            f.op("tensor", lambda e: e.matmul(pss[0][:, 0:128], tab[:, T_U, :], sc[:, S_G, :], start=True, stop=True))
            f.op("tensor", lambda e: e.matmul(pss[0][:, 128:256], tab[:, T_ONE, :], sc[:, S_G, :], start=True, stop=True))
            f.op("vector", lambda e: e.tensor_copy(out=sc[:, S_GC, :], in_=pss[0][:, 0:128]))
            f.op("vector", lambda e: e.tensor_copy(out=sc[:, S_GL, :], in_=pss[0][:, 128:256]))
            f.op("scalar", lambda e: e.activation(out=sc[:, S_EG, :], in_=sc[:, S_GC, :], func=AF.Exp))
            f.op("scalar", lambda e: e.activation(out=sc[:, S_EGL, :], in_=sc[:, S_GL, :], func=AF.Exp))
            f.op("vector", lambda e: e.tensor_tensor(out=sc[:, S_TMP, :], in0=sc[:, S_GL, :], in1=sc[:, S_GC, :], op=ALU.subtract))
            f.op("scalar", lambda e: e.activation(out=sc[:, S_EKD, :], in_=sc[:, S_TMP, :], func=AF.Exp))
            f.op("vector", lambda e: e.tensor_tensor(out=sc[:, S_BG, :], in0=sc[:, S_BETA, :], in1=sc[:, S_EG, :], op=ALU.mult))
            sc_tok = f.tk

            def spill(pbuf_fn, dst, hidx, ssem_i):
                nb = hnew[st["st"] % 2]
                ssem = hst[st["st"] % 2]
                st["st"] += 1
                tk = pbuf_fn(nb)
                nb.wrote(tk)
                tk2 = P.dma("sync", ssem, dst, nb.t[:], waits=nb.rw())
                nb.read(tk2)
                tk3 = P.op("vector", lambda e, nb=nb, hidx=hidx: e.tensor_copy(out=hsb[:, hidx * 3:(hidx + 1) * 3], in_=nb.t[:, TOK - 3:TOK]),
                           waits=nb.rw())
                nb.read(tk3)
                return tk3

            last_h = None
            for i in range(16):
                pC = job(w_in[:, 2048 + i * 128:2048 + (i + 1) * 128], KC, rhsA, pe_waits=A.rw())
                px = job(w_in[:, 4096 + i * 128:4096 + (i + 1) * 128], KC, rhsA)
                hb = hold[st["ld"] % 2]
                st["ld"] += 1
                tk = P.op("scalar", lambda e, pC=pC, hb=hb: e.activation(out=hb.t[:], in_=pC.t[:], func=AF.Copy), waits=pC.rw() + hb.ww())
                pC.read(tk)
                hb.wrote(tk)

                def mk(nb, px=px, hb=hb):
                    t_ = P.op("vector", lambda e: e.tensor_tensor(out=nb.t[:], in0=px.t[:], in1=hb.t[:], op=ALU.mult),
                              waits=px.rw() + hb.rw() + nb.ww())
                    px.read(t_)
                    hb.read(t_)
                    return t_
                last_h = spill(mk, pc_d[i], i, 0)
            for c in range(48):
                pq = job(w_in[:, 6144 + c * 128:6144 + (c + 1) * 128], KC, rhsA)

                def mk(nb, pq=pq, c=c):
                    if c % 2 == 0:
                        t_ = P.op("scalar", lambda e: e.activation(out=nb.t[:], in_=pq.t[:], func=AF.Copy), waits=pq.rw() + nb.ww())
                    else:
                        t_ = P.op("vector", lambda e: e.tensor_copy(out=nb.t[:], in_=pq.t[:]), waits=pq.rw() + nb.ww())
                    pq.read(t_)
                    return t_
                last_h = spill(mk, pqkv_d[c], 16 + c, 0)
            for h in range(HEADS):
                pz = job(w_in[:, 12288 + h * 128:12288 + (h + 1) * 128], KC, rhsA)
                sb = stmp[st["tmp"] % 2]
                ssem = msem[2 + st["tmp"] % 2]
                st["tmp"] += 1
                tk = P.op("scalar", lambda e, pz=pz, sb=sb: e.activation(out=sb.t[:], in_=pz.t[:], func=AF.Silu), waits=pz.rw() + sb.ww())
                pz.read(tk)
                sb.wrote(tk)
                tk2 = P.dma("sync", ssem, sz_d[h], sb.t[:], waits=sb.rw())
                sb.read(tk2)
            f = Flow(start=[last_h, sc_tok])
            f.dma("sync", msem[4], hsend_t.ap(), hsb[:])
            raw_inc("gpsimd", lambda e: e.collective_compute("AllGather", ALU.bypass, replica_groups=GROUPS4,
                                                             ins=[hsend_t.ap().opt()], outs=[hall_t.ap().opt()]),
                    [f.tk], ccsem, 1)
            f.tk = (ccsem, ccsem.n)
            f.dma("sync", msem[4], hal[:], hall_t.ap().rearrange("(j p) f -> p j f", p=128))
            f.op("vector", lambda e: e.tensor_scalar(out=halo[:], in0=hal[:, 0, :], scalar1=tab[:, T_SEL, 0:1], scalar2=None, op0=ALU.mult))
            for j in range(1, 4):
                f.op("vector", lambda e, j=j: e.scalar_tensor_tensor(out=halo[:], in0=hal[:, j, :], scalar=tab[:, T_SEL, j:j + 1], in1=halo[:],
                                                                      op0=ALU.mult, op1=ALU.add))
            return f.tk

        def conv_taps(f, eng, y, xr, wcol0, K):
            for j in range(K):
                off = 3 - (K - 1) + j
                if j == 0:
                    f.op(eng, lambda e, off=off, j=j: e.tensor_scalar(out=y, in0=xr[:, off:off + TOK], scalar1=cw[:, wcol0 + j:wcol0 + j + 1],
                                                                      scalar2=None, op0=ALU.mult))
                else:
                    f.op(eng, lambda e, off=off, j=j: e.scalar_tensor_tensor(out=y, in0=xr[:, off:off + TOK], scalar=cw[:, wcol0 + j:wcol0 + j + 1],
                                                                             in1=y, op0=ALU.mult, op1=ALU.add))

        AR2 = hold[0].t

        def conv_branch(halo_tok):
            a2 = SB_BASE + 65536 * 2 + 32768
            sets = [(nc.alloc_sbuf_tensor_at("cxr0", [128, TOK + 3], F32, offset=a2), nc.alloc_sbuf_tensor_at("cy0", [128, TOK], F32, offset=a2 + 8192)),
                    (nc.alloc_sbuf_tensor_at("cxr1", [128, TOK + 3], F32, offset=a2 + 12288), nc.alloc_sbuf_tensor_at("cy1", [128, TOK], F32, offset=a2 + 20480))]
            flows = [Flow(start=[halo_tok]), Flow(start=[halo_tok])]
            ytoks = []
            for i in range(16):
                pB = job(w_in[:, i * 128:(i + 1) * 128], KC, rhsA)
                xr, y = sets[i % 2]
                f = flows[i % 2]
                f.dma("sync", msem[5 + i % 2], xr[:, 3:TOK + 3], pc_d[i])
                f.op("vector", lambda e, xr=xr, i=i: e.tensor_copy(out=xr[:, 0:3], in_=halo[:, i * 3:(i + 1) * 3]))
                conv_taps(f, "vector", y[:], xr, i * 3, 3)
                tk = f.op("vector", lambda e, y=y, pB=pB, i=i: e.tensor_tensor(out=yconv[:, i, :], in0=pB.t[:], in1=y[:], op=ALU.mult), extra=pB.rw())
                pB.read(tk)
                ytoks.append(tk)
            return ytoks[-2:]

        def delta_phase(dmode=None):
            A2 = SB_BASE + 131072
            A2_END = A2 + 57344
            dc = [SB_BASE]

            def dal(name, shape, dt):
                n = int(np.prod(shape[1:])) * (4 if dt == F32 else 2)
                sz_ = (n + 31) // 32 * 32
                if dc[0] < A2 and dc[0] + sz_ > SB_BASE + 65536:
                    dc[0] = A2
                at = dc[0]
                dc[0] += sz_
                assert dc[0] <= SB_BASE + 65536 or (at >= A2 and dc[0] <= A2_END), (name, dc[0])
                return nc.alloc_sbuf_tensor_at(name, list(shape), dt, offset=at)

            NW = 4
            Xraw = dal("Xraw", [128, 3, TOK + 3], F32)
            Y3s = [dal(f"Y3_{i}", [128, 3, TOK], F32) for i in range(2)]
            kqs = [dal(f"kq_{i}", [128, 2, TOK], BF16) for i in range(2)]
            sqb = dal("sqb", [128, TOK], BF16)
            rinv = dal("rinv", [128, TOK], F32)

            def flowbufs(w):
                d = {}
                d["GB"] = dal(f"GB{w}", [128, 256], F32)
                d["Ebc"] = dal(f"Ebc{w}", [128, 128], F32)
                d["Mm"] = dal(f"Mm{w}", [128, 128], F32)
                d["MT"] = dal(f"MT{w}", [128, 128], F32)
                d["Rr"] = dal(f"Rr{w}", [128, 128], F32)
                base = dc[0]
                d["PQ"] = dal(f"PQ{w}", [128, 4, 128], F32)
                d["UgIb"] = nc.alloc_sbuf_tensor_at(f"UgIb{w}", [128, 256], F32, offset=base)
                d["SBm"] = nc.alloc_sbuf_tensor_at(f"SBm{w}", [128, 128], F32, offset=base + 1024)
                d["Dm"] = nc.alloc_sbuf_tensor_at(f"Dm{w}", [128, 128], F32, offset=base + 1536)
                d["Tt"] = dal(f"Tt{w}", [128, 128], BF16)
                d["kbg"] = dal(f"kbg{w}", [128, 128], BF16)
                d["vbb"] = dal(f"vbb{w}", [128, 128], BF16)
                return d
            FB = [flowbufs(0), flowbufs(1)]
            Xs = dal("Xs", [128, 256], F32)
            Xbf = dal("Xbf", [128, 256], BF16)
            vn1 = dal("vn1", [128, 256], BF16)
            HBbs = [dal(f"HBb{i}", [128, NT, 4, 128], BF16) for i in range(2)]
            HBus = [dal(f"HBu{i}", [128, NT, 128], F32) for i in range(2)]
            FB += [flowbufs(2), flowbufs(3)]
            dsm = [P.dsem(f"dsm{i}") for i in range(8)]
            idn = ident[:]
            fbank = [pss[0], pss[1], psum[1].t[:, 0:512], psum[1].t[:, 512:1024]]
            p1bank = psum[2].t[:, 0:512]
            bpsum = psum[0].t

            def colf(s_, th):
                return sc[:, s_, th:th + 1]

            def bulk(f, h, Y3, kq_bf):
                for c, idx in enumerate((h, 16 + h, 32 + h)):
                    f.dma("sync", dsm[0], Xraw[:, c, 3:TOK + 3], pqkv_d[idx])
                    f.op("vector", lambda e, c=c, idx=idx: e.tensor_copy(out=Xraw[:, c, 0:3], in_=halo[:, (16 + idx) * 3:(16 + idx) * 3 + 3]))
                    yield
                    conv_taps(f, "vector", Y3[:, c, :], Xraw[:, c, :], 48 + idx * 4, 4)
                    yield
                    f.op("scalar", lambda e, c=c: e.activation(out=Y3[:, c, :], in_=Y3[:, c, :], func=AF.Silu))
                    yield
                for c in (0, 1):
                    f.op("scalar", lambda e, c=c: e.activation(out=sqb[:], in_=Y3[:, c, :], func=AF.Square))
                    for hf in range(2):
                        f.op("tensor", lambda e, hf=hf: e.matmul(bpsum[:, hf * 512:(hf + 1) * 512], ones_bf[:], sqb[:, hf * 512:(hf + 1) * 512],
                                                                 start=True, stop=True))
                    yield
                    if c == 0:
                        f.op("vector", lambda e: e.tensor_scalar(out=rinv[:], in0=bpsum[:], scalar1=EPS, scalar2=128.0, op0=ALU.add, op1=ALU.mult))
                    else:
                        f.op("vector", lambda e: e.tensor_scalar(out=rinv[:], in0=bpsum[:], scalar1=EPS, scalar2=None, op0=ALU.add))
                    f.op("scalar", lambda e: e.sqrt(out=rinv[:], in_=rinv[:]))
                    yield
                    f.op("vector", lambda e: e.reciprocal(out=rinv[:], in_=rinv[:]))
                    f.op("vector", lambda e, c=c: e.tensor_tensor(out=Y3[:, c, :], in0=Y3[:, c, :], in1=rinv[:], op=ALU.mult))
                    f.op("scalar", lambda e, c=c: e.activation(out=kq_bf[:, 1 - c, :], in_=Y3[:, c, :], func=AF.Copy))
                    yield

            def tile_flow(f, h, t, w, Y3, kq_bf, HBb, HBu):
                th = t * 16 + h
                tl = slice(t * 128, (t + 1) * 128)
                B = FB[w]
                GB, Ebc, Mm, MT, Rr, PQ, UgIb, SBm, Dm, Tt, kbg, vbb = (B[k] for k in ("GB", "Ebc", "Mm", "MT", "Rr", "PQ", "UgIb", "SBm", "Dm", "Tt", "kbg", "vbb"))
                pab = fbank[w][:, 0:256]
                pa = fbank[w][:, 0:128]
                pb = fbank[w][:, 128:256]
                f.op("vector", lambda e: e.tensor_scalar(out=UgIb[:, 0:128], in0=tab[:, T_U, :], scalar1=colf(S_G, th), scalar2=None, op0=ALU.mult))
                f.op("vector", lambda e: e.tensor_scalar(out=UgIb[:, 128:256], in0=idn, scalar1=colf(S_BETA, th), scalar2=None, op0=ALU.mult))
                f.op("tensor", lambda e: e.matmul(pab, tab[:, T_ONE, :], UgIb[:], start=True, stop=True))
                f.op("scalar", lambda e: e.activation(out=GB[:], in_=pab, func=AF.Copy))
                yield
                f.op("vector", lambda e: e.scalar_tensor_tensor(out=Dm[:], in0=GB[:, 0:128], scalar=colf(S_GC, th), in1=tab[:, T_NEG, :],
                                                                op0=ALU.subtract, op1=ALU.add))
                f.op("scalar", lambda e: e.activation(out=Dm[:], in_=Dm[:], func=AF.Exp))
                f.op("scalar", lambda e: e.activation(out=Ebc[:], in_=GB[:, 0:128], func=AF.Exp))
                f.op("tensor", lambda e: e.matmul(pab.rearrange("p (a b) -> p a b", a=2), kq_bf[:, 0, tl], kq_bf[:, :, tl], start=True, stop=True))
                yield
                f.op("vector", lambda e: e.tensor_tensor(out=HBb[:, t, 0, :], in0=pb, in1=Dm[:], op=ALU.mult))
                f.op("vector", lambda e: e.tensor_tensor(out=SBm[:], in0=GB[:, 128:256], in1=tab[:, T_STR, :], op=ALU.mult))
                f.op("vector", lambda e: e.tensor_tensor(out=Mm[:], in0=pa, in1=Dm[:], op=ALU.mult))
                f.op("vector", lambda e: e.tensor_tensor(out=Mm[:], in0=Mm[:], in1=SBm[:], op=ALU.mult))
                yield
                f.op("tensor", lambda e: e.transpose(pa, Mm[:], idn))
                f.op("scalar", lambda e: e.activation(out=MT[:], in_=pa, func=AF.Copy))
                f.op("vector", lambda e: e.scalar_tensor_tensor(out=Rr[:], in0=Mm[:], scalar=-1.0, in1=idn, op0=ALU.mult, op1=ALU.add))
                yield
                Pc, Qc = Mm[:], MT[:]
                for k in range(6):
                    Pn = PQ[:, (k % 2) * 2, :]
                    Qn = PQ[:, (k % 2) * 2 + 1, :]
                    f.op("tensor", lambda e, Pc=Pc, Qc=Qc: e.matmul(pa, Qc, Pc, start=True, stop=True))
                    f.op("tensor", lambda e, Pc=Pc, Qc=Qc: e.matmul(pb, Pc, Qc, start=True, stop=True))
                    f.op("scalar", lambda e, Pn=Pn: e.activation(out=Pn, in_=pa, func=AF.Copy))
                    f.op("vector", lambda e, Qn=Qn: e.tensor_copy(out=Qn, in_=pb))
                    yield
                    f.op("tensor", lambda e, Qn=Qn: e.matmul(pa, Qn, Rr[:], start=True, stop=True))
                    f.op("vector", lambda e: e.tensor_tensor(out=Rr[:], in0=pa, in1=Rr[:], op=ALU.add))
                    Pc, Qc = Pn, Qn
                    yield
                f.op("vector", lambda e: e.tensor_copy(out=Tt[:], in_=Rr[:]))
                f.op("tensor", lambda e: e.transpose(pa, Y3[:, 1, tl], idn))
                f.op("tensor", lambda e: e.transpose(pb, Y3[:, 2, tl], idn))
                f.op("scalar", lambda e: e.activation(out=kbg[:], in_=pa, func=AF.Copy, scale=colf(S_BG, th)))
                yield
                f.op("vector", lambda e: e.tensor_scalar(out=HBb[:, t, 1, :], in0=pa, scalar1=colf(S_EKD, th), scalar2=None, op0=ALU.mult))
                f.op("scalar", lambda e: e.activation(out=vbb[:], in_=pb, func=AF.Copy, scale=colf(S_BETA, th)))
                f.op("tensor", lambda e: e.matmul(pa, Tt[:], vbb[:], start=True, stop=True))
                f.op("tensor", lambda e: e.matmul(pb, kbg[:], Tt[:], start=True, stop=True))
                yield
                f.op("scalar", lambda e: e.activation(out=HBu[:, t, :], in_=pa, func=AF.Copy))
                f.op("vector", lambda e: e.tensor_copy(out=HBb[:, t, 2, :], in_=pb))
                f.op("vector", lambda e: e.tensor_tensor(out=HBb[:, t, 3, :], in0=kq_bf[:, 1, tl], in1=Ebc[:], op=ALU.mult))
                yield

            def tile_seq(f, h, w, Y3, kq_bf, HBb, HBu):
                for t in range(w, NT, NW):
                    yield from tile_flow(f, h, t, w, Y3, kq_bf, HBb, HBu)

            def pass1(f, h, HBb, HBu):
                pab0 = p1bank[:, 0:256]
                f.op("vector", lambda e: e.memset(Xs[:, 0:128], 0.0))
                f.op("vector", lambda e: e.tensor_copy(out=Xs[:, 128:256], in_=idn))
                f.op("vector", lambda e: e.tensor_copy(out=Xbf[:], in_=Xs[:]))
                yield
                for t in range(NT):
                    th = t * 16 + h
                    f.op("tensor", lambda e, t=t: e.matmul(pab0, HBb[:, t, 2, :], Xbf[:], start=True, stop=True))
                    f.op("vector", lambda e, t=t: e.tensor_tensor(out=vn1[:, 0:128], in0=HBu[:, t, :], in1=pab0[:, 0:128], op=ALU.subtract))
                    f.op("vector", lambda e: e.tensor_scalar(out=vn1[:, 128:256], in0=pab0[:, 128:256], scalar1=-1.0, scalar2=None, op0=ALU.mult))
                    yield
                    f.op("tensor", lambda e, t=t: e.matmul(pab0, HBb[:, t, 1, :], vn1[:], start=True, stop=True))
                    f.op("vector", lambda e, th=th: e.scalar_tensor_tensor(out=Xs[:], in0=Xs[:], scalar=colf(S_EGL, th), in1=pab0, op0=ALU.mult, op1=ALU.add))
                    f.op("scalar", lambda e: e.activation(out=Xbf[:], in_=Xs[:], func=AF.Copy))
                    yield
                f.dma("sync", dsm[1], gsend_t[h // 4].ap()[(h % 4) * 128:(h % 4 + 1) * 128, :], Xs[:])
                f.call(lambda h=h: gs_toks.__setitem__(h, f.tk))
                f.dma("sync", dsm[1], spb_d[h], HBb[:].rearrange("p a b c -> p (a b c)"))
                f.dma("sync", dsm[1], spu_d[h], HBu[:].rearrange("p a b -> p (a b)"))
                yield

            NHD = 1 if dmode in ("delta1", "delta1c", "delta1x") else HEADS
            gs_toks = [None] * HEADS
            bulk_tok = {}
            tiles_tok = {}
            p1_tok = {}
            cc_toks = []
            for step in range(NHD + 2):
                hb, ht, hp = step, step - 1, step - 2
                gens = []
                fb = fl = fp = None
                if hb < NHD:
                    fb = Flow(start=[bulk_tok.get(hb - 1)] + tiles_tok.get(hb - 2, []))
                    gens.append((fb, bulk(fb, hb, Y3s[hb % 2], kqs[hb % 2])))
                if 0 <= ht < NHD:
                    fl = [Flow(start=[bulk_tok[ht], p1_tok.get(ht - 2)] + tiles_tok.get(ht - 1, [])) for _ in range(NW)]
                    gens += [(fl[w], tile_seq(fl[w], ht, w, Y3s[ht % 2], kqs[ht % 2], HBbs[ht % 2], HBus[ht % 2])) for w in range(NW)]
                if 0 <= hp < NHD:
                    fp = Flow(start=tiles_tok[hp] + [p1_tok.get(hp - 1)])
                    gens.append((fp, pass1(fp, hp, HBbs[hp % 2], HBus[hp % 2])))
                run_flows(gens)
                if fb is not None:
                    bulk_tok[hb] = fb.tk
                if fl is not None:
                    tiles_tok[ht] = [f_.tk for f_ in fl]
                if fp is not None:
                    p1_tok[hp] = fp.tk
                    if hp % 4 == 3 and dmode not in ("delta1", "delta1x"):
                        raw_inc("gpsimd", lambda e, q=hp // 4: e.collective_compute("AllGather", ALU.bypass, replica_groups=GROUPS4,
                                                                                    ins=[gsend_t[q].ap().opt()], outs=[gall_t[q].ap().opt()]),
                                [gs_toks[hh] for hh in range(hp - 3, hp + 1)], ccsem, 1)
                        cc_toks.append((ccsem, ccsem.n))
            if dmode == "delta1":
                return
            while len(cc_toks) < 4:
                if dmode == "delta1c" and not cc_toks:
                    raw_inc("gpsimd", lambda e: e.collective_compute("AllGather", ALU.bypass, replica_groups=GROUPS4,
                                                                     ins=[gsend_t[0].ap().opt()], outs=[gall_t[0].ap().opt()]),
                            [gs_toks[0]], ccsem, 1)
                    cc_toks.append((ccsem, ccsem.n))
                else:
                    cc_toks.append(p1_tok[NHD - 1])
            P.barrier()
            dc[0] = SB_BASE

            def p2bufs(i):
                d = {}
                d["HBb"] = dal(f"q_HBb{i}", [128, NT, 4, 128], BF16)
                d["HBu"] = dal(f"q_HBu{i}", [128, NT, 128], F32)
                d["szb"] = dal(f"q_szb{i}", [128, TOK], BF16)
                d["S"] = dal(f"q_S{i}", [128, 128], F32)
                d["Sbf"] = dal(f"q_Sbf{i}", [128, 128], BF16)
                d["vn"] = dal(f"q_vn{i}", [128, 128], BF16)
                d["onb"] = dal(f"q_onb{i}", [128, 128], F32)
                d["ssq"] = dal(f"q_ssq{i}", [128, 8], F32)
                d["AB"] = dal(f"q_AB{i}", [128, 256], F32)
                d["ATs"] = dal(f"q_ATs{i}", [128, 128], F32)
                return d
            PB = [p2bufs(i) for i in range(4)]

            def pass2(f, h, i):
                B = PB[i]
                HBb, HBu, szb, Ss, Sbf, vn, onb, ssq, AB, ATs = (B[k] for k in ("HBb", "HBu", "szb", "S", "Sbf", "vn", "onb", "ssq", "AB", "ATs"))
                pa = fbank[i][:, 0:128]
                pb = fbank[i][:, 128:256]
                f.dma("sync", dsm[4 + i], HBb[:].rearrange("p a b c -> p (a b c)"), spb_d[h])
                f.dma("sync", dsm[4 + i], HBu[:].rearrange("p a b -> p (a b)"), spu_d[h])
                f.dma("sync", dsm[4 + i], szb[:], sz_d[h])
                f.op("vector", lambda e: e.memset(Ss[:], 0.0))
                yield
                f.call(lambda h=h: f.start.append(cc_toks[h // 4]))
                for j in range(3):
                    f.dma("sync", dsm[4 + i], AB[:], gall_t[h // 4].ap()[(j * 4 + h % 4) * 128:(j * 4 + h % 4 + 1) * 128, :])
                    f.op("tensor", lambda e: e.transpose(pa, AB[:, 128:256], idn))
                    f.op("scalar", lambda e: e.activation(out=ATs[:], in_=pa, func=AF.Copy))
                    yield
                    f.op("tensor", lambda e: e.matmul(pb, ATs[:], Ss[:], start=True, stop=True))
                    f.op("vector", lambda e: e.tensor_tensor(out=ATs[:], in0=pb, in1=AB[:, 0:128], op=ALU.add))
                    f.op("vector", lambda e: e.tensor_tensor(out=ATs[:], in0=ATs[:], in1=Ss[:], op=ALU.subtract))
                    f.op("vector", lambda e, j=j: e.scalar_tensor_tensor(out=Ss[:], in0=ATs[:], scalar=tab[:, T_SEL, 4 + j:5 + j], in1=Ss[:],
                                                                          op0=ALU.mult, op1=ALU.add))
                    yield
                f.op("vector", lambda e: e.tensor_copy(out=Sbf[:], in_=Ss[:]))
                for t in range(NT):
                    th = t * 16 + h
                    tl = slice(t * 128, (t + 1) * 128)
                    f.op("tensor", lambda e, t=t: e.matmul(pa, HBb[:, t, 2, :], Sbf[:], start=True, stop=True))
                    f.op("vector", lambda e, t=t: e.tensor_tensor(out=vn[:], in0=HBu[:, t, :], in1=pa, op=ALU.subtract))
                    yield
                    f.op("tensor", lambda e, t=t: e.matmul(pb, HBb[:, t, 3, :], Sbf[:], start=True, stop=False))
                    f.op("tensor", lambda e, t=t: e.matmul(pb, HBb[:, t, 0, :], vn[:], start=False, stop=True))
                    f.op("tensor", lambda e, t=t: e.matmul(pa, HBb[:, t, 1, :], vn[:], start=True, stop=True))
                    yield
                    f.op("vector", lambda e, th=th: e.scalar_tensor_tensor(out=Ss[:], in0=Ss[:], scalar=colf(S_EGL, th), in1=pa, op0=ALU.mult, op1=ALU.add))
                    f.op("scalar", lambda e: e.activation(out=Sbf[:], in_=Ss[:], func=AF.Copy))
                    f.op("vector", lambda e: e.memset(ssq[:, 0:1], 0.0))
                    yield
                    f.op("scalar", lambda e: e.activation(out=onb[:], in_=pb, func=AF.Square, accum_out=ssq[:, 0:1]))
                    f.op("vector", lambda e: e.tensor_scalar(out=ssq[:, 0:1], in0=ssq[:, 0:1], scalar1=1.0 / 128, scalar2=EPS, op0=ALU.mult, op1=ALU.add))
                    f.op("scalar", lambda e: e.sqrt(out=ssq[:, 0:1], in_=ssq[:, 0:1]))
                    yield
                    f.op("vector", lambda e: e.reciprocal(out=ssq[:, 0:1], in_=ssq[:, 0:1]))
                    f.op("vector", lambda e: e.scalar_tensor_tensor(out=onb[:], in0=pb, scalar=ssq[:, 0:1], in1=tab[:, T_GOUT, :], op0=ALU.mult, op1=ALU.mult))
                    f.op("tensor", lambda e: e.transpose(pa, onb[:], idn))
                    f.op("vector", lambda e, tl=tl, h=h: e.tensor_tensor(out=ydn[:, h, tl], in0=pa, in1=szb[:, tl], op=ALU.mult))
                    yield

            prevs = [None] * 4
            for h0 in range(0, NHD, 4):
                fls = []
                gens = []
                for i in range(min(4, NHD - h0)):
                    f_ = Flow(start=[prevs[i]])
                    fls.append(f_)
                    gens.append((f_, pass2(f_, h0 + i, i)))
                run_flows(gens)
                for i, f_ in enumerate(fls):
                    prevs[i] = f_.tk

        def merge_and_out():
            norm_stage(1, reuse=True)
            for i in range(KC):
                pgc = job(w_in[:, 14368 + i * 128:14368 + (i + 1) * 128], KC, rhsA, pe_waits=A.rw())
                hb1 = hold[st["ld"] % 2]; st["ld"] += 1
                tk = P.op("scalar", lambda e, pgc=pgc, hb1=hb1: e.activation(out=hb1.t[:], in_=pgc.t[:], func=AF.Sigmoid), waits=pgc.rw() + hb1.ww())
                pgc.read(tk); hb1.wrote(tk)
                pcp = job(w_cb[:, i * 128:(i + 1) * 128], 16, lambda k, hf: yconv[:, k, hf * 512:(hf + 1) * 512])
                nb = hnew[st["st"] % 2]; st["st"] += 1
                tk = P.op("vector", lambda e, pcp=pcp, hb1=hb1, nb=nb: e.tensor_tensor(out=nb.t[:], in0=pcp.t[:], in1=hb1.t[:], op=ALU.mult),
                          waits=pcp.rw() + hb1.rw() + nb.ww())
                pcp.read(tk); hb1.read(tk); nb.wrote(tk)
                pgd = job(w_in[:, 18464 + i * 128:18464 + (i + 1) * 128], KC, rhsA)
                hb2 = hold[st["ld"] % 2]; st["ld"] += 1
                tk = P.op("scalar", lambda e, pgd=pgd, hb2=hb2: e.activation(out=hb2.t[:], in_=pgd.t[:], func=AF.Sigmoid), waits=pgd.rw() + hb2.ww())
                pgd.read(tk); hb2.wrote(tk)
                pdp = job(w_db[:, i * 128:(i + 1) * 128], 16, lambda k, hf: ydn[:, k, hf * 512:(hf + 1) * 512])
                tk = P.op("vector", lambda e, pdp=pdp, hb2=hb2: e.tensor_tensor(out=hb2.t[:], in0=pdp.t[:], in1=hb2.t[:], op=ALU.mult),
                          waits=pdp.rw() + hb2.rw())
                pdp.read(tk); hb2.wrote(tk)
                sb = stmp[st["tmp"] % 2]
                ssem = msem[2 + st["tmp"] % 2]
                st["tmp"] += 1
                tk = P.op("vector", lambda e, sb=sb, nb=nb, hb2=hb2: e.tensor_tensor(out=sb.t[:], in0=nb.t[:], in1=hb2.t[:], op=ALU.add),
                          waits=nb.rw() + hb2.rw() + sb.ww())
                nb.read(tk); hb2.read(tk); sb.wrote(tk)
                tk2 = P.dma("sync", ssem, mrg_d[i], sb.t[:], waits=sb.rw())
                sb.read(tk2)
            P.barrier()
            toks = []
            for q4 in range(4):
                toks.append(P.dma("sync", msem[4 + q4 % 2], actA[:, q4 * 8:(q4 + 1) * 8, :], mrg_d[q4 * 8:(q4 + 1) * 8].rearrange("k p t -> p k t")))
            A.wrote(toks[0])
            for tk in toks[1:]:
                A.wrote_more(tk)
            for i in range(KC):
                po = job(w_out[:, i * 128:(i + 1) * 128], KC, rhsA, pe_waits=A.rw())
                resid_update(i, po, 1.0)


    if MIX and DELTA:
        P.dma("sync", msem[0], tab[:], tab_in)
        P.dma("sync", msem[0], cw[:], cw_in)
        P.dma("sync", msem[1], sc[:].rearrange("p a b -> p (a b)"), sc_in)
        P.dma("sync", msem[4], halo[:], halo_in)
        P.barrier()
        delta_phase(debug)
        P.barrier()
        dbg = nc.dram_tensor("dbg", [128, 16 * TOK], BF16, kind="ExternalOutput").ap()
        P.dma("sync", msem[0], dbg, ydn[:].rearrange("p a b -> p (a b)"))
        P.barrier()
        with nc.Block() as block:
            P.replay(block)
        return nc, stack
    stage_T0()
    P.barrier()
    if debug not in ("mixA", "mixB"):
        ffn(0, w1g, w1u, w1d)
        P.barrier()
    if debug != "hT":
        htok = mixer_front()
        P.barrier()
        conv_branch(htok)
        P.barrier()
        if debug != "mixA":
            delta_phase()
            P.barrier()
        if debug in ("mixA", "mixB"):
            dbg = nc.dram_tensor("dbg", [128, 16 * TOK], BF16, kind="ExternalOutput").ap()
            dbg2 = nc.dram_tensor("dbg2", [128, 12 * 128], F32, kind="ExternalOutput").ap()
            dbg3 = nc.dram_tensor("dbg3", [128, 192], F32, kind="ExternalOutput").ap()
            src = yconv if debug == "mixA" else ydn
            P.dma("sync", msem[0], dbg, src[:].rearrange("p a b -> p (a b)"))
            P.dma("sync", msem[1], dbg2, sc[:].rearrange("p a b -> p (a b)"))
            P.dma("sync", msem[4], dbg3, halo[:])
        else:
            merge_and_out()
            P.barrier()
            if debug != "h2":
                ffn(2, w2g, w2u, w2d)
                P.barrier()
                final_stage()
    P.barrier()

    with nc.Block() as block:
        P.replay(block)
    return nc, stack


def make_in_maps(inputs, debug=None):
    f = lambda k: np.asarray(inputs[k], dtype=np.float32)
    x = f("x").reshape(NCORES, TOK, D)
    gl = lambda v: np.asarray(v, np.float32).reshape(KC, 128).T
    gam = np.ascontiguousarray(np.concatenate([gl(f("ffn1_norm")[0]), gl(f("mix_norm")[0]),
                                               gl(f("ffn2_norm")[0]), gl(f("final_norm"))], axis=1))
    common = {
        "gam": gam,
        "w1g": f("ffn1_w_gate")[0], "w1u": f("ffn1_w_up")[0], "w1d": f("ffn1_w_down")[0],
        "ident": np.eye(128, dtype=np.float32),
    }
    if debug != "hT":
        common.update({
            "w2g": f("ffn2_w_gate")[0], "w2u": f("ffn2_w_up")[0], "w2d": f("ffn2_w_down")[0],
            "w_in": f("w_in")[0], "w_cb": f("w_conv_branch")[0], "w_db": f("w_dn_branch")[0], "w_out": f("w_out")[0],
        })
        cw = np.zeros((128, 240), np.float32)
        cw[:, 0:48] = f("conv_mixer_w")[0].reshape(16, 128, 3).transpose(1, 0, 2).reshape(128, 48)
        cw[:, 48:240] = f("dn_conv_w")[0].reshape(48, 128, 4).transpose(1, 0, 2).reshape(128, 192)
        common["cw"] = cw
        s_ = np.arange(128)[:, None]
        c_ = np.arange(128)[None, :]
        tab = np.zeros((128, 9, 128), np.float32)
        tab[:, 0] = (s_ <= c_)
        tab[:, 1] = np.where(c_ >= s_, 0.0, -30000.0)
        tab[:, 2] = (c_ > s_)
        tab[:, 3] = 1.0
        tab[:, 4] = np.tile(f("dn_dt_bias")[0], NT)[None, :]
        tab[:, 5] = np.tile(f("dn_a_log")[0], NT)[None, :]
        tab[:, 6] = f("dn_out_norm")[0][None, :]
    maps = []
    for c in range(NCORES):
        m = dict(common)
        m["x"] = np.ascontiguousarray(x[c])
        if debug != "hT":
            t = tab.copy()
            r = c % 4
            if r > 0:
                t[:, 7, r - 1] = 1.0
            for j in range(3):
                if j < r:
                    t[:, 7, 4 + j] = 1.0
            m["tab"] = t
        maps.append(m)
    return maps


def kernel(**inputs):
    nc, stack = build()
    in_maps = make_in_maps(inputs)
    res = run_bass_kernel_spmd(nc, in_maps, core_ids=list(range(NCORES)))
    outs = [np.asarray(r["out"]) for r in res.results]
    return np.stack(outs, 0).reshape(2, 4096, D).astype(np.float32)
```

```python
import contextlib
import numpy as np
import concourse.bass as bass
import concourse.mybir as mybir
from concourse.bass_utils import run_bass_kernel_spmd

F32 = mybir.dt.float32
BF16 = mybir.dt.bfloat16
AF = mybir.ActivationFunctionType
ALU = mybir.AluOpType

NCORES = 8
TOK = 1024
D = 4096
KC = 32
DFF = 11008
NIN = 22560
EPS = 1e-6
HEADS = 16
NT = 8
ENGS = ("sync", "scalar", "vector", "gpsimd", "tensor")


class Sem:
    def __init__(self, h, name):
        self.h = h
        self.n = 0
        self.name = name


class Prog:
    def __init__(self, nc, stack):
        self.nc = nc
        self.stack = stack
        self.ops = {e: [] for e in ENGS}
        self.nsem = 0
        self.done = {e: self.sem("dn_" + e) for e in ENGS}
        self.seen = {e: {} for e in ENGS}
        self.dma_sems = []

    def sem(self, name):
        self.nsem += 1
        return Sem(self.stack.enter_context(self.nc.semaphore(name)), name)

    def dsem(self, name):
        s = self.sem(name)
        self.dma_sems.append(s)
        return s

    def _waits(self, eng, waits):
        best = {}
        for tok in waits:
            if tok is None:
                continue
            s, v = tok
            if v > best.get(s, (None, 0))[1]:
                best[s] = (s, v)
        out = []
        seen = self.seen[eng]
        for s, v in best.values():
            if seen.get(s, 0) >= v:
                continue
            seen[s] = v
            out.append((s.h, v))
        return out

    def op(self, eng, fn, waits=(), count=True):
        w = self._waits(eng, waits)
        inc = None
        tok = None
        if count:
            s = self.done[eng]
            if s.n >= 30000:
                s = self.done[eng] = self.sem("dn_" + eng + str(self.nsem))
            s.n += 1
            inc = (s.h, 1)
            tok = (s, s.n)
        self.ops[eng].append((w, fn, inc))
        return tok

    def dma(self, eng, sem, out, in_, waits=()):
        w = self._waits(eng, waits)
        sem.n += 16
        self.ops[eng].append((w, lambda e: e.dma_start(out=out, in_=in_), (sem.h, 16)))
        return (sem, sem.n)

    def wait_only(self, eng, waits):
        w = self._waits(eng, waits)
        if w:
            self.ops[eng].append((w, None, None))

    def barrier(self):
        toks = [(s, s.n) for s in self.done.values()] + [(s, s.n) for s in self.dma_sems]
        for en in ENGS:
            self.wait_only(en, toks)

    def replay(self, block):
        def mk(lst):
            def run(e):
                for w, fn, inc in lst:
                    for h, v in w:
                        e.wait_ge(h, v)
                    if fn is None:
                        continue
                    ins = fn(e)
                    if inc is not None:
                        ins.then_inc(inc[0], inc[1])
            return run
        for en in ENGS:
            if self.ops[en]:
                getattr(block, en)(mk(self.ops[en]))


class Buf:
    def __init__(self, t):
        self.t = t
        self.ready = []
        self.readers = []

    def ww(self):
        return self.ready + self.readers

    def wrote(self, tok):
        self.ready = [tok]
        self.readers = []

    def wrote_more(self, tok):
        self.ready.append(tok)

    def rw(self):
        return list(self.ready)

    def read(self, tok):
        self.readers.append(tok)


def build(debug=None):
    nc = bass.Bass("TRN2", target_bir_lowering=False)
    stack = contextlib.ExitStack()
    P = Prog(nc, stack)

    def din(name, shape):
        return nc.dram_tensor(name, list(shape), F32, kind="ExternalInput").ap()

    x = din("x", [TOK, D])
    gam_in = din("gam", [128, 4 * KC])
    DBGM = debug in ("mixA", "mixB") or (isinstance(debug, str) and debug.startswith("delta"))
    if not DBGM:
        w1g = din("w1g", [D, DFF]); w1u = din("w1u", [D, DFF]); w1d = din("w1d", [DFF, D])
    if debug != "hT" and not (isinstance(debug, str) and debug.startswith("delta")):
        w_in = din("w_in", [D, NIN])
        if not DBGM:
            w2g = din("w2g", [D, DFF]); w2u = din("w2u", [D, DFF]); w2d = din("w2d", [DFF, D])
            w_cb = din("w_cb", [2048, D]); w_db = din("w_db", [2048, D]); w_out = din("w_out", [D, D])
    ident_in = din("ident", [128, 128])
    MIX = debug != "hT"
    if MIX:
        cw_in = din("cw", [128, 240])
        tab_in = din("tab", [128, 9, 128])
        def dscr(name, shape, dt=F32):
            return nc.dram_tensor(name, list(shape), dt).ap()
        DELTA = isinstance(debug, str) and debug.startswith("delta")
        pc_d = dscr("pc_d", [16, 128, TOK])
        if DELTA:
            pqkv_d = din("pqkv_in", [48, 128, TOK])
            sz_d = nc.dram_tensor("sz_in", [16, 128, TOK], BF16, kind="ExternalInput").ap()
            sc_in = din("sc_in", [128, 12 * 128]); halo_in = din("halo_in", [128, 192])
        else:
            pqkv_d = dscr("pqkv_d", [48, 128, TOK])
            sz_d = dscr("sz_d", [16, 128, TOK], BF16)
        spb_d = dscr("spb_d", [16, 128, 8 * 4 * 128], BF16); spu_d = dscr("spu_d", [16, 128, 8 * 128])
        gsend_t = [nc.dram_tensor(f"gsend{q}", [4 * 128, 256], F32) for q in range(4)]
        gall_t = [nc.dram_tensor(f"gall{q}", [4 * 4 * 128, 256], F32) for q in range(4)]
        hsend_t = nc.dram_tensor("hsend", [128, 192], F32); hall_t = nc.dram_tensor("hall", [4 * 128, 192], F32)
        mrg_d = dscr("mrg_d", [KC, 128, TOK], BF16)
    out = nc.dram_tensor("out", [TOK, D], F32, kind="ExternalOutput").ap()
    hT = nc.dram_tensor("hT", [KC, 128, TOK], F32,
                        kind=("ExternalOutput" if debug in ("hT", "h2") else "Internal")).ap()

    SB_BASE = 16512
    SB_END = 229375
    cur = [SB_BASE]

    def alloc(name, shape, dt, at=None):
        n = int(np.prod(shape[1:])) * (4 if dt == F32 else 2)
        if at is None:
            at = cur[0]
            cur[0] += (n + 31) // 32 * 32
        return nc.alloc_sbuf_tensor_at(name, list(shape), dt, offset=at)

    actA = alloc("actA", [128, KC, TOK], BF16)
    regB = cur[0]
    actT = alloc("actT", [128, 29, TOK], BF16)
    cur[0] = regB + 64 * 1024
    NWB = 4
    wbuf = [Buf(alloc(f"wb{i}", [128, KC, 128], BF16)) for i in range(NWB)]
    hold = [Buf(alloc(f"hold{i}", [128, TOK], F32)) for i in range(2)]
    hnew = [Buf(alloc(f"hnew{i}", [128, TOK], F32)) for i in range(2)]
    stmp = [Buf(alloc(f"stmp{i}", [128, TOK], BF16)) for i in range(2)]
    rstd = Buf(alloc("rstd", [128, TOK], F32))
    ident = alloc("ident", [128, 128], F32)
    ones_bf = alloc("ones_bf", [128, 128], BF16)
    gam = alloc("gam", [128, 4, KC], F32)
    xin = Buf(alloc("xin", [128, D], F32, at=regB))
    xin2 = Buf(alloc("xin2", [128, D], F32, at=regB + 32768))
    xst = [Buf(alloc(f"xst{i}", [128, 4, 128], F32, at=regB + 16384 + 2048 * i)) for i in range(2)]
    yconv = alloc("yconv", [128, 16, TOK], BF16, at=regB)
    ydn = alloc("ydn", [128, 16, TOK], BF16, at=regB + 32768)
    tab = alloc("tab", [128, 9, 128], F32)
    cw = alloc("cw", [128, 240], F32)
    sc = alloc("sc", [128, 12, 128], F32)
    hsb = alloc("hsb", [128, 192], F32)
    hal = alloc("hal", [128, 4, 192], F32)
    halo = alloc("halo", [128, 192], F32)
    wab = alloc("wab", [128, KC, 32], BF16)
    assert cur[0] <= SB_END, cur[0]
    dcur = [SB_BASE]
    def dalloc(name, shape, dt):
        n = int(np.prod(shape[1:])) * (4 if dt == F32 else 2)
        sz_ = (n + 31) // 32 * 32
        if dcur[0] + sz_ > SB_BASE + 65536 and dcur[0] <= SB_BASE + 65536:
            dcur[0] = SB_BASE + 163840
        at = dcur[0]
        dcur[0] += sz_
        assert dcur[0] <= SB_BASE + 65536 or (at >= SB_BASE + 163840 and dcur[0] <= SB_BASE + 163840 + 24576), (name, dcur[0])
        return nc.alloc_sbuf_tensor_at(name, list(shape), dt, offset=at)

    NPS = 3
    psum = [Buf(stack.enter_context(nc.psum_tensor(f"ps{i}", [128, TOK], F32))) for i in range(NPS)]
    pss = [stack.enter_context(nc.psum_tensor(f"pss{i}", [128, 512], F32)) for i in range(2)]
    A = Buf(actA)
    ACT = Buf(actT)
    HT = [Buf(None) for _ in range(KC)]

    c_ld = P.dsem("c_ld")
    ctok = [P.dma("sync", c_ld, ident[:], ident_in),
            P.dma("sync", c_ld, gam[:].rearrange("p a b -> p (a b)"), gam_in),
            P.op("vector", lambda e: e.memset(ones_bf[:], 1.0))]

    st = {"wb": 0, "ps": 0, "ld": 0, "ld4": 0, "st": 0, "tmp": 0}
    wsem = [P.dsem(f"wld{i}") for i in range(NWB)]
    hld = [P.dsem(f"hld{i}") for i in range(4)]
    hst = [P.dsem(f"hst{i}") for i in range(2)]

    def next_ps():
        p = psum[st["ps"] % NPS]
        st["ps"] += 1
        return p

    def job(W_ap, nk, rhs_fn, pe_waits=()):
        b = st["wb"] % NWB
        st["wb"] += 1
        wb = wbuf[b]
        pr = next_ps()
        src = W_ap.rearrange("(k p) f -> p k f", p=128)
        tok = P.dma("gpsimd", wsem[b], wb.t[:, 0:nk, :], src, waits=wb.ww())
        wb.wrote(tok)
        waits = wb.rw() + pr.ww() + list(pe_waits) + ctok
        last = None
        for k in range(nk):
            for hf in range(2):
                fin = (k == nk - 1 and hf == 1)
                fn = (lambda e, k=k, hf=hf: e.matmul(pr.t[:, hf * 512:(hf + 1) * 512], wb.t[:, k, :], rhs_fn(k, hf),
                                                     start=(k == 0), stop=(k == nk - 1)))
                last = P.op("tensor", fn, waits=waits if (k == 0 and hf == 0) else (), count=fin)
        wb.read(last)
        pr.wrote(last)
        return pr

    xld = P.dsem("xld")
    xld2 = P.dsem("xld2")
    xsts = [P.dsem(f"xsts{i}") for i in range(2)]

    xin0 = xin

    def stage_T0():
        cnt = 0
        xins = [xin0, xin2]
        xlds = [xld, xld2]

        def ld(t):
            xb = xins[t % 2]
            tok = P.dma("sync", xlds[t % 2], xb.t[:], x[t * 128:(t + 1) * 128, :], waits=xb.ww())
            xb.wrote(tok)
        ld(0)
        for t in range(NT):
            xcur = xins[t % 2]
            if t + 1 < NT:
                ld(t + 1)
            for kg in range(KC // 4):
                pr = next_ps()
                last = None
                for kk in range(4):
                    k = kg * 4 + kk
                    last = P.op("tensor", lambda e, k=k, kk=kk, pr=pr, xcur=xcur: e.transpose(pr.t[:, kk * 128:(kk + 1) * 128],
                                                                                   xcur.t[:, k * 128:(k + 1) * 128], ident[:]),
                                waits=(xcur.rw() + pr.ww() + ctok) if kk == 0 else (), count=(kk == 3))
                pr.wrote(last)
                xcur.read(last)
                si = cnt % 2
                cnt += 1
                sb = xst[si]
                tk = P.op("vector", lambda e, pr=pr, sb=sb: e.tensor_copy(out=sb.t[:].rearrange("p a b -> p (a b)"), in_=pr.t[:, 0:512]),
                          waits=pr.rw() + sb.ww())
                pr.read(tk)
                sb.wrote(tk)
                dst = hT[kg * 4:(kg + 1) * 4, :, t * 128:(t + 1) * 128].rearrange("k p j -> p k j")
                tk2 = P.dma("sync", xsts[si], dst, sb.t[:], waits=sb.rw())
                sb.read(tk2)
                for k in range(kg * 4, kg * 4 + 4):
                    HT[k].wrote_more(tk2)

    def load_h(k, deep=False):
        if deep:
            b = st["ld4"] % 4
            st["ld4"] += 1
            hb = (hold + hnew)[b]
        else:
            b = st["ld"] % 2
            st["ld"] += 1
            hb = hold[b]
        tok = P.dma("sync", hld[b], hb.t[:], hT[k], waits=hb.ww() + HT[k].rw())
        hb.wrote(tok)
        HT[k].read(tok)
        return hb

    def norm_stats():
        pr = next_ps()
        last = None
        for k in range(KC):
            hb = load_h(k, deep=True)
            sb = stmp[st["tmp"] % 2]
            st["tmp"] += 1
            tk = P.op("scalar", lambda e, hb=hb, sb=sb: e.activation(out=sb.t[:], in_=hb.t[:], func=AF.Square),
                      waits=hb.rw() + sb.ww())
            hb.read(tk)
            sb.wrote(tk)
            for hf in range(2):
                last = P.op("tensor", lambda e, k=k, hf=hf, sb=sb, pr=pr: e.matmul(pr.t[:, hf * 512:(hf + 1) * 512], ones_bf[:],
                                                                                   sb.t[:, hf * 512:(hf + 1) * 512],
                                                                                   start=(k == 0), stop=(k == KC - 1)),
                            waits=(sb.rw() + (pr.ww() + ctok if k == 0 else [])) if hf == 0 else (), count=(hf == 1))
            sb.read(last)
        pr.wrote(last)
        t1 = P.op("vector", lambda e: e.tensor_scalar(out=rstd.t[:], in0=pr.t[:], scalar1=1.0 / D, scalar2=EPS, op0=ALU.mult, op1=ALU.add),
                  waits=pr.rw() + rstd.ww())
        pr.read(t1)
        t2 = P.op("scalar", lambda e: e.sqrt(out=rstd.t[:], in_=rstd.t[:]), waits=[t1])
        t3 = P.op("vector", lambda e: e.reciprocal(out=rstd.t[:], in_=rstd.t[:]), waits=[t2])
        rstd.wrote(t3)

    rs_d = nc.dram_tensor("rs_d", [128, TOK], F32).ap()
    rsem = P.dsem("rsem")
    rs_tok = []

    def norm_stage(gi, save=False, reuse=False):
        if reuse:
            tk = P.dma("sync", rsem, rstd.t[:], rs_d, waits=rstd.ww() + rs_tok)
            rstd.wrote(tk)
        else:
            norm_stats()
            if save:
                tk = P.dma("sync", rsem, rs_d, rstd.t[:], waits=rstd.rw())
                rstd.read(tk)
                rs_tok.append(tk)
        first = True
        for k in range(KC):
            hb = load_h(k, deep=True)
            tk = P.op("vector", lambda e, k=k, hb=hb: e.scalar_tensor_tensor(out=actA[:, k, :], in0=hb.t[:], scalar=gam[:, gi, k:k + 1], in1=rstd.t[:],
                                                                             op0=ALU.mult, op1=ALU.mult),
                      waits=hb.rw() + rstd.rw() + (A.ww() if first else []))
            first = False
            hb.read(tk)
            rstd.read(tk)
            if k == 0:
                A.wrote(tk)
            else:
                A.wrote_more(tk)

    def resid_update(i, pd, scale):
        hb = load_h(i)
        nb = hnew[st["st"] % 2]
        ssem = hst[st["st"] % 2]
        st["st"] += 1
        tk = P.op("vector", lambda e: e.scalar_tensor_tensor(out=nb.t[:], in0=pd.t[:], scalar=scale, in1=hb.t[:],
                                                             op0=ALU.mult, op1=ALU.add),
                  waits=pd.rw() + hb.rw() + nb.ww())
        pd.read(tk)
        hb.read(tk)
        nb.wrote(tk)
        tk2 = P.dma("sync", ssem, hT[i], nb.t[:], waits=nb.rw() + HT[i].ww())
        nb.read(tk2)
        HT[i].wrote(tk2)

    def ffn(gi, Wg, Wu, Wd):
        norm_stage(gi)
        groups = [(0, 29), (29, 29), (58, 28)]
        for gidx, (f0, G) in enumerate(groups):
            for jj in range(G):
                j = f0 + jj
                pg = job(Wg[:, j * 128:(j + 1) * 128], KC, lambda k, hf: actA[:, k, hf * 512:(hf + 1) * 512], pe_waits=A.rw())
                pu = job(Wu[:, j * 128:(j + 1) * 128], KC, lambda k, hf: actA[:, k, hf * 512:(hf + 1) * 512])
                A.read(pu.ready[0])
                sb = stmp[st["tmp"] % 2]
                st["tmp"] += 1
                tk = P.op("scalar", lambda e, pg=pg, sb=sb: e.activation(out=sb.t[:], in_=pg.t[:], func=AF.Silu),
                          waits=pg.rw() + sb.ww())
                pg.read(tk)
                sb.wrote(tk)
                tk2 = P.op("vector", lambda e, pu=pu, sb=sb, jj=jj: e.tensor_tensor(out=actT[:, jj, :], in0=pu.t[:], in1=sb.t[:], op=ALU.mult),
                           waits=pu.rw() + sb.rw() + (ACT.ww() if jj == 0 else []))
                pu.read(tk2)
                sb.read(tk2)
                if jj == 0:
                    ACT.wrote(tk2)
                else:
                    ACT.wrote_more(tk2)
            for i in range(KC):
                pd = job(Wd[f0 * 128:(f0 + G) * 128, i * 128:(i + 1) * 128], G,
                         lambda k, hf: actT[:, k, hf * 512:(hf + 1) * 512], pe_waits=ACT.rw())
                ACT.read(pd.ready[0])
                resid_update(i, pd, 0.5)

    osem = P.dsem("osem")

    def final_stage():
        norm_stats()
        cnt = 0
        hb_next = load_h(0)
        for k in range(KC):
            hb = hb_next
            nb = hnew[st["st"] % 2]
            st["st"] += 1
            tk = P.op("vector", lambda e, k=k, hb=hb, nb=nb: e.scalar_tensor_tensor(out=nb.t[:], in0=hb.t[:], scalar=gam[:, 3, k:k + 1], in1=rstd.t[:],
                                                                                    op0=ALU.mult, op1=ALU.mult),
                      waits=hb.rw() + rstd.rw() + nb.ww())
            hb.read(tk)
            rstd.read(tk)
            nb.wrote(tk)
            if k + 1 < KC:
                hb_next = load_h(k + 1)
            for tg in range(2):
                pr = next_ps()
                last = None
                for tt in range(4):
                    t = tg * 4 + tt
                    last = P.op("tensor", lambda e, t=t, tt=tt, pr=pr, nb=nb: e.transpose(pr.t[:, tt * 128:(tt + 1) * 128],
                                                                                          nb.t[:, t * 128:(t + 1) * 128], ident[:]),
                                waits=(nb.rw() + pr.ww() + ctok) if tt == 0 else (), count=(tt == 3))
                pr.wrote(last)
                nb.read(last)
                sb = xst[cnt % 2]
                cnt += 1
                tk = P.op("vector", lambda e, pr=pr, sb=sb: e.tensor_copy(out=sb.t[:].rearrange("p a b -> p (a b)"), in_=pr.t[:, 0:512]),
                          waits=pr.rw() + sb.ww())
                pr.read(tk)
                sb.wrote(tk)
                dst = out[tg * 512:(tg + 1) * 512, k * 128:(k + 1) * 128].rearrange("(tt p) f -> p tt f", p=128)
                tk2 = P.dma("sync", xsts[(cnt - 1) % 2], dst, sb.t[:], waits=sb.rw())
                sb.read(tk2)

    if MIX:
        class Flow:
            def __init__(self, start=()):
                self.tk = None
                self.start = list(start)
                self.q = []
                self.lazy = False

            def op(self, eng, fn, extra=()):
                if self.lazy:
                    self.q.append(("op", eng, fn, list(extra)))
                    return None
                self.tk = P.op(eng, fn, waits=[self.tk] + self.start + list(extra))
                self.start = []
                return self.tk

            def dma(self, eng, sem, out_, in_, extra=()):
                if self.lazy:
                    self.q.append(("dma", eng, sem, out_, in_, list(extra)))
                    return None
                self.tk = P.dma(eng, sem, out_, in_, waits=[self.tk] + self.start + list(extra))
                self.start = []
                return self.tk

            def call(self, fn):
                if self.lazy:
                    self.q.append(("call", fn))
                else:
                    fn()

            def emit_one(self):
                it = self.q.pop(0)
                self.lazy = False
                if it[0] == "op":
                    self.op(it[1], it[2], it[3])
                elif it[0] == "dma":
                    self.dma(it[1], it[2], it[3], it[4], it[5])
                else:
                    it[1]()
                self.lazy = True

        def run_flows(pairs):
            flows = []
            for f, g in pairs:
                f.lazy = True
                for _ in g:
                    pass
                flows.append(f)
            lens = [len(f.q) for f in flows]
            R = max(lens) if lens else 0
            done = [0] * len(flows)
            for r in range(R):
                for i, f in enumerate(flows):
                    tgt = ((r + 1) * lens[i] + R - 1) // R
                    while done[i] < tgt and f.q:
                        f.emit_one()
                        done[i] += 1
            for f in flows:
                while f.q:
                    f.emit_one()
                f.lazy = False

        def interleave(gens):
            gens = list(gens)
            while gens:
                nxt = []
                for g in gens:
                    try:
                        next(g)
                        nxt.append(g)
                    except StopIteration:
                        pass
                gens = nxt

        def raw_inc(eng, fn, waits, sem, amt):
            w = P._waits(eng, waits)
            sem.n += amt
            P.ops[eng].append((w, fn, (sem.h, amt)))
            return (sem, sem.n)

        rhsA = lambda k, hf: actA[:, k, hf * 512:(hf + 1) * 512]
        T_U, T_NEG, T_STR, T_ONE, T_DTB, T_ALOG, T_GOUT, T_SEL = range(8)
        S_G, S_BETA, S_GC, S_GL, S_EG, S_EGL, S_EKD, S_BG, S_NEGA, S_TMP = range(10)
        GROUPS4 = [[0, 1, 2, 3], [4, 5, 6, 7]]
        msem = [P.dsem(f"msem{i}") for i in range(8)]
        ccsem = P.dsem("ccsem")

        def mixer_front():
            norm_stage(1, save=True)
            f = Flow(start=A.rw() + ctok)
            f.dma("sync", msem[0], tab[:], tab_in)
            f.dma("sync", msem[0], cw[:], cw_in)
            f.dma("gpsimd", msem[1], wab[:], w_in[:, 14336:14368].rearrange("(k p) f -> p k f", p=128))
            for t in range(NT):
                slot = pss[t // 4][:, (t % 4) * 128:(t % 4) * 128 + 32]
                for k in range(KC):
                    tk = P.op("tensor", lambda e, k=k, t=t, slot=slot: e.matmul(slot, actA[:, k, t * 128:(t + 1) * 128], wab[:, k, :],
                                                                                start=(k == 0), stop=(k == KC - 1)),
                              waits=[f.tk] if k == 0 else (), count=(k == KC - 1))
                f.tk = tk
                f.op("vector", lambda e, t=t, slot=slot: e.tensor_copy(out=sc[:, S_G, t * 16:(t + 1) * 16], in_=slot[:, 0:16]))
                f.op("vector", lambda e, t=t, slot=slot: e.tensor_copy(out=sc[:, S_BETA, t * 16:(t + 1) * 16], in_=slot[:, 16:32]))
            f.op("vector", lambda e: e.tensor_tensor(out=sc[:, S_TMP, :], in0=sc[:, S_G, :], in1=tab[:, T_DTB, :], op=ALU.add))
            f.op("scalar", lambda e: e.activation(out=sc[:, S_TMP, :], in_=sc[:, S_TMP, :], func=AF.Exp))
            f.op("scalar", lambda e: e.activation(out=sc[:, S_TMP, :], in_=sc[:, S_TMP, :], func=AF.Ln, bias=1.0))
            f.op("scalar", lambda e: e.activation(out=sc[:, S_NEGA, :], in_=tab[:, T_ALOG, :], func=AF.Exp))
            f.op("vector", lambda e: e.tensor_scalar(out=sc[:, S_NEGA, :], in0=sc[:, S_NEGA, :], scalar1=-1.0, scalar2=None, op0=ALU.mult))
            f.op("vector", lambda e: e.tensor_tensor(out=sc[:, S_G, :], in0=sc[:, S_TMP, :], in1=sc[:, S_NEGA, :], op=ALU.mult))
            f.op("scalar", lambda e: e.activation(out=sc[:, S_BETA, :], in_=sc[:, S_BETA, :], func=AF.Sigmoid))
            f.op("tensor", lambda e: e.matmul(pss[0][:, 0:128], tab[:, T_U, :], sc[:, S_G, :], start=True, stop=True))
            f.op("tensor", lambda e: e.matmul(pss[0][:, 128:256], tab[:, T_ONE, :], sc[:, S_G, :], start=True, stop=True))
            f.op("vector", lambda e: e.tensor_copy(out=sc[:, S_GC, :], in_=pss[0][:, 0:128]))
            f.op("vector", lambda e: e.tensor_copy(out=sc[:, S_GL, :], in_=pss[0][:, 128:256]))
            f.op("scalar", lambda e: e.activation(out=sc[:, S_EG, :], in_=sc[:, S_GC, :], func=AF.Exp))
            f.op("scalar", lambda e: e.activation(out=sc[:, S_EGL, :], in_=sc[:, S_GL, :], func=AF.Exp))
            f.op("vector", lambda e: e.tensor_tensor(out=sc[:, S_TMP, :], in0=sc[:, S_GL, :], in1=sc[:, S_GC, :], op=ALU.subtract))
            f.op("scalar", lambda e: e.activation(out=sc[:, S_EKD, :], in_=sc[:, S_TMP, :], func=AF.Exp))
            f.op("vector", lambda e: e.tensor_tensor(out=sc[:, S_BG, :], in0=sc[:, S_BETA, :], in1=sc[:, S_EG, :], op=ALU.mult))
            sc_tok = f.tk

            def spill(pbuf_fn, dst, hidx, ssem_i):
                nb = hnew[st["st"] % 2]
                ssem = hst[st["st"] % 2]
                st["st"] += 1
                tk = pbuf_fn(nb)
                nb.wrote(tk)
                tk2 = P.dma("sync", ssem, dst, nb.t[:], waits=nb.rw())
                nb.read(tk2)
                tk3 = P.op("vector", lambda e, nb=nb, hidx=hidx: e.tensor_copy(out=hsb[:, hidx * 3:(hidx + 1) * 3], in_=nb.t[:, TOK - 3:TOK]),
                           waits=nb.rw())
                nb.read(tk3)
                return tk3

            last_h = None
            for i in range(16):
                pC = job(w_in[:, 2048 + i * 128:2048 + (i + 1) * 128], KC, rhsA, pe_waits=A.rw())
                px = job(w_in[:, 4096 + i * 128:4096 + (i + 1) * 128], KC, rhsA)
                hb = hold[st["ld"] % 2]
                st["ld"] += 1
                tk = P.op("scalar", lambda e, pC=pC, hb=hb: e.activation(out=hb.t[:], in_=pC.t[:], func=AF.Copy), waits=pC.rw() + hb.ww())
                pC.read(tk)
                hb.wrote(tk)

                def mk(nb, px=px, hb=hb):
                    t_ = P.op("vector", lambda e: e.tensor_tensor(out=nb.t[:], in0=px.t[:], in1=hb.t[:], op=ALU.mult),
                              waits=px.rw() + hb.rw() + nb.ww())
                    px.read(t_)
                    hb.read(t_)
                    return t_
                last_h = spill(mk, pc_d[i], i, 0)
            for c in range(48):
                pq = job(w_in[:, 6144 + c * 128:6144 + (c + 1) * 128], KC, rhsA)

                def mk(nb, pq=pq, c=c):
                    if c % 2 == 0:
                        t_ = P.op("scalar", lambda e: e.activation(out=nb.t[:], in_=pq.t[:], func=AF.Copy), waits=pq.rw() + nb.ww())
                    else:
                        t_ = P.op("vector", lambda e: e.tensor_copy(out=nb.t[:], in_=pq.t[:]), waits=pq.rw() + nb.ww())
                    pq.read(t_)
                    return t_
                last_h = spill(mk, pqkv_d[c], 16 + c, 0)
            for h in range(HEADS):
                pz = job(w_in[:, 12288 + h * 128:12288 + (h + 1) * 128], KC, rhsA)
                sb = stmp[st["tmp"] % 2]
                ssem = msem[2 + st["tmp"] % 2]
                st["tmp"] += 1
                tk = P.op("scalar", lambda e, pz=pz, sb=sb: e.activation(out=sb.t[:], in_=pz.t[:], func=AF.Silu), waits=pz.rw() + sb.ww())
                pz.read(tk)
                sb.wrote(tk)
                tk2 = P.dma("sync", ssem, sz_d[h], sb.t[:], waits=sb.rw())
                sb.read(tk2)
            f = Flow(start=[last_h, sc_tok])
            f.dma("sync", msem[4], hsend_t.ap(), hsb[:])
            raw_inc("gpsimd", lambda e: e.collective_compute("AllGather", ALU.bypass, replica_groups=GROUPS4,
                                                             ins=[hsend_t.ap().opt()], outs=[hall_t.ap().opt()]),
                    [f.tk], ccsem, 1)
            f.tk = (ccsem, ccsem.n)
            f.dma("sync", msem[4], hal[:], hall_t.ap().rearrange("(j p) f -> p j f", p=128))
            f.op("vector", lambda e: e.tensor_scalar(out=halo[:], in0=hal[:, 0, :], scalar1=tab[:, T_SEL, 0:1], scalar2=None, op0=ALU.mult))
            for j in range(1, 4):
                f.op("vector", lambda e, j=j: e.scalar_tensor_tensor(out=halo[:], in0=hal[:, j, :], scalar=tab[:, T_SEL, j:j + 1], in1=halo[:],
                                                                      op0=ALU.mult, op1=ALU.add))
            return f.tk

        def conv_taps(f, eng, y, xr, wcol0, K):
            for j in range(K):
                off = 3 - (K - 1) + j
                if j == 0:
                    f.op(eng, lambda e, off=off, j=j: e.tensor_scalar(out=y, in0=xr[:, off:off + TOK], scalar1=cw[:, wcol0 + j:wcol0 + j + 1],
                                                                      scalar2=None, op0=ALU.mult))
                else:
                    f.op(eng, lambda e, off=off, j=j: e.scalar_tensor_tensor(out=y, in0=xr[:, off:off + TOK], scalar=cw[:, wcol0 + j:wcol0 + j + 1],
                                                                             in1=y, op0=ALU.mult, op1=ALU.add))

        AR2 = hold[0].t

        def conv_branch(halo_tok):
            a2 = SB_BASE + 65536 * 2 + 32768
            sets = [(nc.alloc_sbuf_tensor_at("cxr0", [128, TOK + 3], F32, offset=a2), nc.alloc_sbuf_tensor_at("cy0", [128, TOK], F32, offset=a2 + 8192)),
                    (nc.alloc_sbuf_tensor_at("cxr1", [128, TOK + 3], F32, offset=a2 + 12288), nc.alloc_sbuf_tensor_at("cy1", [128, TOK], F32, offset=a2 + 20480))]
            flows = [Flow(start=[halo_tok]), Flow(start=[halo_tok])]
            ytoks = []
            for i in range(16):
                pB = job(w_in[:, i * 128:(i + 1) * 128], KC, rhsA)
                xr, y = sets[i % 2]
                f = flows[i % 2]
                f.dma("sync", msem[5 + i % 2], xr[:, 3:TOK + 3], pc_d[i])
                f.op("vector", lambda e, xr=xr, i=i: e.tensor_copy(out=xr[:, 0:3], in_=halo[:, i * 3:(i + 1) * 3]))
                conv_taps(f, "vector", y[:], xr, i * 3, 3)
                tk = f.op("vector", lambda e, y=y, pB=pB, i=i: e.tensor_tensor(out=yconv[:, i, :], in0=pB.t[:], in1=y[:], op=ALU.mult), extra=pB.rw())
                pB.read(tk)
                ytoks.append(tk)
            return ytoks[-2:]

        def delta_phase(dmode=None):
            A2 = SB_BASE + 131072
            A2_END = A2 + 57344
            dc = [SB_BASE]

            def dal(name, shape, dt):
                n = int(np.prod(shape[1:])) * (4 if dt == F32 else 2)
                sz_ = (n + 31) // 32 * 32
                if dc[0] < A2 and dc[0] + sz_ > SB_BASE + 65536:
                    dc[0] = A2
                at = dc[0]
                dc[0] += sz_
                assert dc[0] <= SB_BASE + 65536 or (at >= A2 and dc[0] <= A2_END), (name, dc[0])
                return nc.alloc_sbuf_tensor_at(name, list(shape), dt, offset=at)

            NW = 4
            Xraw = dal("Xraw", [128, 3, TOK + 3], F32)
            Y3s = [dal(f"Y3_{i}", [128, 3, TOK], F32) for i in range(2)]
            kqs = [dal(f"kq_{i}", [128, 2, TOK], BF16) for i in range(2)]
            sqb = dal("sqb", [128, TOK], BF16)
            rinv = dal("rinv", [128, TOK], F32)

            def flowbufs(w):
                d = {}
                d["GB"] = dal(f"GB{w}", [128, 256], F32)
                d["Ebc"] = dal(f"Ebc{w}", [128, 128], F32)
                d["Mm"] = dal(f"Mm{w}", [128, 128], F32)
                d["MT"] = dal(f"MT{w}", [128, 128], F32)
                d["Rr"] = dal(f"Rr{w}", [128, 128], F32)
                base = dc[0]
                d["PQ"] = dal(f"PQ{w}", [128, 4, 128], F32)
                d["UgIb"] = nc.alloc_sbuf_tensor_at(f"UgIb{w}", [128, 256], F32, offset=base)
                d["SBm"] = nc.alloc_sbuf_tensor_at(f"SBm{w}", [128, 128], F32, offset=base + 1024)
                d["Dm"] = nc.alloc_sbuf_tensor_at(f"Dm{w}", [128, 128], F32, offset=base + 1536)
                d["Tt"] = dal(f"Tt{w}", [128, 128], BF16)
                d["kbg"] = dal(f"kbg{w}", [128, 128], BF16)
                d["vbb"] = dal(f"vbb{w}", [128, 128], BF16)
                return d
            FB = [flowbufs(0), flowbufs(1)]
            Xs = dal("Xs", [128, 256], F32)
            Xbf = dal("Xbf", [128, 256], BF16)
            vn1 = dal("vn1", [128, 256], BF16)
            HBbs = [dal(f"HBb{i}", [128, NT, 4, 128], BF16) for i in range(2)]
            HBus = [dal(f"HBu{i}", [128, NT, 128], F32) for i in range(2)]
            FB += [flowbufs(2), flowbufs(3)]
            dsm = [P.dsem(f"dsm{i}") for i in range(8)]
            idn = ident[:]
            fbank = [pss[0], pss[1], psum[1].t[:, 0:512], psum[1].t[:, 512:1024]]
            p1bank = psum[2].t[:, 0:512]
            bpsum = psum[0].t

            def colf(s_, th):
                return sc[:, s_, th:th + 1]

            def bulk(f, h, Y3, kq_bf):
                for c, idx in enumerate((h, 16 + h, 32 + h)):
                    f.dma("sync", dsm[0], Xraw[:, c, 3:TOK + 3], pqkv_d[idx])
                    f.op("vector", lambda e, c=c, idx=idx: e.tensor_copy(out=Xraw[:, c, 0:3], in_=halo[:, (16 + idx) * 3:(16 + idx) * 3 + 3]))
                    yield
                    conv_taps(f, "vector", Y3[:, c, :], Xraw[:, c, :], 48 + idx * 4, 4)
                    yield
                    f.op("scalar", lambda e, c=c: e.activation(out=Y3[:, c, :], in_=Y3[:, c, :], func=AF.Silu))
                    yield
                for c in (0, 1):
                    f.op("scalar", lambda e, c=c: e.activation(out=sqb[:], in_=Y3[:, c, :], func=AF.Square))
                    for hf in range(2):
                        f.op("tensor", lambda e, hf=hf: e.matmul(bpsum[:, hf * 512:(hf + 1) * 512], ones_bf[:], sqb[:, hf * 512:(hf + 1) * 512],
                                                                 start=True, stop=True))
                    yield
                    if c == 0:
                        f.op("vector", lambda e: e.tensor_scalar(out=rinv[:], in0=bpsum[:], scalar1=EPS, scalar2=128.0, op0=ALU.add, op1=ALU.mult))
                    else:
                        f.op("vector", lambda e: e.tensor_scalar(out=rinv[:], in0=bpsum[:], scalar1=EPS, scalar2=None, op0=ALU.add))
                    f.op("scalar", lambda e: e.sqrt(out=rinv[:], in_=rinv[:]))
                    yield
                    f.op("vector", lambda e: e.reciprocal(out=rinv[:], in_=rinv[:]))
                    f.op("vector", lambda e, c=c: e.tensor_tensor(out=Y3[:, c, :], in0=Y3[:, c, :], in1=rinv[:], op=ALU.mult))
                    f.op("scalar", lambda e, c=c: e.activation(out=kq_bf[:, 1 - c, :], in_=Y3[:, c, :], func=AF.Copy))
                    yield

            def tile_flow(f, h, t, w, Y3, kq_bf, HBb, HBu):
                th = t * 16 + h
                tl = slice(t * 128, (t + 1) * 128)
                B = FB[w]
                GB, Ebc, Mm, MT, Rr, PQ, UgIb, SBm, Dm, Tt, kbg, vbb = (B[k] for k in ("GB", "Ebc", "Mm", "MT", "Rr", "PQ", "UgIb", "SBm", "Dm", "Tt", "kbg", "vbb"))
                pab = fbank[w][:, 0:256]
                pa = fbank[w][:, 0:128]
                pb = fbank[w][:, 128:256]
                f.op("vector", lambda e: e.tensor_scalar(out=UgIb[:, 0:128], in0=tab[:, T_U, :], scalar1=colf(S_G, th), scalar2=None, op0=ALU.mult))
                f.op("vector", lambda e: e.tensor_scalar(out=UgIb[:, 128:256], in0=idn, scalar1=colf(S_BETA, th), scalar2=None, op0=ALU.mult))
                f.op("tensor", lambda e: e.matmul(pab, tab[:, T_ONE, :], UgIb[:], start=True, stop=True))
                f.op("scalar", lambda e: e.activation(out=GB[:], in_=pab, func=AF.Copy))
                yield
                f.op("vector", lambda e: e.scalar_tensor_tensor(out=Dm[:], in0=GB[:, 0:128], scalar=colf(S_GC, th), in1=tab[:, T_NEG, :],
                                                                op0=ALU.subtract, op1=ALU.add))
                f.op("scalar", lambda e: e.activation(out=Dm[:], in_=Dm[:], func=AF.Exp))
                f.op("scalar", lambda e: e.activation(out=Ebc[:], in_=GB[:, 0:128], func=AF.Exp))
                f.op("tensor", lambda e: e.matmul(pab.rearrange("p (a b) -> p a b", a=2), kq_bf[:, 0, tl], kq_bf[:, :, tl], start=True, stop=True))
                yield
                f.op("vector", lambda e: e.tensor_tensor(out=HBb[:, t, 0, :], in0=pb, in1=Dm[:], op=ALU.mult))
                f.op("vector", lambda e: e.tensor_tensor(out=SBm[:], in0=GB[:, 128:256], in1=tab[:, T_STR, :], op=ALU.mult))
                f.op("vector", lambda e: e.tensor_tensor(out=Mm[:], in0=pa, in1=Dm[:], op=ALU.mult))
                f.op("vector", lambda e: e.tensor_tensor(out=Mm[:], in0=Mm[:], in1=SBm[:], op=ALU.mult))
                yield
                f.op("tensor", lambda e: e.transpose(pa, Mm[:], idn))
                f.op("scalar", lambda e: e.activation(out=MT[:], in_=pa, func=AF.Copy))
                f.op("vector", lambda e: e.scalar_tensor_tensor(out=Rr[:], in0=Mm[:], scalar=-1.0, in1=idn, op0=ALU.mult, op1=ALU.add))
                yield
                Pc, Qc = Mm[:], MT[:]
                for k in range(6):
                    Pn = PQ[:, (k % 2) * 2, :]
                    Qn = PQ[:, (k % 2) * 2 + 1, :]
                    f.op("tensor", lambda e, Pc=Pc, Qc=Qc: e.matmul(pa, Qc, Pc, start=True, stop=True))
                    f.op("tensor", lambda e, Pc=Pc, Qc=Qc: e.matmul(pb, Pc, Qc, start=True, stop=True))
                    f.op("scalar", lambda e, Pn=Pn: e.activation(out=Pn, in_=pa, func=AF.Copy))
                    f.op("vector", lambda e, Qn=Qn: e.tensor_copy(out=Qn, in_=pb))
                    yield
                    f.op("tensor", lambda e, Qn=Qn: e.matmul(pa, Qn, Rr[:], start=True, stop=True))
                    f.op("vector", lambda e: e.tensor_tensor(out=Rr[:], in0=pa, in1=Rr[:], op=ALU.add))
                    Pc, Qc = Pn, Qn
                    yield
                f.op("vector", lambda e: e.tensor_copy(out=Tt[:], in_=Rr[:]))
                f.op("tensor", lambda e: e.transpose(pa, Y3[:, 1, tl], idn))
                f.op("tensor", lambda e: e.transpose(pb, Y3[:, 2, tl], idn))
                f.op("scalar", lambda e: e.activation(out=kbg[:], in_=pa, func=AF.Copy, scale=colf(S_BG, th)))
                yield
                f.op("vector", lambda e: e.tensor_scalar(out=HBb[:, t, 1, :], in0=pa, scalar1=colf(S_EKD, th), scalar2=None, op0=ALU.mult))
                f.op("scalar", lambda e: e.activation(out=vbb[:], in_=pb, func=AF.Copy, scale=colf(S_BETA, th)))
                f.op("tensor", lambda e: e.matmul(pa, Tt[:], vbb[:], start=True, stop=True))
                f.op("tensor", lambda e: e.matmul(pb, kbg[:], Tt[:], start=True, stop=True))
                yield
                f.op("scalar", lambda e: e.activation(out=HBu[:, t, :], in_=pa, func=AF.Copy))
                f.op("vector", lambda e: e.tensor_copy(out=HBb[:, t, 2, :], in_=pb))
                f.op("vector", lambda e: e.tensor_tensor(out=HBb[:, t, 3, :], in0=kq_bf[:, 1, tl], in1=Ebc[:], op=ALU.mult))
                yield

            def tile_seq(f, h, w, Y3, kq_bf, HBb, HBu):
                for t in range(w, NT, NW):
                    yield from tile_flow(f, h, t, w, Y3, kq_bf, HBb, HBu)

            def pass1(f, h, HBb, HBu):
                pab0 = p1bank[:, 0:256]
                f.op("vector", lambda e: e.memset(Xs[:, 0:128], 0.0))
                f.op("vector", lambda e: e.tensor_copy(out=Xs[:, 128:256], in_=idn))
                f.op("vector", lambda e: e.tensor_copy(out=Xbf[:], in_=Xs[:]))
                yield
                for t in range(NT):
                    th = t * 16 + h
                    f.op("tensor", lambda e, t=t: e.matmul(pab0, HBb[:, t, 2, :], Xbf[:], start=True, stop=True))
                    f.op("vector", lambda e, t=t: e.tensor_tensor(out=vn1[:, 0:128], in0=HBu[:, t, :], in1=pab0[:, 0:128], op=ALU.subtract))
                    f.op("vector", lambda e: e.tensor_scalar(out=vn1[:, 128:256], in0=pab0[:, 128:256], scalar1=-1.0, scalar2=None, op0=ALU.mult))
                    yield
                    f.op("tensor", lambda e, t=t: e.matmul(pab0, HBb[:, t, 1, :], vn1[:], start=True, stop=True))
                    f.op("vector", lambda e, th=th: e.scalar_tensor_tensor(out=Xs[:], in0=Xs[:], scalar=colf(S_EGL, th), in1=pab0, op0=ALU.mult, op1=ALU.add))
                    f.op("scalar", lambda e: e.activation(out=Xbf[:], in_=Xs[:], func=AF.Copy))
                    yield
                f.dma("sync", dsm[1], gsend_t[h // 4].ap()[(h % 4) * 128:(h % 4 + 1) * 128, :], Xs[:])
                f.call(lambda h=h: gs_toks.__setitem__(h, f.tk))
                f.dma("sync", dsm[1], spb_d[h], HBb[:].rearrange("p a b c -> p (a b c)"))
                f.dma("sync", dsm[1], spu_d[h], HBu[:].rearrange("p a b -> p (a b)"))
                yield

            NHD = 1 if dmode in ("delta1", "delta1c", "delta1x") else HEADS
            gs_toks = [None] * HEADS
            bulk_tok = {}
            tiles_tok = {}
            p1_tok = {}
            cc_toks = []
            for step in range(NHD + 2):
                hb, ht, hp = step, step - 1, step - 2
                gens = []
                fb = fl = fp = None
                if hb < NHD:
                    fb = Flow(start=[bulk_tok.get(hb - 1)] + tiles_tok.get(hb - 2, []))
                    gens.append((fb, bulk(fb, hb, Y3s[hb % 2], kqs[hb % 2])))
                if 0 <= ht < NHD:
                    fl = [Flow(start=[bulk_tok[ht], p1_tok.get(ht - 2)] + tiles_tok.get(ht - 1, [])) for _ in range(NW)]
                    gens += [(fl[w], tile_seq(fl[w], ht, w, Y3s[ht % 2], kqs[ht % 2], HBbs[ht % 2], HBus[ht % 2])) for w in range(NW)]
                if 0 <= hp < NHD:
                    fp = Flow(start=tiles_tok[hp] + [p1_tok.get(hp - 1)])
                    gens.append((fp, pass1(fp, hp, HBbs[hp % 2], HBus[hp % 2])))
                run_flows(gens)
                if fb is not None:
                    bulk_tok[hb] = fb.tk
                if fl is not None:
                    tiles_tok[ht] = [f_.tk for f_ in fl]
                if fp is not None:
                    p1_tok[hp] = fp.tk
                    if hp % 4 == 3 and dmode not in ("delta1", "delta1x"):
                        raw_inc("gpsimd", lambda e, q=hp // 4: e.collective_compute("AllGather", ALU.bypass, replica_groups=GROUPS4,
                                                                                    ins=[gsend_t[q].ap().opt()], outs=[gall_t[q].ap().opt()]),
                                [gs_toks[hh] for hh in range(hp - 3, hp + 1)], ccsem, 1)
                        cc_toks.append((ccsem, ccsem.n))
            if dmode == "delta1":
                return
            while len(cc_toks) < 4:
                if dmode == "delta1c" and not cc_toks:
                    raw_inc("gpsimd", lambda e: e.collective_compute("AllGather", ALU.bypass, replica_groups=GROUPS4,
                                                                     ins=[gsend_t[0].ap().opt()], outs=[gall_t[0].ap().opt()]),
                            [gs_toks[0]], ccsem, 1)
                    cc_toks.append((ccsem, ccsem.n))
                else:
                    cc_toks.append(p1_tok[NHD - 1])
            P.barrier()
            dc[0] = SB_BASE

            def p2bufs(i):
                d = {}
                d["HBb"] = dal(f"q_HBb{i}", [128, NT, 4, 128], BF16)
                d["HBu"] = dal(f"q_HBu{i}", [128, NT, 128], F32)
                d["szb"] = dal(f"q_szb{i}", [128, TOK], BF16)
                d["S"] = dal(f"q_S{i}", [128, 128], F32)
                d["Sbf"] = dal(f"q_Sbf{i}", [128, 128], BF16)
                d["vn"] = dal(f"q_vn{i}", [128, 128], BF16)
                d["onb"] = dal(f"q_onb{i}", [128, 128], F32)
                d["ssq"] = dal(f"q_ssq{i}", [128, 8], F32)
                d["AB"] = dal(f"q_AB{i}", [128, 256], F32)
                d["ATs"] = dal(f"q_ATs{i}", [128, 128], F32)
                return d
            PB = [p2bufs(i) for i in range(4)]

            def pass2(f, h, i):
                B = PB[i]
                HBb, HBu, szb, Ss, Sbf, vn, onb, ssq, AB, ATs = (B[k] for k in ("HBb", "HBu", "szb", "S", "Sbf", "vn", "onb", "ssq", "AB", "ATs"))
                pa = fbank[i][:, 0:128]
                pb = fbank[i][:, 128:256]
                f.dma("sync", dsm[4 + i], HBb[:].rearrange("p a b c -> p (a b c)"), spb_d[h])
                f.dma("sync", dsm[4 + i], HBu[:].rearrange("p a b -> p (a b)"), spu_d[h])
                f.dma("sync", dsm[4 + i], szb[:], sz_d[h])
                f.op("vector", lambda e: e.memset(Ss[:], 0.0))
                yield
                f.call(lambda h=h: f.start.append(cc_toks[h // 4]))
                for j in range(3):
                    f.dma("sync", dsm[4 + i], AB[:], gall_t[h // 4].ap()[(j * 4 + h % 4) * 128:(j * 4 + h % 4 + 1) * 128, :])
                    f.op("tensor", lambda e: e.transpose(pa, AB[:, 128:256], idn))
                    f.op("scalar", lambda e: e.activation(out=ATs[:], in_=pa, func=AF.Copy))
                    yield
                    f.op("tensor", lambda e: e.matmul(pb, ATs[:], Ss[:], start=True, stop=True))
                    f.op("vector", lambda e: e.tensor_tensor(out=ATs[:], in0=pb, in1=AB[:, 0:128], op=ALU.add))
                    f.op("vector", lambda e: e.tensor_tensor(out=ATs[:], in0=ATs[:], in1=Ss[:], op=ALU.subtract))
                    f.op("vector", lambda e, j=j: e.scalar_tensor_tensor(out=Ss[:], in0=ATs[:], scalar=tab[:, T_SEL, 4 + j:5 + j], in1=Ss[:],
                                                                          op0=ALU.mult, op1=ALU.add))
                    yield
                f.op("vector", lambda e: e.tensor_copy(out=Sbf[:], in_=Ss[:]))
                for t in range(NT):
                    th = t * 16 + h
                    tl = slice(t * 128, (t + 1) * 128)
                    f.op("tensor", lambda e, t=t: e.matmul(pa, HBb[:, t, 2, :], Sbf[:], start=True, stop=True))
                    f.op("vector", lambda e, t=t: e.tensor_tensor(out=vn[:], in0=HBu[:, t, :], in1=pa, op=ALU.subtract))
                    yield
                    f.op("tensor", lambda e, t=t: e.matmul(pb, HBb[:, t, 3, :], Sbf[:], start=True, stop=False))
                    f.op("tensor", lambda e, t=t: e.matmul(pb, HBb[:, t, 0, :], vn[:], start=False, stop=True))
                    f.op("tensor", lambda e, t=t: e.matmul(pa, HBb[:, t, 1, :], vn[:], start=True, stop=True))
                    yield
                    f.op("vector", lambda e, th=th: e.scalar_tensor_tensor(out=Ss[:], in0=Ss[:], scalar=colf(S_EGL, th), in1=pa, op0=ALU.mult, op1=ALU.add))
                    f.op("scalar", lambda e: e.activation(out=Sbf[:], in_=Ss[:], func=AF.Copy))
                    f.op("vector", lambda e: e.memset(ssq[:, 0:1], 0.0))
                    yield
                    f.op("scalar", lambda e: e.activation(out=onb[:], in_=pb, func=AF.Square, accum_out=ssq[:, 0:1]))
                    f.op("vector", lambda e: e.tensor_scalar(out=ssq[:, 0:1], in0=ssq[:, 0:1], scalar1=1.0 / 128, scalar2=EPS, op0=ALU.mult, op1=ALU.add))
                    f.op("scalar", lambda e: e.sqrt(out=ssq[:, 0:1], in_=ssq[:, 0:1]))
                    yield
                    f.op("vector", lambda e: e.reciprocal(out=ssq[:, 0:1], in_=ssq[:, 0:1]))
                    f.op("vector", lambda e: e.scalar_tensor_tensor(out=onb[:], in0=pb, scalar=ssq[:, 0:1], in1=tab[:, T_GOUT, :], op0=ALU.mult, op1=ALU.mult))
                    f.op("tensor", lambda e: e.transpose(pa, onb[:], idn))
                    f.op("vector", lambda e, tl=tl, h=h: e.tensor_tensor(out=ydn[:, h, tl], in0=pa, in1=szb[:, tl], op=ALU.mult))
                    yield

            prevs = [None] * 4
            for h0 in range(0, NHD, 4):
                fls = []
                gens = []
                for i in range(min(4, NHD - h0)):
                    f_ = Flow(start=[prevs[i]])
                    fls.append(f_)
                    gens.append((f_, pass2(f_, h0 + i, i)))
                run_flows(gens)
                for i, f_ in enumerate(fls):
                    prevs[i] = f_.tk

        def merge_and_out():
            norm_stage(1, reuse=True)
            for i in range(KC):
                pgc = job(w_in[:, 14368 + i * 128:14368 + (i + 1) * 128], KC, rhsA, pe_waits=A.rw())
                hb1 = hold[st["ld"] % 2]; st["ld"] += 1
                tk = P.op("scalar", lambda e, pgc=pgc, hb1=hb1: e.activation(out=hb1.t[:], in_=pgc.t[:], func=AF.Sigmoid), waits=pgc.rw() + hb1.ww())
                pgc.read(tk); hb1.wrote(tk)
                pcp = job(w_cb[:, i * 128:(i + 1) * 128], 16, lambda k, hf: yconv[:, k, hf * 512:(hf + 1) * 512])
                nb = hnew[st["st"] % 2]; st["st"] += 1
                tk = P.op("vector", lambda e, pcp=pcp, hb1=hb1, nb=nb: e.tensor_tensor(out=nb.t[:], in0=pcp.t[:], in1=hb1.t[:], op=ALU.mult),
                          waits=pcp.rw() + hb1.rw() + nb.ww())
                pcp.read(tk); hb1.read(tk); nb.wrote(tk)
                pgd = job(w_in[:, 18464 + i * 128:18464 + (i + 1) * 128], KC, rhsA)
                hb2 = hold[st["ld"] % 2]; st["ld"] += 1
                tk = P.op("scalar", lambda e, pgd=pgd, hb2=hb2: e.activation(out=hb2.t[:], in_=pgd.t[:], func=AF.Sigmoid), waits=pgd.rw() + hb2.ww())
                pgd.read(tk); hb2.wrote(tk)
                pdp = job(w_db[:, i * 128:(i + 1) * 128], 16, lambda k, hf: ydn[:, k, hf * 512:(hf + 1) * 512])
                tk = P.op("vector", lambda e, pdp=pdp, hb2=hb2: e.tensor_tensor(out=hb2.t[:], in0=pdp.t[:], in1=hb2.t[:], op=ALU.mult),
                          waits=pdp.rw() + hb2.rw())
                pdp.read(tk); hb2.wrote(tk)
                sb = stmp[st["tmp"] % 2]
                ssem = msem[2 + st["tmp"] % 2]
                st["tmp"] += 1
                tk = P.op("vector", lambda e, sb=sb, nb=nb, hb2=hb2: e.tensor_tensor(out=sb.t[:], in0=nb.t[:], in1=hb2.t[:], op=ALU.add),
                          waits=nb.rw() + hb2.rw() + sb.ww())
                nb.read(tk); hb2.read(tk); sb.wrote(tk)
                tk2 = P.dma("sync", ssem, mrg_d[i], sb.t[:], waits=sb.rw())
                sb.read(tk2)
            P.barrier()
            toks = []
            for q4 in range(4):
                toks.append(P.dma("sync", msem[4 + q4 % 2], actA[:, q4 * 8:(q4 + 1) * 8, :], mrg_d[q4 * 8:(q4 + 1) * 8].rearrange("k p t -> p k t")))
            A.wrote(toks[0])
            for tk in toks[1:]:
                A.wrote_more(tk)
            for i in range(KC):
                po = job(w_out[:, i * 128:(i + 1) * 128], KC, rhsA, pe_waits=A.rw())
                resid_update(i, po, 1.0)


    if MIX and DELTA:
        P.dma("sync", msem[0], tab[:], tab_in)
        P.dma("sync", msem[0], cw[:], cw_in)
        P.dma("sync", msem[1], sc[:].rearrange("p a b -> p (a b)"), sc_in)
        P.dma("sync", msem[4], halo[:], halo_in)
        P.barrier()
        delta_phase(debug)
        P.barrier()
        dbg = nc.dram_tensor("dbg", [128, 16 * TOK], BF16, kind="ExternalOutput").ap()
        P.dma("sync", msem[0], dbg, ydn[:].rearrange("p a b -> p (a b)"))
        P.barrier()
        with nc.Block() as block:
            P.replay(block)
        return nc, stack
    stage_T0()
    P.barrier()
    if debug not in ("mixA", "mixB"):
        ffn(0, w1g, w1u, w1d)
        P.barrier()
    if debug != "hT":
        htok = mixer_front()
        P.barrier()
        conv_branch(htok)
        P.barrier()
        if debug != "mixA":
            delta_phase()
            P.barrier()
        if debug in ("mixA", "mixB"):
            dbg = nc.dram_tensor("dbg", [128, 16 * TOK], BF16, kind="ExternalOutput").ap()
            dbg2 = nc.dram_tensor("dbg2", [128, 12 * 128], F32, kind="ExternalOutput").ap()
            dbg3 = nc.dram_tensor("dbg3", [128, 192], F32, kind="ExternalOutput").ap()
            src = yconv if debug == "mixA" else ydn
            P.dma("sync", msem[0], dbg, src[:].rearrange("p a b -> p (a b)"))
            P.dma("sync", msem[1], dbg2, sc[:].rearrange("p a b -> p (a b)"))
            P.dma("sync", msem[4], dbg3, halo[:])
        else:
            merge_and_out()
            P.barrier()
            if debug != "h2":
                ffn(2, w2g, w2u, w2d)
                P.barrier()
                final_stage()
    P.barrier()

    with nc.Block() as block:
        P.replay(block)
    return nc, stack


def make_in_maps(inputs, debug=None):
    f = lambda k: np.asarray(inputs[k], dtype=np.float32)
    x = f("x").reshape(NCORES, TOK, D)
    gl = lambda v: np.asarray(v, np.float32).reshape(KC, 128).T
    gam = np.ascontiguousarray(np.concatenate([gl(f("ffn1_norm")[0]), gl(f("mix_norm")[0]),
                                               gl(f("ffn2_norm")[0]), gl(f("final_norm"))], axis=1))
    common = {
        "gam": gam,
        "w1g": f("ffn1_w_gate")[0], "w1u": f("ffn1_w_up")[0], "w1d": f("ffn1_w_down")[0],
        "ident": np.eye(128, dtype=np.float32),
    }
    if debug != "hT":
        common.update({
            "w2g": f("ffn2_w_gate")[0], "w2u": f("ffn2_w_up")[0], "w2d": f("ffn2_w_down")[0],
            "w_in": f("w_in")[0], "w_cb": f("w_conv_branch")[0], "w_db": f("w_dn_branch")[0], "w_out": f("w_out")[0],
        })
        cw = np.zeros((128, 240), np.float32)
        cw[:, 0:48] = f("conv_mixer_w")[0].reshape(16, 128, 3).transpose(1, 0, 2).reshape(128, 48)
        cw[:, 48:240] = f("dn_conv_w")[0].reshape(48, 128, 4).transpose(1, 0, 2).reshape(128, 192)
        common["cw"] = cw
        s_ = np.arange(128)[:, None]
        c_ = np.arange(128)[None, :]
        tab = np.zeros((128, 9, 128), np.float32)
        tab[:, 0] = (s_ <= c_)
        tab[:, 1] = np.where(c_ >= s_, 0.0, -30000.0)
        tab[:, 2] = (c_ > s_)
        tab[:, 3] = 1.0
        tab[:, 4] = np.tile(f("dn_dt_bias")[0], NT)[None, :]
        tab[:, 5] = np.tile(f("dn_a_log")[0], NT)[None, :]
        tab[:, 6] = f("dn_out_norm")[0][None, :]
    maps = []
    for c in range(NCORES):
        m = dict(common)
        m["x"] = np.ascontiguousarray(x[c])
        if debug != "hT":
            t = tab.copy()
            r = c % 4
            if r > 0:
                t[:, 7, r - 1] = 1.0
            for j in range(3):
                if j < r:
                    t[:, 7, 4 + j] = 1.0
            m["tab"] = t
        maps.append(m)
    return maps


def kernel(**inputs):
    nc, stack = build()
    in_maps = make_in_maps(inputs)
    res = run_bass_kernel_spmd(nc, in_maps, core_ids=list(range(NCORES)))
    outs = [np.asarray(r["out"]) for r in res.results]
    return np.stack(outs, 0).reshape(2, 4096, D).astype(np.float32)
```
